# Optimizing a Trainium2 kernel written in Bass

```python
import math
import jax, jax.numpy as jnp
from jax import lax
import numpy as np

D_MODEL = 1024
BATCH = 8
SEQ = 4096
DEPTH = 2

GRID_W = 64
CTX_LEN = 256
HEAD_DIM = 64
ROPE_HALF = HEAD_DIM // 4
ROPE_BASE = 10000.0
QBLOCK = 128
EPS = 1e-6
NEG_INF = -1e30
GLA_HEADS = 4
GLA_DK = 64
GLA_DV = 128
GLA_RANK = 16
GLA_TAU = 16.0
GLA_CHUNK = 64
WIN_HEADS = 8
WIN_KV = 2
WINDOW = 128
GLB_HEADS = 8
GLB_KV = 4
DIF_HEADS = 4
DIF_KV = 2
DIF_DV = 2 * HEAD_DIM
BRANCH_W = GLA_HEADS * GLA_DV
N_BRANCH = 4
IN_WIDTHS = (
    GLA_HEADS * GLA_DK, GLA_HEADS * GLA_DK, BRANCH_W, GLA_RANK, GLA_RANK, BRANCH_W,
    WIN_HEADS * HEAD_DIM, WIN_KV * HEAD_DIM, WIN_KV * HEAD_DIM, BRANCH_W,
    GLB_HEADS * HEAD_DIM, GLB_KV * HEAD_DIM, GLB_KV * HEAD_DIM, BRANCH_W,
    DIF_HEADS * 2 * HEAD_DIM, DIF_KV * 2 * HEAD_DIM, DIF_KV * DIF_DV, BRANCH_W,
)
IN_COLS = sum(IN_WIDTHS)

kernel_name = 'hybrid_gla_window_global_diff_trunk'


def ln_plain(x):
    xf = x.astype(jnp.float32)
    mu = jnp.mean(xf, axis=-1, keepdims=True)
    var = jnp.mean(jnp.square(xf - mu), axis=-1, keepdims=True)
    return ((xf - mu) * lax.rsqrt(var + EPS)).astype(x.dtype)


def ln_affine(x, g, b):
    return ln_plain(x) * g + b


def rms_norm(x, g):
    xf = x.astype(jnp.float32)
    y = xf * lax.rsqrt(jnp.mean(jnp.square(xf), axis=-1, keepdims=True) + EPS)
    return (y * g).astype(x.dtype)


def rope_2d(x, cos, sin):
    xr = x.reshape(x.shape[:-1] + (2, 2, ROPE_HALF))
    x1, x2 = xr[..., 0, :], xr[..., 1, :]
    c, s = cos.astype(x.dtype), sin.astype(x.dtype)
    return jnp.stack([x1 * c - x2 * s, x2 * c + x1 * s], axis=-2).reshape(x.shape)


def split_heads(t, n_kv, n_grp, d):
    b, l, _ = t.shape
    return t.reshape(b, l, n_kv, n_grp, d).transpose(0, 2, 3, 1, 4)


def merge_heads(o):
    b, hk, g, l, d = o.shape
    return o.transpose(0, 3, 1, 2, 4).reshape(b, l, hk * g * d)


def split_cols(p):
    parts, start = [], 0
    for w in IN_WIDTHS:
        parts.append(p[..., start:start + w])
        start += w
    return parts


def sweep_query_blocks(fn, q):
    n = q.shape[-2]
    nb = n // QBLOCK
    qb = jnp.moveaxis(q.reshape(q.shape[:-2] + (nb, QBLOCK, q.shape[-1])), -3, 0)
    out = jnp.moveaxis(lax.map(fn, qb), 0, -3)
    return out.reshape(out.shape[:-3] + (n, out.shape[-1]))


def gla_chunked(q, k, v, log_a, s0):
    b_, h_, l_, dk = q.shape
    dv = v.shape[-1]
    nc = l_ // GLA_CHUNK
    f32 = jnp.float32
    qc = q.astype(f32).reshape(b_, h_, nc, GLA_CHUNK, dk)
    kc = k.astype(f32).reshape(b_, h_, nc, GLA_CHUNK, dk)
    vc = v.astype(f32).reshape(b_, h_, nc, GLA_CHUNK, dv)
    cum = jnp.cumsum(log_a.astype(f32).reshape(b_, h_, nc, GLA_CHUNK, dk), axis=3)
    cum_last = cum[:, :, :, -1:, :]
    q_t = qc * jnp.exp(cum)
    k_t = kc * jnp.exp(-cum)
    tri = jnp.tril(jnp.ones((GLA_CHUNK, GLA_CHUNK), dtype=bool))
    a = jnp.where(tri, jnp.einsum('bhncd,bhnsd->bhncs', q_t, k_t), 0.0)
    o_intra = jnp.einsum('bhncs,bhnsv->bhncv', a, vc)
    kv = jnp.einsum('bhncd,bhncv->bhndv', kc * jnp.exp(cum_last - cum), vc)
    decay = jnp.exp(cum_last[:, :, :, 0, :])

    def step(s, inp):
        dec, kv_c = inp
        return dec[..., None] * s + kv_c, s

    _, s_starts = lax.scan(step, s0.astype(f32), (jnp.moveaxis(decay, 2, 0), jnp.moveaxis(kv, 2, 0)))
    s_starts = jnp.moveaxis(s_starts, 0, 2)
    o_inter = jnp.einsum('bhncd,bhndv->bhncv', q_t, s_starts)
    return (o_intra + o_inter).reshape(b_, h_, l_, dv)


def gla_final_state(k, v, log_a):
    cum = jnp.cumsum(log_a.astype(jnp.float32), axis=2)
    w = jnp.exp(cum[:, :, -1:, :] - cum)
    return jnp.einsum('bhld,bhlv->bhdv', k.astype(jnp.float32) * w, v.astype(jnp.float32))


def head_rms(o, g, dtype):
    of = o.astype(jnp.float32)
    y = of * lax.rsqrt(jnp.mean(jnp.square(of), axis=-1, keepdims=True) + EPS)
    b_, h_, l_, d_ = y.shape
    return (y.transpose(0, 2, 1, 3).reshape(b_, l_, h_ * d_) * g).astype(dtype)


def gla_mixer(px, pc, w_gate, b_gate, norm_g, need_ctx):
    def prep(q, k, v, g_f, g_b):
        qh = split_heads(q, GLA_HEADS, 1, GLA_DK)[:, :, 0] * (GLA_DK ** -0.5)
        kh = split_heads(k, GLA_HEADS, 1, GLA_DK)[:, :, 0]
        vh = split_heads(v, GLA_HEADS, 1, GLA_DV)[:, :, 0]
        la_f = jax.nn.log_sigmoid((g_f @ w_gate[0] + b_gate[0]).astype(jnp.float32)) / GLA_TAU
        la_b = jax.nn.log_sigmoid((g_b @ w_gate[1] + b_gate[1]).astype(jnp.float32)) / GLA_TAU
        la_f = split_heads(la_f, GLA_HEADS, 1, GLA_DK)[:, :, 0]
        la_b = split_heads(la_b, GLA_HEADS, 1, GLA_DK)[:, :, 0]
        return qh, kh, vh, la_f, la_b

    def flip(t):
        return jnp.flip(t, axis=2)

    qx, kx, vx, lfx, lbx = prep(*px)
    qc, kc, vc, lfc, lbc = prep(*pc)
    s_f = gla_final_state(kc, vc, lfc)
    s_b = gla_final_state(flip(kc), flip(vc), flip(lbc))
    o_x = gla_chunked(qx, kx, vx, lfx, s_f) + flip(gla_chunked(flip(qx), flip(kx), flip(vx), flip(lbx), s_b))
    out_x = head_rms(o_x, norm_g, px[2].dtype)
    out_c = None
    if need_ctx:
        zero = jnp.zeros_like(s_f)
        o_c = gla_chunked(qc, kc, vc, lfc, zero) + flip(gla_chunked(flip(qc), flip(kc), flip(vc), flip(lbc), zero))
        out_c = head_rms(o_c, norm_g, pc[2].dtype)
    return out_x, out_c


def window_mixer(px, pc, sink, cos, sin, need_ctx):
    grp = WIN_HEADS // WIN_KV
    scale = HEAD_DIM ** -0.5
    q = rope_2d(split_heads(px[0], WIN_KV, grp, HEAD_DIM), cos, sin) * scale
    k = rope_2d(split_heads(px[1], WIN_KV, 1, HEAD_DIM)[:, :, 0], cos, sin)
    v = split_heads(px[2], WIN_KV, 1, HEAD_DIM)[:, :, 0]
    kc = split_heads(pc[1], WIN_KV, 1, HEAD_DIM)[:, :, 0]
    vc = split_heads(pc[2], WIN_KV, 1, HEAD_DIM)[:, :, 0]
    b_, hk, _, n, d = q.shape
    nb = n // WINDOW
    nw = 3 * WINDOW
    lc = kc.shape[2]
    qb = q.reshape(b_, hk, grp, nb, WINDOW, d)
    pad = ((0, 0), (0, 0), (WINDOW, WINDOW), (0, 0))
    kp = jnp.pad(k, pad).reshape(b_, hk, nb + 2, WINDOW, d)
    vp = jnp.pad(v, pad).reshape(b_, hk, nb + 2, WINDOW, d)
    kw = jnp.concatenate([kp[:, :, :-2], kp[:, :, 1:-1], kp[:, :, 2:]], axis=3)
    vw = jnp.concatenate([vp[:, :, :-2], vp[:, :, 1:-1], vp[:, :, 2:]], axis=3)
    s_win = jnp.einsum('bhgnqd,bhnkd->bhgnqk', qb, kw).astype(jnp.float32)
    qi = jnp.arange(WINDOW)[:, None]
    kj = jnp.arange(nw)[None, :]
    key_pos = jnp.arange(nb)[:, None, None] * WINDOW - WINDOW + kj[None]
    band = (kj >= qi) & (kj <= qi + 2 * WINDOW)
    mask = band[None] & (key_pos >= 0) & (key_pos < n)
    s_win = jnp.where(mask, s_win, NEG_INF)
    s_ctx = jnp.einsum('bhgnqd,bhkd->bhgnqk', qb, kc).astype(jnp.float32)
    sink_f = sink.astype(jnp.float32).reshape(1, hk, grp, 1, 1, 1)
    s_sink = jnp.broadcast_to(sink_f, s_win.shape[:-1] + (1,))
    p = jax.nn.softmax(jnp.concatenate([s_win, s_ctx, s_sink], axis=-1), axis=-1)
    o = (jnp.einsum('bhgnqk,bhnkd->bhgnqd', p[..., :nw].astype(v.dtype), vw)
         + jnp.einsum('bhgnqk,bhkd->bhgnqd', p[..., nw:nw + lc].astype(v.dtype), vc))
    out_x = merge_heads(o.reshape(b_, hk, grp, n, d))
    out_c = None
    if need_ctx:
        qc = split_heads(pc[0], WIN_KV, grp, HEAD_DIM) * scale
        s = jnp.einsum('bhgqd,bhkd->bhgqk', qc, kc).astype(jnp.float32)
        s_sink_c = jnp.broadcast_to(sink_f[..., 0], s.shape[:-1] + (1,))
        pcx = jax.nn.softmax(jnp.concatenate([s, s_sink_c], axis=-1), axis=-1)[..., :-1]
        out_c = merge_heads(jnp.einsum('bhgqk,bhkd->bhgqd', pcx.astype(vc.dtype), vc))
    return out_x, out_c


def global_mixer(px, pc, q_gain, k_gain, cos, sin, need_ctx):
    grp = GLB_HEADS // GLB_KV
    scale = HEAD_DIM ** -0.5
    q = rope_2d(rms_norm(split_heads(px[0], GLB_KV, grp, HEAD_DIM), q_gain), cos, sin) * scale
    k = rope_2d(rms_norm(split_heads(px[1], GLB_KV, 1, HEAD_DIM)[:, :, 0], k_gain), cos, sin)
    v = split_heads(px[2], GLB_KV, 1, HEAD_DIM)[:, :, 0]
    kc = rms_norm(split_heads(pc[1], GLB_KV, 1, HEAD_DIM)[:, :, 0], k_gain)
    vc = split_heads(pc[2], GLB_KV, 1, HEAD_DIM)[:, :, 0]
    k_all = jnp.concatenate([kc, k], axis=2)
    v_all = jnp.concatenate([vc, v], axis=2)

    def block(qb):
        s = jnp.einsum('bhgqd,bhkd->bhgqk', qb, k_all).astype(jnp.float32)
        p = jax.nn.softmax(s, axis=-1)
        return jnp.einsum('bhgqk,bhkd->bhgqd', p.astype(v_all.dtype), v_all)

    out_x = merge_heads(sweep_query_blocks(block, q))
    out_c = None
    if need_ctx:
        qc = rms_norm(split_heads(pc[0], GLB_KV, grp, HEAD_DIM), q_gain) * scale
        p = jax.nn.softmax(jnp.einsum('bhgqd,bhkd->bhgqk', qc, kc).astype(jnp.float32), axis=-1)
        out_c = merge_heads(jnp.einsum('bhgqk,bhkd->bhgqd', p.astype(vc.dtype), vc))
    return out_x, out_c


def diff_mixer(px, pc, lam_p, sub_g, lam_init, cos, sin, need_ctx):
    grp = DIF_HEADS // DIF_KV
    scale = HEAD_DIM ** -0.5

    def q_heads(t):
        b_, l_, _ = t.shape
        return t.reshape(b_, l_, DIF_KV, grp, 2, HEAD_DIM).transpose(0, 2, 3, 4, 1, 5)

    def k_heads(t):
        b_, l_, _ = t.shape
        return t.reshape(b_, l_, DIF_KV, 2, HEAD_DIM).transpose(0, 2, 3, 1, 4)

    q = rope_2d(q_heads(px[0]), cos, sin) * scale
    k = rope_2d(k_heads(px[1]), cos, sin)
    v = split_heads(px[2], DIF_KV, 1, DIF_DV)[:, :, 0]
    kc = k_heads(pc[1])
    vc = split_heads(pc[2], DIF_KV, 1, DIF_DV)[:, :, 0]
    lp = lam_p.astype(jnp.float32)
    lam = jnp.exp(jnp.sum(lp[0] * lp[1])) - jnp.exp(jnp.sum(lp[2] * lp[3])) + lam_init
    k_all = jnp.concatenate([kc, k], axis=3)
    v_all = jnp.concatenate([vc, v], axis=2)

    def diff_weights(s):
        p = jax.nn.softmax(s.astype(jnp.float32), axis=-1)
        return p[:, :, :, 0] - lam * p[:, :, :, 1]

    def block(qb):
        w = diff_weights(jnp.einsum('bhgmqd,bhmkd->bhgmqk', qb, k_all))
        return jnp.einsum('bhgqk,bhkd->bhgqd', w.astype(v_all.dtype), v_all)

    o = sweep_query_blocks(block, q)
    out_x = merge_heads(rms_norm(o, sub_g) * (1.0 - lam_init))
    out_c = None
    if need_ctx:
        qc = q_heads(pc[0]) * scale
        w = diff_weights(jnp.einsum('bhgmqd,bhmkd->bhgmqk', qc, kc))
        o_c = jnp.einsum('bhgqk,bhkd->bhgqd', w.astype(vc.dtype), vc)
        out_c = merge_heads(rms_norm(o_c, sub_g) * (1.0 - lam_init))
    return out_x, out_c


def merge_branches(h, branches, w_merge, w_up, w_out):
    terms = [jax.nn.sigmoid(h @ w_merge[i]) * (o @ w_up[i]) for i, o in enumerate(branches)]
    return sum(terms[1:], terms[0]) @ w_out


def hybrid_layer(x, xc, c, c_ctx, w_ada, b_ada, w_in, gla_w_gate, gla_b_gate, gla_norm, win_sink,
                 glb_q_norm, glb_k_norm, diff_lambda, diff_norm, w_merge, w_up, w_out, ln_g, ln_b,
                 cos, sin, lam_init, need_ctx):
    alpha = (2 * DEPTH) ** 0.25
    shift_x, scale_x, gate_x = jnp.split(jax.nn.silu(c) @ w_ada + b_ada, 3, axis=-1)
    shift_c, scale_c, gate_c = jnp.split(jax.nn.silu(c_ctx) @ w_ada + b_ada, 3, axis=-1)
    h = ln_plain(x) * (1.0 + scale_x[:, None, :]) + shift_x[:, None, :]
    hc = ln_plain(xc) * (1.0 + scale_c) + shift_c
    px = split_cols(h @ w_in)
    pc = split_cols(hc @ w_in)
    oa, oca = gla_mixer(px[0:5], pc[0:5], gla_w_gate, gla_b_gate, gla_norm, need_ctx)
    ob, ocb = window_mixer(px[6:9], pc[6:9], win_sink, cos, sin, need_ctx)
    og, ocg = global_mixer(px[10:13], pc[10:13], glb_q_norm, glb_k_norm, cos, sin, need_ctx)
    od, ocd = diff_mixer(px[14:17], pc[14:17], diff_lambda, diff_norm, lam_init, cos, sin, need_ctx)
    z_idx = (5, 9, 13, 17)
    branches = [o * jax.nn.silu(px[i]) for o, i in zip((oa, ob, og, od), z_idx)]
    out_x = merge_branches(h, branches, w_merge, w_up, w_out)
    x_new = ln_affine(alpha * x + gate_x[:, None, :] * out_x, ln_g, ln_b)
    xc_new = None
    if need_ctx:
        branches_c = [o * jax.nn.silu(pc[i]) for o, i in zip((oca, ocb, ocg, ocd), z_idx)]
        out_c = merge_branches(hc, branches_c, w_merge, w_up, w_out)
        xc_new = ln_affine(alpha * xc + gate_c * out_c, ln_g, ln_b)
    return x_new, xc_new


def setup_inputs(seed: int = 0) -> dict:
    key = jax.random.key(seed)
    ks = jax.random.split(key, 20)
    f32 = jnp.float32
    L, D = DEPTH, D_MODEL
    beta = (8 * DEPTH) ** -0.25

    def nrm(k, shape, s):
        return jax.random.normal(k, shape, f32) * s

    return {
        'x': nrm(ks[0], (BATCH, SEQ, D), 1.0),
        'c': nrm(ks[1], (BATCH, D), 1.0),
        'ctx': nrm(ks[2], (BATCH, CTX_LEN, D), 1.0),
        'c_ctx': nrm(ks[3], (D,), 1.0),
        'w_ada': nrm(ks[4], (L, D, 3 * D), 0.5 * D ** -0.5),
        'b_ada': nrm(ks[5], (L, 3 * D), 0.02),
        'w_in': nrm(ks[6], (L, D, IN_COLS), D ** -0.5),
        'gla_w_gate': nrm(ks[7], (L, 2, GLA_RANK, GLA_HEADS * GLA_DK), GLA_RANK ** -0.5),
        'gla_b_gate': nrm(ks[8], (L, 2, GLA_HEADS * GLA_DK), 0.1),
        'gla_norm': 1.0 + nrm(ks[9], (L, BRANCH_W), 0.02),
        'win_sink': nrm(ks[10], (L, WIN_HEADS), 1.0),
        'glb_q_norm': 1.0 + nrm(ks[11], (L, HEAD_DIM), 0.02),
        'glb_k_norm': 1.0 + nrm(ks[12], (L, HEAD_DIM), 0.02),
        'diff_lambda': nrm(ks[13], (L, 4, HEAD_DIM), 0.1),
        'diff_norm': 1.0 + nrm(ks[14], (L, DIF_DV), 0.02),
        'w_merge': nrm(ks[15], (L, N_BRANCH, D, D), D ** -0.5),
        'w_up': nrm(ks[16], (L, N_BRANCH, BRANCH_W, D), beta * BRANCH_W ** -0.5),
        'w_out': nrm(ks[17], (L, D, D), beta * D ** -0.5),
        'ln_g': 1.0 + nrm(ks[18], (L, D), 0.02),
        'ln_b': nrm(ks[19], (L, D), 0.02),
    }


def reference(x, c, ctx, c_ctx, w_ada, b_ada, w_in, gla_w_gate, gla_b_gate, gla_norm, win_sink,
              glb_q_norm, glb_k_norm, diff_lambda, diff_norm, w_merge, w_up, w_out, ln_g, ln_b):
    n = x.shape[1]
    n_rows = n // GRID_W
    row = jnp.broadcast_to(jnp.arange(n_rows)[:, None], (n_rows, GRID_W)).reshape(n).astype(jnp.float32)
    col = jnp.broadcast_to(jnp.arange(GRID_W)[None, :], (n_rows, GRID_W)).reshape(n).astype(jnp.float32)
    freqs = ROPE_BASE ** (-jnp.arange(ROPE_HALF, dtype=jnp.float32) / ROPE_HALF)
    ang = jnp.stack([row[:, None] * freqs, col[:, None] * freqs], axis=1)
    cos, sin = jnp.cos(ang), jnp.sin(ang)
    xc = ctx
    for l in range(DEPTH):
        lam_init = 0.8 - 0.6 * math.exp(-0.3 * l)
        x, xc = hybrid_layer(x, xc, c, c_ctx, w_ada[l], b_ada[l], w_in[l], gla_w_gate[l], gla_b_gate[l],
                             gla_norm[l], win_sink[l], glb_q_norm[l], glb_k_norm[l], diff_lambda[l],
                             diff_norm[l], w_merge[l], w_up[l], w_out[l], ln_g[l], ln_b[l],
                             cos, sin, lam_init, l < DEPTH - 1)
    return x
```

```python
import contextlib
import math
import numpy as np
import concourse.bass as bass
import concourse.mybir as mybir
from concourse.bass_utils import run_bass_kernel_spmd

F32 = mybir.dt.float32
BF16 = mybir.dt.bfloat16
AF = mybir.ActivationFunctionType
ALU = mybir.AluOpType
AX = mybir.AxisListType

D = 1024
NCTX = 256
NX = 4096
T = NCTX + NX
NT = T // 128
INC = 5920
DEPTH = 2
EPS = 1e-6
BLOCKS = [(0, 256)] + [(256 + 512 * i, 512) for i in range(8)]


class Buf:
    __slots__ = ("name", "writers", "readers", "psum")

    def __init__(self, name, psum=False):
        self.name = name
        self.writers = {}
        self.readers = {}
        self.psum = psum


class Sched:
    NS = 8
    ENGS = ("pe", "act", "dve", "pool", "sp")

    def __init__(self, nc, needed=None, es=None):
        self.nc = nc
        self.dry = needed is None
        self.needed = {e: set() for e in self.ENGS} if self.dry else needed
        self.rank = None
        if not self.dry:
            self.rank = {}
            for e in self.ENGS:
                self.rank[e] = {idx: i + 1 for i, idx in enumerate(sorted(self.needed[e]))}
        self.h = {"pe": nc.tensor, "act": nc.scalar, "dve": nc.vector, "pool": nc.gpsimd, "sp": nc.sync}
        self.count = {e: 0 for e in self.ENGS}
        self.seen = {e: {} for e in self.ENGS}
        self.ndma = {e: 0 for e in self.ENGS}
        self.sem = {}
        self.n_wait = 0
        self.n_ins = 0
        if not self.dry:
            for e in self.ENGS:
                self.sem[("p", e)] = es.enter_context(nc.semaphore("prog_" + e))
            for q in ("sp", "pool", "act"):
                for s in range(self.NS):
                    self.sem[("d", q, s)] = es.enter_context(nc.semaphore(f"dma_{q}_{s}"))

    def _wait(self, eng, key, val):
        if self.seen[eng].get(key, 0) >= val:
            return
        self.seen[eng][key] = val
        self.n_wait += 1
        if key[0] == "p":
            if self.dry:
                self.needed[key[1]].add(val - 1)
                return
            v = self.rank[key[1]][val - 1]
        else:
            if self.dry:
                return
            v = val
        self.h[eng].wait_ge(self.sem[key], v)

    def _deps(self, eng, reads, writes, skip_own=True):
        own = ("p", eng) if skip_own else None
        waits = {}
        for b in reads:
            for k, v in b.writers.items():
                if waits.get(k, 0) < v:
                    waits[k] = v
            if b.psum:
                for k, v in b.readers.items():
                    if k == own:
                        continue
                    if waits.get(k, 0) < v:
                        waits[k] = v
        for b in writes:
            for k, v in b.writers.items():
                if k == own:
                    continue
                if waits.get(k, 0) < v:
                    waits[k] = v
            for k, v in b.readers.items():
                if k == own:
                    continue
                if waits.get(k, 0) < v:
                    waits[k] = v
        for k, v in waits.items():
            self._wait(eng, k, v)

    def op(self, eng, fn, reads=(), writes=()):
        self._deps(eng, reads, writes)
        idx = self.count[eng]
        self.count[eng] = idx + 1
        self.n_ins += 1
        key = ("p", eng)
        if not self.dry:
            ins = fn(self.h[eng])
            if idx in self.rank[eng]:
                ins.then_inc(self.sem[key], 1)
        for b in reads:
            b.readers[key] = idx + 1
        for b in writes:
            b.writers[key] = idx + 1

    def dma(self, q, out, in_, reads=(), writes=(), **kw):
        self._deps(q, reads, writes, skip_own=False)
        i = self.ndma[q]
        self.ndma[q] = i + 1
        self.n_ins += 1
        slot = i % self.NS
        key = ("d", q, slot)
        val = 16 * (i // self.NS + 1)
        if val > 16:
            self._wait(q, key, val - 16)
        if not self.dry:
            self.h[q].dma_start(out=out, in_=in_, **kw).then_inc(self.sem[key], 16)
        for b in reads:
            b.readers[key] = val
        for b in writes:
            b.writers[key] = val

    def barrier(self):
        toks = {}
        for e in self.ENGS:
            if self.count[e] > 0:
                toks[("p", e)] = self.count[e]
        for q in ("sp", "pool", "act"):
            n = self.ndma[q]
            for s in range(self.NS):
                if n > s:
                    last = ((n - 1 - s) // self.NS) * self.NS + s
                    toks[("d", q, s)] = 16 * (last // self.NS + 1)
        for e in self.ENGS:
            for k, v in toks.items():
                if k == ("p", e):
                    continue
                self._wait(e, k, v)

    def finish(self):
        self.barrier()


class Ctx:
    pass


_UID = [0]


def alloc(es, nc, name, shape, dt):
    _UID[0] += 1
    t = es.enter_context(nc.sbuf_tensor(f"sb{_UID[0]}_{name}", list(shape), dt))
    return t, Buf(name)


def rope_tables():
    half = 16
    freqs = (np.float32(10000.0) ** (-np.arange(half, dtype=np.float32) / np.float32(half))).astype(np.float32)
    t = np.arange(NX)
    row = (t // 64).astype(np.float32)
    col = (t % 64).astype(np.float32)
    ang = np.stack([row[:, None] * freqs[None, :], col[:, None] * freqs[None, :]], axis=1).astype(np.float32)
    cos = np.cos(ang).astype(np.float32)
    sin = np.sin(ang).astype(np.float32)
    CT = np.ones((128, T), np.float32)
    ST = np.zeros((128, T), np.float32)
    for p in range(128):
        d = p % 64
        axis = d // 32
        f = d % 16
        CT[p, NCTX:] = cos[:, axis, f]
        ST[p, NCTX:] = sin[:, axis, f]
    return CT, ST


def make_ctx(nc, S, es):
    K = Ctx()
    K.nc, K.S, K.es = nc, S, es
    K.ps = []
    K.psall = es.enter_context(nc.psum_tensor("psall", [128, 4096], F32))
    for i in range(8):
        K.ps.append((K.psall[:, i * 512:(i + 1) * 512], Buf(f"ps{i}", psum=True)))
    return K


def load_consts(K, dr):
    nc, S, es = K.nc, K.S, K.es
    K.ident, K.ident_b = alloc(es, nc, "ident", [128, 128], F32)
    S.dma("sp", K.ident[:], dr["ident"][:, :], writes=[K.ident_b])
    K.ones, K.ones_b = alloc(es, nc, "ones", [128, 128], F32)
    S.op("pool", lambda e: e.memset(K.ones[:], 1.0), writes=[K.ones_b])
    K.epsc, K.epsc_b = alloc(es, nc, "epsc", [128, 1], F32)
    S.op("pool", lambda e: e.memset(K.epsc[:], EPS), writes=[K.epsc_b])
    K.modT, K.modT_b = alloc(es, nc, "modT", [128, 24, 2], F32)
    K.gate, K.gate_b = [], []
    for n in range(2):
        g, g_b = alloc(es, nc, f"gate{n}", [128, 1024], F32)
        K.gate.append(g)
        K.gate_b.append(g_b)
    K.sc, K.sc_b = alloc(es, nc, "sc", [128, 8, 2], F32)
    craw, craw_b = alloc(es, nc, "craw", [128, 8, 2], F32)
    with nc.allow_non_contiguous_dma(reason="tiny conditioning vector load"):
        S.dma("sp", craw[:, :, 0], dr["c"].rearrange("(k p) -> p k", p=128), writes=[craw_b])
        S.dma("sp", craw[:, :, 1], dr["c_ctx"].rearrange("(k p) -> p k", p=128), writes=[craw_b])
    S.op("act", lambda e: e.activation(out=K.sc[:], in_=craw[:], func=AF.Silu), reads=[craw_b], writes=[K.sc_b])


def phase0_mod(K, dr, l):
    nc, S = K.nc, K.S
    with contextlib.ExitStack() as es:
        wst = [alloc(es, nc, f"wada{i}", [128, 8, 512], F32) for i in range(2)]
        K.scb, K.scb_b = alloc(es, nc, "scb", [128, 8, 2, 128], F32)
        for k in range(8):
            for n in range(2):
                S.op("dve", lambda e, k=k, n=n: e.tensor_scalar(out=K.scb[:, k, n, :], in0=K.ones[:], scalar1=K.sc[:, k, n:n + 1],
                                                              scalar2=None, op0=ALU.mult),
                     reads=[K.sc_b, K.ones_b], writes=[K.scb_b])
        bT, bT_b = alloc(es, nc, "badaT", [128, 24], F32)
        gb, gb_b = alloc(es, nc, "gbias", [128, 1024], F32)
        with nc.allow_non_contiguous_dma(reason="tiny bias load"):
            S.dma("sp", bT[:], dr["b_ada"][l].rearrange("(j p) -> p j", p=128), writes=[bT_b])
        S.dma("sp", gb[:], dr["b_ada"][l, 2048:3072].partition_broadcast(128), writes=[gb_b])
        wv = dr["w_ada"][l].rearrange("(k p) c -> p k c", p=128)
        pm, pm_b = K.ps[0]
        pmv = pm[:, 0:48].rearrange("p (j n) -> p j n", n=2)
        for g in range(6):
            w, w_b = wst[g % 2]
            S.dma("sp", w[:], wv[:, :, g * 512:(g + 1) * 512], writes=[w_b])
            for jj in range(4):
                j = g * 4 + jj
                for k in range(8):
                    S.op("pe", lambda e, w=w, jj=jj, j=j, k=k: e.matmul(pmv[:, j, :], lhsT=w[:, k, jj * 128:(jj + 1) * 128],
                                                                     rhs=K.sc[:, k, :], start=(k == 0), stop=(k == 7)),
                         reads=[w_b, K.sc_b], writes=[pm_b])
            if g >= 4:
                half = g - 4
                for n in range(2):
                    pg, pg_b = K.ps[1 + n]
                    for k in range(8):
                        S.op("pe", lambda e, w=w, n=n, k=k, pg=pg: e.matmul(pg[:], lhsT=K.scb[:, k, n, :], rhs=w[:, k, :],
                                                                         start=(k == 0), stop=(k == 7)),
                             reads=[w_b, K.scb_b], writes=[pg_b])
                    S.op("dve", lambda e, n=n, half=half, pg=pg: e.tensor_tensor(out=K.gate[n][:, half * 512:(half + 1) * 512], in0=pg[:],
                                                                              in1=gb[:, half * 512:(half + 1) * 512], op=ALU.add),
                         reads=[pg_b, gb_b], writes=[K.gate_b[n]])
        for n in range(2):
            S.op("dve", lambda e, n=n: e.tensor_tensor(out=K.modT[:, :, n], in0=pmv[:, :, n], in1=bT[:], op=ALU.add),
                 reads=[pm_b, bT_b], writes=[K.modT_b])
        S.op("dve", lambda e: e.tensor_scalar_add(out=K.modT[:, 8:16, :], in0=K.modT[:, 8:16, :], scalar1=1.0),
             reads=[K.modT_b], writes=[K.modT_b])
        S.barrier()


def phase1_ln(K, dr, l, x_src, c_src, src_b, hT, hT_b):
    nc, S = K.nc, K.S
    with contextlib.ExitStack() as es:
        xt = [alloc(es, nc, f"xt{i}", [128, 1024], F32) for i in range(3)]
        xn = [alloc(es, nc, f"xn{i}", [128, 1024], F32) for i in range(2)]
        st = [alloc(es, nc, f"st{i}", [128, 16], F32) for i in range(2)]
        ti = 0
        for bi, (t0, n) in enumerate(BLOCKS):
            mn = 1 if bi == 0 else 0
            for j in range(n // 128):
                tok = t0 + j * 128
                x_t, x_b = xt[ti % 3]
                n_t, n_b = xn[ti % 2]
                s_t, s_b = st[ti % 2]
                src = c_src[tok:tok + 128, :] if bi == 0 else x_src[tok - NCTX:tok - NCTX + 128, :]
                S.dma("sp", x_t[:], src, reads=[src_b], writes=[x_b])
                S.op("dve", lambda e, s_t=s_t, x_t=x_t: e.bn_stats(s_t[:, 0:6], x_t[:, 0:512]), reads=[x_b], writes=[s_b])
                S.op("dve", lambda e, s_t=s_t, x_t=x_t: e.bn_stats(s_t[:, 6:12], x_t[:, 512:1024]), reads=[x_b], writes=[s_b])
                S.op("dve", lambda e, s_t=s_t: e.bn_aggr(s_t[:, 12:14], s_t[:, 0:12]), reads=[s_b], writes=[s_b])
                S.op("act", lambda e, s_t=s_t: e.activation(out=s_t[:, 15:16], in_=s_t[:, 13:14], func=AF.Sqrt, bias=K.epsc[:, 0:1], scale=1.0),
                     reads=[s_b, K.epsc_b], writes=[s_b])
                S.op("dve", lambda e, s_t=s_t: e.reciprocal(out=s_t[:, 14:15], in_=s_t[:, 15:16]), reads=[s_b], writes=[s_b])
                S.op("dve", lambda e, s_t=s_t, x_t=x_t, n_t=n_t: e.tensor_scalar(out=n_t[:], in0=x_t[:], scalar1=s_t[:, 12:13],
                                                                            scalar2=s_t[:, 14:15], op0=ALU.subtract, op1=ALU.mult),
                     reads=[x_b, s_b], writes=[n_b])
                for k in range(8):
                    pk, pk_b = K.ps[k]
                    S.op("pe", lambda e, pk=pk, n_t=n_t, k=k, j=j: e.transpose(out=pk[:, j * 128:(j + 1) * 128],
                                                                          in_=n_t[:, k * 128:(k + 1) * 128], identity=K.ident[:]),
                         reads=[n_b, K.ident_b], writes=[pk_b])
                ti += 1
            for k in range(8):
                pk, pk_b = K.ps[k]
                if k % 2 == 0:
                    S.op("dve", lambda e, pk=pk, k=k: e.tensor_scalar(out=hT[:, k, t0:t0 + n], in0=pk[:, 0:n], scalar1=K.modT[:, 8 + k, mn:mn + 1],
                                                                  scalar2=K.modT[:, k, mn:mn + 1], op0=ALU.mult, op1=ALU.add),
                         reads=[pk_b, K.modT_b], writes=[hT_b])
                else:
                    S.op("act", lambda e, pk=pk, k=k: e.activation(out=hT[:, k, t0:t0 + n], in_=pk[:, 0:n], func=AF.Identity,
                                                               scale=K.modT[:, 8 + k, mn:mn + 1], bias=K.modT[:, k, mn:mn + 1]),
                         reads=[pk_b, K.modT_b], writes=[hT_b])
            S.dma("pool", dr["hT_d"][:, :, t0:t0 + n], hT[:, :, t0:t0 + n], reads=[hT_b], writes=[dr["hT_d_b"]])
        S.barrier()


def _fm_chunks():
    ch = []
    for c in range(2):
        ch.append((0 + 128 * c, 128, "qA"))
    for c in range(2):
        ch.append((256 + 128 * c, 128, "plain"))
    ch.append((1024, 32, "g"))
    for base in (1056, 2336, 3872, 5408):
        for c in range(4):
            ch.append((base + 128 * c, 128, "silu"))
    for c in range(4):
        ch.append((1568 + 128 * c, 128, "rope"))
    ch.append((2080, 128, "rope"))
    for c in range(4):
        ch.append((4384 + 128 * c, 128, "rope"))
    for c in range(2):
        ch.append((4896 + 128 * c, 128, "rope"))
    for c in range(4):
        ch.append((2848 + 128 * c, 128, "ropeq"))
    for c in range(2):
        ch.append((3360 + 128 * c, 128, "ropek"))
    return ch


VS_W = 1792


def phase2_proj(K, dr, l, hT, hT_b):
    nc, S = K.nc, K.S
    wv_all = dr["w_in"][l].rearrange("(k p) c -> p k c", p=128)
    with contextlib.ExitStack() as es:
        CT, CT_b = alloc(es, nc, "CT", [128, T], F32)
        ST, ST_b = alloc(es, nc, "ST", [128, T], F32)
        S.dma("sp", CT[:], dr["ropeC"][:, :], writes=[CT_b])
        S.dma("sp", ST[:], dr["ropeS"][:, :], writes=[ST_b])
        bd, bd_b = alloc(es, nc, "bd64", [128, 128], F32)
        S.dma("sp", bd[:], dr["bd64"][:, :], writes=[bd_b])
        gq, gq_b = alloc(es, nc, "gq", [128, 4], F32)
        with nc.allow_non_contiguous_dma(reason="tiny gain vectors"):
            for hh in range(2):
                for ci, nm in enumerate(("glb_q_norm", "glb_k_norm")):
                    src = dr[nm][l]
                    S.dma("sp", gq[hh * 64:(hh + 1) * 64, 2 * ci:2 * ci + 1], src.rearrange("(d o) -> d o", o=1), writes=[gq_b])
                    for a in range(2):
                        for hf in range(2):
                            p0 = hh * 64 + a * 32 + hf * 16
                            s0 = a * 32 + (1 - hf) * 16
                            S.dma("sp", gq[p0:p0 + 16, 2 * ci + 1:2 * ci + 2], src[s0:s0 + 16].rearrange("(d o) -> d o", o=1), writes=[gq_b])
        wt = [alloc(es, nc, f"wch{i}", [128, 8, 128], BF16) for i in range(3)]
        wr = [alloc(es, nc, f"wchR{i}", [128, 8, 128], BF16) for i in range(2)]
        stg = [alloc(es, nc, f"stg{i}", [128, 512], BF16) for i in range(4)]
        f32t = [alloc(es, nc, f"f32t{i}", [128, 512], F32) for i in range(8)]
        gst = [alloc(es, nc, f"gst{i}", [32, 512], F32) for i in range(2)]
        cnt = {"stg": 0, "f": 0, "bank": 0, "w": 0, "wr": 0, "g": 0}

        def nxt(name, pool):
            i = cnt[name]
            cnt[name] = i + 1
            return pool[i % len(pool)]

        def bank():
            return nxt("bank", K.ps)

        PT = dr["PT"]
        PT_b = dr["PT_b"]
        chunks = _fm_chunks()
        wv, wv_b = alloc(es, nc, "wvtok", [128, 8, 1408], BF16)
        wtiles = {}

        def load_w(i):
            if i >= len(chunks):
                return
            c0_, M_, _ = chunks[i]
            w_, wb_ = wt[i % 3]
            S.dma("pool", w_[:, :, 0:M_], wv_all[:, :, c0_:c0_ + M_], writes=[wb_])
            wtiles[i] = (w_, wb_)

        def perm_w(i):
            if i >= len(chunks) or not chunks[i][2].startswith("rope"):
                return
            w_, wb_ = wtiles[i]
            wR_, wRb_ = wr[i % 2]
            w4 = w_[:, :, :].rearrange("p k (g h f) -> p (k g) h f", h=2, f=16)
            r4 = wR_[:, :, :].rearrange("p k (g h f) -> p (k g) h f", h=2, f=16)
            S.op("pool", lambda e: e.tensor_scalar(out=r4[:, :, 0, :], in0=w4[:, :, 1, :], scalar1=-1.0, scalar2=None, op0=ALU.mult),
                 reads=[wb_], writes=[wRb_])
            S.op("pool", lambda e: e.tensor_copy(out=r4[:, :, 1, :], in_=w4[:, :, 0, :]), reads=[wb_], writes=[wRb_])

        load_w(0)
        load_w(1)
        for (d0, s0, m) in ((0, 512, 512), (512, 2208, 128), (640, 3616, 256), (896, 5152, 256), (1152, 256, 256)):
            S.dma("pool", wv[:, :, d0:d0 + m], wv_all[:, :, s0:s0 + m], writes=[wv_b])
        perm_w(0)
        for ci, (c0, M, kind) in enumerate(chunks):
            load_w(ci + 2)
            perm_w(ci + 1)
            w, w_b = wtiles[ci]
            isrope = kind.startswith("rope")
            if isrope:
                wR, wR_b = wr[ci % 2]
            for bi, (t0, n) in enumerate(BLOCKS):
                p1, p1_b = bank()
                for k in range(8):
                    S.op("pe", lambda e, k=k: e.matmul(p1[0:M, 0:n], lhsT=w[:, k, 0:M], rhs=hT[:, k, t0:t0 + n], start=(k == 0), stop=(k == 7)),
                         reads=[w_b, hT_b], writes=[p1_b])
                if isrope:
                    p2, p2_b = bank()
                    for k in range(8):
                        S.op("pe", lambda e, k=k: e.matmul(p2[0:M, 0:n], lhsT=wR[:, k, 0:M], rhs=hT[:, k, t0:t0 + n], start=(k == 0), stop=(k == 7)),
                             reads=[wR_b, hT_b], writes=[p2_b])
                if kind == "g":
                    g_t, g_b = nxt("g", gst)
                    S.op("act", lambda e: e.activation(out=g_t[0:32, 0:n], in_=p1[0:32, 0:n], func=AF.Copy), reads=[p1_b], writes=[g_b])
                    S.dma("sp", dr["GT"][:, t0:t0 + n], g_t[0:32, 0:n], reads=[g_b], writes=[dr["GT_b"]])
                    continue
                o_t, o_b = nxt("stg", stg)
                if kind == "plain":
                    S.op("act", lambda e: e.activation(out=o_t[:, 0:n], in_=p1[:, 0:n], func=AF.Copy), reads=[p1_b], writes=[o_b])
                elif kind == "qA":
                    S.op("act", lambda e: e.activation(out=o_t[:, 0:n], in_=p1[:, 0:n], func=AF.Copy, scale=0.125), reads=[p1_b], writes=[o_b])
                elif kind == "silu":
                    S.op("act", lambda e: e.activation(out=o_t[:, 0:n], in_=p1[:, 0:n], func=AF.Silu), reads=[p1_b], writes=[o_b])
                elif kind == "rope":
                    a_t, a_b = nxt("f", f32t)
                    b_t, b_b = nxt("f", f32t)
                    S.op("dve", lambda e: e.tensor_tensor(out=a_t[:, 0:n], in0=p1[:, 0:n], in1=CT[:, t0:t0 + n], op=ALU.mult),
                         reads=[p1_b, CT_b], writes=[a_b])
                    S.op("dve", lambda e: e.tensor_tensor(out=b_t[:, 0:n], in0=p2[:, 0:n], in1=ST[:, t0:t0 + n], op=ALU.mult),
                         reads=[p2_b, ST_b], writes=[b_b])
                    S.op("pool", lambda e: e.tensor_tensor(out=o_t[:, 0:n], in0=a_t[:, 0:n], in1=b_t[:, 0:n], op=ALU.add),
                         reads=[a_b, b_b], writes=[o_b])
                else:
                    gi = 0 if kind == "ropeq" else 2
                    sq_t, sq_b = nxt("f", f32t)
                    a_t, a_b = nxt("f", f32t)
                    b_t, b_b = nxt("f", f32t)
                    r_t, r_b = nxt("f", f32t)
                    S.op("act", lambda e: e.activation(out=sq_t[:, 0:n], in_=p1[:, 0:n], func=AF.Square), reads=[p1_b], writes=[sq_b])
                    p3, p3_b = bank()
                    S.op("pe", lambda e: e.matmul(p3[:, 0:n], lhsT=bd[:], rhs=sq_t[:, 0:n], start=True, stop=True),
                         reads=[bd_b, sq_b], writes=[p3_b])
                    S.op("act", lambda e: e.activation(out=r_t[:, 0:n], in_=p3[:, 0:n], func=AF.Sqrt, bias=K.epsc[:, 0:1], scale=1.0),
                         reads=[p3_b, K.epsc_b], writes=[r_b])
                    S.op("dve", lambda e: e.reciprocal(out=r_t[:, 0:n], in_=r_t[:, 0:n]), reads=[r_b], writes=[r_b])
                    S.op("dve", lambda e: e.scalar_tensor_tensor(out=a_t[:, 0:n], in0=p1[:, 0:n], scalar=gq[:, gi:gi + 1], in1=CT[:, t0:t0 + n],
                                                               op0=ALU.mult, op1=ALU.mult), reads=[p1_b, CT_b, gq_b], writes=[a_b])
                    S.op("dve", lambda e: e.scalar_tensor_tensor(out=b_t[:, 0:n], in0=p2[:, 0:n], scalar=gq[:, gi + 1:gi + 2], in1=ST[:, t0:t0 + n],
                                                               op0=ALU.mult, op1=ALU.mult), reads=[p2_b, ST_b, gq_b], writes=[b_b])
                    S.op("pool", lambda e: e.tensor_tensor(out=a_t[:, 0:n], in0=a_t[:, 0:n], in1=b_t[:, 0:n], op=ALU.add),
                         reads=[a_b, b_b], writes=[a_b])
                    S.op("pool", lambda e: e.tensor_tensor(out=o_t[:, 0:n], in0=a_t[:, 0:n], in1=r_t[:, 0:n], op=ALU.mult),
                         reads=[a_b, r_b], writes=[o_b])
                S.dma("sp", PT[c0:c0 + M, t0:t0 + n], o_t[0:M, 0:n], reads=[o_b], writes=[PT_b])
        vst = [alloc(es, nc, f"vst{i}", [128, VS_W], BF16) for i in range(2)]
        for v_t, v_b in vst:
            S.op("pool", lambda e, v_t=v_t: e.memset(v_t[:, 512:1280], 1.0), writes=[v_b])
        for j in range(NT):
            v_t, v_b = vst[j % 2]
            pa, pa_b = bank()
            pb, pb_b = bank()
            pc, pc_b = bank()
            for (pp, pp_b, c0, m) in ((pa, pa_b, 0, 512), (pb, pb_b, 512, 512), (pc, pc_b, 1024, 384)):
                for k in range(8):
                    S.op("pe", lambda e, k=k, pp=pp, c0=c0, m=m: e.matmul(pp[:, 0:m], lhsT=hT[:, k, j * 128:(j + 1) * 128], rhs=wv[:, k, c0:c0 + m],
                                                                        start=(k == 0), stop=(k == 7)), reads=[hT_b, wv_b], writes=[pp_b])
            S.op("act", lambda e: e.activation(out=v_t[:, 0:512], in_=pa[:, 0:512], func=AF.Copy), reads=[pa_b], writes=[v_b])
            S.op("dve", lambda e: e.tensor_copy(out=v_t[:, 512:768].rearrange("p (h c) -> p h c", c=128)[:, :, 0:64],
                                                in_=pb[:, 0:128].rearrange("p (h c) -> p h c", c=64)), reads=[pb_b], writes=[v_b])
            S.op("dve", lambda e: e.tensor_copy(out=v_t[:, 768:1280].rearrange("p (h c) -> p h c", c=128)[:, :, 0:64],
                                                in_=pb[:, 128:384].rearrange("p (h c) -> p h c", c=64)), reads=[pb_b], writes=[v_b])
            S.op("act", lambda e: e.activation(out=v_t[:, 1280:1408], in_=pb[:, 384:512], func=AF.Copy), reads=[pb_b], writes=[v_b])
            S.op("act", lambda e: e.activation(out=v_t[:, 1408:1792], in_=pc[:, 0:384], func=AF.Copy), reads=[pc_b], writes=[v_b])
            S.dma("sp", dr["VS"][:, j, :], v_t[:], reads=[v_b], writes=[dr["VS_b"]])
        S.barrier()


def _qblocks(need_ctx):
    qb = []
    if need_ctx:
        qb.append((0, 256, [0, 1]))
    for i in range(8):
        qb.append((256 + 512 * i, 512, list(range(NT))))
    return qb


def _emit_pairs(steps, emit_qk, emit_pv, skp=2, defer=12):
    pend = []
    deferred = []

    def tick(flush=False):
        for d_ in list(deferred):
            d_[0] -= 1
            if d_[0] <= 0 or flush:
                deferred.remove(d_)
                d_[1]()

    def run_pv(st):
        t_ = emit_pv(st)
        if t_ is not None:
            deferred.append([defer, t_])

    for p in range(0, len(steps), 2):
        pair = steps[p:p + 2]
        if pair[0].get("newg") or pair[0].get("newq"):
            while pend:
                for st in pend.pop(0):
                    run_pv(st)
        for st in pair:
            emit_qk(st)
        pend.append(pair)
        if len(pend) > skp:
            for st in pend.pop(0):
                run_pv(st)
        tick()
    while pend:
        for st in pend.pop(0):
            run_pv(st)
    tick(flush=True)


def phase3_global(K, dr, l, need_ctx):
    nc, S = K.nc, K.S
    PT, VS, BR = dr["PT"], dr["VS"], dr["BR"]
    with contextlib.ExitStack() as es:
        sets = []
        for i in range(2):
            kt = alloc(es, nc, f"c_kt{i}", [128, T], BF16)
            qt = alloc(es, nc, f"c_qt{i}", [128, T], BF16)
            vv = alloc(es, nc, f"c_v{i}", [128, NT, 128], BF16)
            sets.append((kt, qt, vv))
        pts = [alloc(es, nc, f"c_pt{i}", [128, 1024], BF16) for i in range(4)]
        dens = [alloc(es, nc, f"c_den{i}", [64, 512], F32) for i in range(4)]
        outs = [alloc(es, nc, f"c_out{i}", [64, 512], BF16) for i in range(4)]
        ci = {"s": 0, "pt": 0, "o": 0}

        def load(g):
            (kt, kt_b), (qt, qt_b), (vv, vv_b) = sets[g % 2]
            r0 = 3360 + 64 * g
            S.dma("sp", kt[0:64, :], PT[r0:r0 + 64, :], reads=[dr["PT_b"]], writes=[kt_b])
            S.dma("sp", kt[64:128, :], PT[r0:r0 + 64, :], reads=[dr["PT_b"]], writes=[kt_b])
            q0 = 2848 + 128 * g
            S.dma("sp", qt[:, :], PT[q0:q0 + 128, :], reads=[dr["PT_b"]], writes=[qt_b])
            S.dma("sp", vv[:, :, :], VS[:, :, 768 + 128 * g:768 + 128 * (g + 1)], reads=[dr["VS_b"]], writes=[vv_b])

        steps = []
        for g in range(4):
            for bi_, (t0, n, ktiles) in enumerate(_qblocks(need_ctx)):
                for ji, j in enumerate(ktiles):
                    for hh in range(2):
                        steps.append(dict(g=g, hh=hh, t0=t0, n=n, j=j, first=(ji == 0), last=(ji == len(ktiles) - 1),
                                          newg=(hh == 0 and bi_ == 0 and ji == 0), blk=g * 16 + bi_))

        def emit_qk(st):
            g, hh, t0, n, j = st["g"], st["hh"], st["t0"], st["n"], st["j"]
            if st["newg"]:
                if g == 0:
                    load(0)
                if g + 1 < 4:
                    load(g + 1)
            (kt, kt_b), (qt, qt_b), (vv, vv_b) = sets[g % 2]
            hp = slice(hh * 64, hh * 64 + 64)
            bk = ci["s"] % 4
            psx, ps_b = K.ps[bk]
            ci["s"] += 1
            if hh == 0:
                pt, pt_b = pts[ci["pt"] % len(pts)]
                ci["pt"] += 1
                ci["cur"] = (pt, pt_b)
            pt, pt_b = ci["cur"]
            st["pt"] = (pt, pt_b)
            S.op("pe", lambda e: e.matmul(psx[:, 0:n], lhsT=kt[hp, j * 128:(j + 1) * 128], rhs=qt[hp, t0:t0 + n], start=True, stop=True),
                 reads=[kt_b, qt_b], writes=[ps_b])
            if hh == 1:
                src = K.psall[:, (bk - 1) * 512:(bk + 1) * 512].rearrange("p (k c) -> p k c", c=512)[:, :, 0:n]
                S.op("act", lambda e: e.activation(out=pt[:, :].rearrange("p (k c) -> p k c", c=512)[:, :, 0:n], in_=src, func=AF.Exp, scale=0.125),
                     reads=[K.ps[bk - 1][1], ps_b], writes=[pt_b])

        def emit_pv(st):
            g, hh, t0, n, j = st["g"], st["hh"], st["t0"], st["n"], st["j"]
            (kt, kt_b), (qt, qt_b), (vv, vv_b) = sets[g % 2]
            pt, pt_b = st["pt"]
            po, po_b = K.ps[4 + 2 * (st["blk"] % 2) + hh]
            S.op("pe", lambda e: e.matmul(po[:, 0:n], lhsT=vv[:, j, :], rhs=pt[:, hh * 512:hh * 512 + n], start=st["first"], stop=st["last"]),
                 reads=[vv_b, pt_b], writes=[po_b])
            if st["last"]:
                h = 2 * g + hh
                dn, dn_b = dens[ci["o"] % len(dens)]
                ot, ot_b = outs[ci["o"] % len(outs)]
                ci["o"] += 1
                S.op("act", lambda e: e.activation(out=dn[0:64, 0:n], in_=po[64:128, 0:n], func=AF.Copy), reads=[po_b], writes=[dn_b])
                S.op("dve", lambda e: e.reciprocal(out=dn[0:64, 0:n], in_=dn[0:64, 0:n]), reads=[dn_b], writes=[dn_b])
                S.op("dve", lambda e: e.tensor_tensor(out=ot[0:64, 0:n], in0=po[0:64, 0:n], in1=dn[0:64, 0:n], op=ALU.mult),
                     reads=[po_b, dn_b], writes=[ot_b])
                S.dma("pool", BR[2, h * 64:(h + 1) * 64, t0:t0 + n], ot[0:64, 0:n], reads=[ot_b], writes=[dr["BR_b"]])

        _emit_pairs(steps, emit_qk, emit_pv)
        S.barrier()


def phase4_diff(K, dr, l, need_ctx, after_loads=None):
    nc, S = K.nc, K.S
    PT, VS, BR = dr["PT"], dr["VS"], dr["BR"]
    lam_init = 0.8 - 0.6 * math.exp(-0.3 * l)
    with contextlib.ExitStack() as es:
        lp, lp_b = alloc(es, nc, "d_lp", [128, 256], F32)
        sm, sm_b = alloc(es, nc, "d_sm", [128, 8], F32)
        S.dma("sp", lp[:], dr["diff_lambda"][l].rearrange("a d -> (a d)").partition_broadcast(128), writes=[lp_b])
        S.op("dve", lambda e: e.tensor_tensor(out=lp[:, 0:64], in0=lp[:, 0:64], in1=lp[:, 64:128], op=ALU.mult), reads=[lp_b], writes=[lp_b])
        S.op("dve", lambda e: e.tensor_tensor(out=lp[:, 128:192], in0=lp[:, 128:192], in1=lp[:, 192:256], op=ALU.mult), reads=[lp_b], writes=[lp_b])
        S.op("dve", lambda e: e.reduce_sum(out=sm[:, 0:1], in_=lp[:, 0:64], axis=AX.X), reads=[lp_b], writes=[sm_b])
        S.op("dve", lambda e: e.reduce_sum(out=sm[:, 1:2], in_=lp[:, 128:192], axis=AX.X), reads=[lp_b], writes=[sm_b])
        S.op("act", lambda e: e.activation(out=sm[:, 2:4], in_=sm[:, 0:2], func=AF.Exp), reads=[sm_b], writes=[sm_b])
        S.op("dve", lambda e: e.scalar_tensor_tensor(out=sm[:, 4:5], in0=sm[:, 3:4], scalar=-lam_init, in1=sm[:, 2:3], op0=ALU.add, op1=ALU.subtract),
             reads=[sm_b], writes=[sm_b])
        sg, sg_b = alloc(es, nc, "d_sg", [128, 1], F32)
        with nc.allow_non_contiguous_dma(reason="tiny gain vector"):
            S.dma("sp", sg[:], dr["diff_norm"][l].rearrange("(d o) -> d o", o=1), writes=[sg_b])
        S.op("dve", lambda e: e.tensor_scalar(out=sg[:], in0=sg[:], scalar1=1.0 - lam_init, scalar2=None, op0=ALU.mult), reads=[sg_b], writes=[sg_b])
        onesb, onesb_b = alloc(es, nc, "d_onesb", [128, 128], BF16)
        S.op("pool", lambda e: e.memset(onesb[:], 1.0), writes=[onesb_b])
        avg, avg_b = alloc(es, nc, "d_avg", [128, 128], F32)
        S.op("pool", lambda e: e.memset(avg[:], 1.0 / 128.0), writes=[avg_b])
        sets = []
        for i in range(2):
            kt = alloc(es, nc, f"d_kt{i}", [128, T], BF16)
            vv = alloc(es, nc, f"d_v{i}", [128, NT, 128], BF16)
            sets.append((kt, vv))
        qts = [alloc(es, nc, f"d_qt{i}", [128, T], BF16) for i in range(2)]
        pts = [alloc(es, nc, f"d_pt{i}", [128, 1024], BF16) for i in range(4)]
        f32t = [alloc(es, nc, f"d_f{i}", [128, 512], F32) for i in range(8)]
        outs = [alloc(es, nc, f"d_out{i}", [128, 512], BF16) for i in range(2)]
        ci = {"s": 0, "pt": 0, "o": 0, "f": 0}

        def nf():
            i = ci["f"]
            ci["f"] += 1
            return f32t[i % 8]

        def load_kv(g, what=("k", "v")):
            (kt, kt_b), (vv, vv_b) = sets[g % 2]
            r0 = 4896 + 128 * g
            if "k" in what:
                S.dma("sp", kt[:, :], PT[r0:r0 + 128, :], reads=[dr["PT_b"]], writes=[kt_b])
            if "v" in what:
                S.dma("sp", vv[:, :, :], VS[:, :, 1280 + 128 * g:1280 + 128 * (g + 1)], reads=[dr["VS_b"]], writes=[vv_b])

        def load_q(hq):
            qt, qt_b = qts[hq % 2]
            q0 = 4384 + 128 * hq
            S.dma("sp", qt[:, :], PT[q0:q0 + 128, :], reads=[dr["PT_b"]], writes=[qt_b])

        SK = 3
        steps = []
        for hq in range(4):
            for bi_, (t0, n, ktiles) in enumerate(_qblocks(need_ctx)):
                for ji, j in enumerate(ktiles):
                    for m in range(2):
                        steps.append(dict(hq=hq, t0=t0, n=n, j=j, m=m, first=(ji == 0), last=(ji == len(ktiles) - 1),
                                          newq=(bi_ == 0 and ji == 0 and m == 0)))
        acc = [(K.ps[4], K.ps[6]), (K.ps[5], K.ps[7])]

        def emit_qk(st):
            hq, t0, n, j, m = st["hq"], st["t0"], st["n"], st["j"], st["m"]
            if st["newq"]:
                if hq == 0:
                    load_kv(0, ("k",))
                    load_q(0)
                    load_kv(0, ("v",))
                    load_kv(1)
                    if after_loads is not None:
                        after_loads()
                if hq + 1 < 4:
                    load_q(hq + 1)
            (kt, kt_b), (vv, vv_b) = sets[(hq // 2) % 2]
            qt, qt_b = qts[hq % 2]
            mp = slice(m * 64, m * 64 + 64)
            bk = ci["s"] % 4
            psx, ps_b = K.ps[bk]
            ci["s"] += 1
            if m == 0:
                pt, pt_b = pts[ci["pt"] % len(pts)]
                ci["pt"] += 1
                ci["cur"] = (pt, pt_b)
            pt, pt_b = ci["cur"]
            st["pt"] = (pt, pt_b)
            S.op("pe", lambda e: e.matmul(psx[:, 0:n], lhsT=kt[mp, j * 128:(j + 1) * 128], rhs=qt[mp, t0:t0 + n], start=True, stop=True),
                 reads=[kt_b, qt_b], writes=[ps_b])
            if m == 1:
                src = K.psall[:, (bk - 1) * 512:(bk + 1) * 512].rearrange("p (k c) -> p k c", c=512)[:, :, 0:n]
                S.op("act", lambda e: e.activation(out=pt[:, :].rearrange("p (k c) -> p k c", c=512)[:, :, 0:n], in_=src, func=AF.Exp, scale=0.125),
                     reads=[K.ps[bk - 1][1], ps_b], writes=[pt_b])

        def emit_pv(st):
            hq, t0, n, j, m = st["hq"], st["t0"], st["n"], st["j"], st["m"]
            (kt, kt_b), (vv, vv_b) = sets[(hq // 2) % 2]
            pt, pt_b = st["pt"]
            (po, po_b), (pd, pd_b) = acc[m]
            st_, sp_ = st["first"], st["last"]
            S.op("pe", lambda e: e.matmul(po[:, 0:n], lhsT=vv[:, j, :], rhs=pt[:, m * 512:m * 512 + n], start=st_, stop=sp_), reads=[vv_b, pt_b], writes=[po_b])
            S.op("pe", lambda e: e.matmul(pd[:, 0:n], lhsT=onesb[:, :], rhs=pt[:, m * 512:m * 512 + n], start=st_, stop=sp_), reads=[onesb_b, pt_b], writes=[pd_b])
            if not (st["last"] and m == 1):
                return
            (po0, po0_b), (pd0, pd0_b) = acc[0]
            (po1, po1_b), (pd1, pd1_b) = acc[1]
            r0_, r0b = nf()
            r1_, r1b = nf()
            a_, ab = nf()
            b_, bb = nf()
            S.op("act", lambda e: e.activation(out=r0_[:, 0:n], in_=pd0[:, 0:n], func=AF.Copy), reads=[pd0_b], writes=[r0b])
            S.op("dve", lambda e: e.tensor_copy(out=a_[:, 0:n], in_=po0[:, 0:n]), reads=[po0_b], writes=[ab])
            S.op("act", lambda e: e.activation(out=r1_[:, 0:n], in_=pd1[:, 0:n], func=AF.Copy), reads=[pd1_b], writes=[r1b])
            S.op("dve", lambda e: e.tensor_copy(out=b_[:, 0:n], in_=po1[:, 0:n]), reads=[po1_b], writes=[bb])
            S.op("dve", lambda e: e.reciprocal(out=r0_[:, 0:n], in_=r0_[:, 0:n]), reads=[r0b], writes=[r0b])
            S.op("dve", lambda e: e.reciprocal(out=r1_[:, 0:n], in_=r1_[:, 0:n]), reads=[r1b], writes=[r1b])
            S.op("pool", lambda e: e.tensor_tensor(out=a_[:, 0:n], in0=a_[:, 0:n], in1=r0_[:, 0:n], op=ALU.mult), reads=[ab, r0b], writes=[ab])
            S.op("pool", lambda e: e.tensor_tensor(out=b_[:, 0:n], in0=b_[:, 0:n], in1=r1_[:, 0:n], op=ALU.mult), reads=[bb, r1b], writes=[bb])
            S.op("dve", lambda e: e.scalar_tensor_tensor(out=a_[:, 0:n], in0=b_[:, 0:n], scalar=sm[:, 4:5], in1=a_[:, 0:n], op0=ALU.mult, op1=ALU.add),
                 reads=[ab, bb, sm_b], writes=[ab])
            S.op("pool", lambda e: e.tensor_tensor(out=b_[:, 0:n], in0=a_[:, 0:n], in1=a_[:, 0:n], op=ALU.mult), reads=[ab], writes=[bb])
            return lambda: fin_tail(hq, t0, n, a_, ab, b_, bb, r0_, r0b, r1_, r1b)

        def fin_tail(hq, t0, n, a_, ab, b_, bb, r0_, r0b, r1_, r1b):
            psm, psm_b = K.ps[ci["s"] % 4]
            ci["s"] += 2
            S.op("pe", lambda e: e.matmul(psm[:, 0:n], lhsT=avg[:, :], rhs=b_[:, 0:n], start=True, stop=True), reads=[avg_b, bb], writes=[psm_b])
            S.op("act", lambda e: e.activation(out=r1_[:, 0:n], in_=psm[:, 0:n], func=AF.Ln, bias=K.epsc[:, 0:1], scale=1.0),
                 reads=[psm_b, K.epsc_b], writes=[r1b])
            S.op("act", lambda e: e.activation(out=r0_[:, 0:n], in_=r1_[:, 0:n], func=AF.Exp, scale=-0.5), reads=[r1b], writes=[r0b])
            ot, ot_b = outs[ci["o"] % 2]
            ci["o"] += 1
            S.op("dve", lambda e: e.scalar_tensor_tensor(out=ot[:, 0:n], in0=a_[:, 0:n], scalar=sg[:, 0:1], in1=r0_[:, 0:n], op0=ALU.mult, op1=ALU.mult),
                 reads=[ab, r0b, sg_b], writes=[ot_b])
            S.dma("pool", BR[3, hq * 128:(hq + 1) * 128, t0:t0 + n], ot[:, 0:n], reads=[ot_b], writes=[dr["BR_b"]])

        _emit_pairs(steps, emit_qk, emit_pv)
        S.barrier()


def band_mask():
    kl = np.arange(128)[:, None]
    ql = np.arange(128)[None, :]
    m0 = (kl >= ql).astype(np.float32)
    m1 = (kl <= ql).astype(np.float32)
    return np.stack([np.tile(m0, (1, 4)), np.tile(m1, (1, 4))], 0)


def phase5_window(K, dr, l, need_ctx):
    nc, S = K.nc, K.S
    PT, VS, BR = dr["PT"], dr["VS"], dr["BR"]
    with contextlib.ExitStack() as es:
        kt, kt_b = alloc(es, nc, "w_kt", [128, T], BF16)
        q4, q4_b = alloc(es, nc, "w_q4", [128, 4, T], BF16)
        vv, vv_b = alloc(es, nc, "w_v", [128, NT, 256], BF16)
        mk, mk_b = alloc(es, nc, "w_mask", [128, 2, 512], BF16)
        S.dma("sp", kt[:, :], PT[2080:2208, :], reads=[dr["PT_b"]], writes=[kt_b])
        for g in range(2):
            S.dma("sp", q4[g * 64:(g + 1) * 64, :, :], PT[1568 + 256 * g:1568 + 256 * (g + 1), :].rearrange("(i d) t -> d i t", d=64),
                  reads=[dr["PT_b"]], writes=[q4_b])
        S.dma("sp", vv[:, :, :], VS[:, :, 512:768], reads=[dr["VS_b"]], writes=[vv_b])
        for m in range(2):
            S.dma("sp", mk[:, m, :], dr["bandmask"][m], writes=[mk_b])
        sk, sk_b = alloc(es, nc, "w_sink", [1, 16], F32)
        S.dma("sp", sk[0:1, 0:8], dr["win_sink"][l].rearrange("(o h) -> o h", o=1), writes=[sk_b])
        S.op("act", lambda e: e.activation(out=sk[0:1, 8:16], in_=sk[0:1, 0:8], func=AF.Exp), reads=[sk_b], writes=[sk_b])
        srow, srow_b = alloc(es, nc, "w_srow", [1, 2, 512], F32)
        for h in range(8):
            S.op("dve", lambda e, h=h: e.tensor_scalar(out=srow[0:1, h // 4, (h % 4) * 128:(h % 4 + 1) * 128], in0=K.ones[0:1, 0:128],
                                                      scalar1=sk[0:1, 8 + h:9 + h], scalar2=None, op0=ALU.mult),
                 reads=[sk_b, K.ones_b], writes=[srow_b])
        sel, sel_b = alloc(es, nc, "w_sel", [1, 128], F32)
        S.op("pool", lambda e: e.memset(sel[0:1, 0:64], 0.0), writes=[sel_b])
        S.op("pool", lambda e: e.memset(sel[0:1, 64:128], 1.0), writes=[sel_b])
        pts = [alloc(es, nc, f"w_pt{i}", [128, 512], BF16) for i in range(6)]
        dens = [alloc(es, nc, f"w_den{i}", [64, 512], F32) for i in range(2)]
        outs = [alloc(es, nc, f"w_out{i}", [64, 512], BF16) for i in range(2)]
        ci = {"s": 0, "pt": 0, "o": 0}
        qblocks = []
        if need_ctx:
            qblocks += [(0, [(0, None), (1, None)]), (128, [(0, None), (1, None)])]
        for nb in range(32):
            kl = [(0, None), (1, None)]
            if nb > 0:
                kl.append((2 + nb - 1, 0))
            kl.append((2 + nb, None))
            if nb < 31:
                kl.append((2 + nb + 1, 1))
            qblocks.append((256 + 128 * nb, kl))
        steps = []
        bno = 0
        for g in range(2):
            for (t0, klist) in qblocks:
                for ji, (j, msk) in enumerate(klist):
                    steps.append(dict(g=g, t0=t0, j=j, msk=msk, first=(ji == 0), last=(ji == len(klist) - 1), blk=bno))
                bno += 1

        def emit_qk(st):
            g, t0, j, msk = st["g"], st["t0"], st["j"], st["msk"]
            gp = slice(g * 64, g * 64 + 64)
            psx, ps_b = K.ps[ci["s"] % 6]
            ci["s"] += 1
            pt, pt_b = pts[ci["pt"] % len(pts)]
            ci["pt"] += 1
            st["pt"] = (pt, pt_b)
            S.op("pe", lambda e: e.matmul(psx[:, :].rearrange("p (i q) -> p i q", i=4), lhsT=kt[gp, j * 128:(j + 1) * 128],
                                          rhs=q4[gp, :, t0:t0 + 128], start=True, stop=True), reads=[kt_b, q4_b], writes=[ps_b])
            S.op("act", lambda e: e.activation(out=pt[:, :], in_=psx[:, :], func=AF.Exp, scale=0.125), reads=[ps_b], writes=[pt_b])
            if msk is not None:
                S.op("pool", lambda e: e.tensor_tensor(out=pt[:, :], in0=pt[:, :], in1=mk[:, msk, :], op=ALU.mult),
                     reads=[pt_b, mk_b], writes=[pt_b])

        def emit_pv(st):
            g, t0, j = st["g"], st["t0"], st["j"]
            pt, pt_b = st["pt"]
            po, po_b = K.ps[6 + st["blk"] % 2]
            S.op("pe", lambda e: e.matmul(po[:, :], lhsT=vv[:, j, g * 128:(g + 1) * 128], rhs=pt[:, :], start=st["first"], stop=False),
                 reads=[vv_b, pt_b], writes=[po_b])
            if not st["last"]:
                return
            S.op("pe", lambda e: e.matmul(po[:, :], lhsT=sel[0:1, :], rhs=srow[0:1, g, :], start=False, stop=True),
                 reads=[sel_b, srow_b], writes=[po_b])
            dn, dn_b = dens[st["blk"] % 2]
            ot, ot_b = outs[st["blk"] % 2]
            S.op("act", lambda e: e.activation(out=dn[0:64, :], in_=po[64:128, :], func=AF.Copy), reads=[po_b], writes=[dn_b])
            S.op("dve", lambda e: e.reciprocal(out=dn[0:64, :], in_=dn[0:64, :]), reads=[dn_b], writes=[dn_b])
            S.op("dve", lambda e: e.tensor_tensor(out=ot[0:64, :], in0=po[0:64, :], in1=dn[0:64, :], op=ALU.mult), reads=[po_b, dn_b], writes=[ot_b])
            with nc.allow_non_contiguous_dma(reason="head-interleaved branch store"):
                S.dma("pool", BR[1, 256 * g:256 * (g + 1), t0:t0 + 128].rearrange("(i d) q -> d i q", d=64),
                      ot[0:64, :].rearrange("p (i q) -> p i q", i=4), reads=[ot_b], writes=[dr["BR_b"]])

        pend = []
        for st in steps:
            emit_qk(st)
            pend.append(st)
            if len(pend) > 3:
                emit_pv(pend.pop(0))
        while pend:
            emit_pv(pend.pop(0))
        S.barrier()


def phase7_weights(K, dr, l, es):
    nc, S = K.nc, K.S
    wm, wm_b = alloc(es, nc, "m_wm", [128, 4, 8, 1024], BF16)
    wu, wu_b = alloc(es, nc, "m_wu", [128, 4, 4, 1024], BF16)
    wo, wo_b = alloc(es, nc, "m_wo", [128, 8, 1024], BF16)
    def issue():
        for i in range(4):
            for kk in range(0, 8, 4):
                S.dma("pool", wm[:, i, kk:kk + 4, :], dr["w_merge"][l, i].rearrange("(k p) c -> p k c", p=128)[:, kk:kk + 4, :], writes=[wm_b])
            S.dma("pool", wu[:, i, :, :], dr["w_up"][l, i].rearrange("(k p) c -> p k c", p=128), writes=[wu_b])
        for kk in range(0, 8, 4):
            S.dma("pool", wo[:, kk:kk + 4, :], dr["w_out"][l].rearrange("(k p) c -> p k c", p=128)[:, kk:kk + 4, :], writes=[wo_b])

    return ((wm, wm_b), (wu, wu_b), (wo, wo_b)), issue


def phase7_merge(K, dr, l, need_ctx, x_src, c_src, src_b, x_dst, c_dst, x_dst_b, c_dst_b, W7):
    nc, S = K.nc, K.S
    alpha = (2 * DEPTH) ** 0.25
    with contextlib.ExitStack() as es:
        (wm, wm_b), (wu, wu_b), (wo, wo_b) = W7
        lng, lng_b = alloc(es, nc, "m_lng", [128, 1024], F32)
        lnb, lnb_b = alloc(es, nc, "m_lnb", [128, 1024], F32)
        S.dma("sp", lng[:], dr["ln_g"][l].partition_broadcast(128), writes=[lng_b])
        S.dma("sp", lnb[:], dr["ln_b"][l].partition_broadcast(128), writes=[lnb_b])
        hbs = [alloc(es, nc, f"m_h{i}", [128, 8, 512], BF16) for i in range(2)]
        brs = [alloc(es, nc, f"m_br{i}", [128, 16, 512], BF16) for i in range(1)]
        zs = [alloc(es, nc, f"m_z{i}", [128, 4, 512], BF16) for i in range(2)]
        sigs = [alloc(es, nc, f"m_sig{i}", [128, 512], F32) for i in range(2)]
        accs = [alloc(es, nc, f"m_acc{i}", [128, 512], F32) for i in range(2)]
        tmps = [alloc(es, nc, f"m_tmp{i}", [128, 512], F32) for i in range(2)]
        mTs = [alloc(es, nc, f"m_mT{i}", [128, 8, 512], BF16) for i in range(2)]
        xts = [alloc(es, nc, f"m_x{i}", [128, 1024], F32) for i in range(1)]
        rts = [alloc(es, nc, f"m_r{i}", [128, 1024], F32) for i in range(1)]
        sts = [alloc(es, nc, f"m_st{i}", [128, 16], F32) for i in range(2)]
        zrows = (1056, 2336, 3872, 5408)
        ci = {"b": 0, "sig": 0, "tmp": 0, "tile": 0}
        blocks = BLOCKS if need_ctx else BLOCKS[1:]
        nb = len(blocks)
        br, _ = brs[0]
        br_bs = [Buf(f"br{i}") for i in range(4)]

        def load_hb(bi):
            if bi >= nb:
                return
            t0, n = blocks[bi]
            hb, hb_b = hbs[bi % 2]
            S.dma("sp", hb[:, :, 0:n], dr["hT_d"][:, :, t0:t0 + n], reads=[dr["hT_d_b"]], writes=[hb_b])

        def issue_loads(bi):
            t0, n = blocks[bi]
            for i in range(4):
                zz, zz_b = zs[i % 2]
                br_b = br_bs[i]
                S.dma("sp", br[:, 4 * i:4 * i + 4, 0:n], dr["BR"][i, :, t0:t0 + n].rearrange("(k p) t -> p k t", p=128), reads=[dr["BR_b"]], writes=[br_b])
                S.dma("sp", zz[:, :, 0:n], dr["PT"][zrows[i]:zrows[i] + 512, t0:t0 + n].rearrange("(k p) t -> p k t", p=128),
                      reads=[dr["PT_b"]], writes=[zz_b])
                S.op("pool", lambda e, i=i, zz=zz: e.tensor_tensor(out=br[:, 4 * i:4 * i + 4, 0:n], in0=br[:, 4 * i:4 * i + 4, 0:n], in1=zz[:, :, 0:n], op=ALU.mult),
                     reads=[br_b, zz_b], writes=[br_b])

        def gu_chunk(bi, c):
            t0, n = blocks[bi]
            hb, hb_b = hbs[bi % 2]
            mT, mT_b = mTs[bi % 2]
            ac, ac_b = accs[c % 2]
            for i in range(4):
                pg, pg_b = K.ps[ci["b"] % 6]
                ci["b"] += 1
                pu, pu_b = K.ps[ci["b"] % 6]
                ci["b"] += 1
                for k in range(8):
                    S.op("pe", lambda e, k=k: e.matmul(pg[:, 0:n], lhsT=wm[:, i, k, c * 128:(c + 1) * 128], rhs=hb[:, k, 0:n], start=(k == 0), stop=(k == 7)),
                         reads=[wm_b, hb_b], writes=[pg_b])
                for k in range(4):
                    S.op("pe", lambda e, k=k: e.matmul(pu[:, 0:n], lhsT=wu[:, i, k, c * 128:(c + 1) * 128], rhs=br[:, 4 * i + k, 0:n], start=(k == 0), stop=(k == 3)),
                         reads=[wu_b, br_bs[i]], writes=[pu_b])
                sg, sg_b = sigs[ci["sig"] % 2]
                ci["sig"] += 1
                S.op("act", lambda e: e.activation(out=sg[:, 0:n], in_=pg[:, 0:n], func=AF.Sigmoid), reads=[pg_b], writes=[sg_b])
                if i == 0:
                    S.op("dve", lambda e: e.tensor_tensor(out=ac[:, 0:n], in0=pu[:, 0:n], in1=sg[:, 0:n], op=ALU.mult), reads=[pu_b, sg_b], writes=[ac_b])
                else:
                    tm, tm_b = tmps[ci["tmp"] % 2]
                    ci["tmp"] += 1
                    S.op("dve", lambda e: e.tensor_tensor(out=tm[:, 0:n], in0=pu[:, 0:n], in1=sg[:, 0:n], op=ALU.mult), reads=[pu_b, sg_b], writes=[tm_b])
                    if i < 3:
                        S.op("pool", lambda e: e.tensor_tensor(out=ac[:, 0:n], in0=ac[:, 0:n], in1=tm[:, 0:n], op=ALU.add), reads=[ac_b, tm_b], writes=[ac_b])
                    else:
                        S.op("pool", lambda e: e.tensor_tensor(out=mT[:, c, 0:n], in0=ac[:, 0:n], in1=tm[:, 0:n], op=ALU.add), reads=[ac_b, tm_b], writes=[mT_b])

        def out_tile(bi, j):
            t0, n = blocks[bi]
            isctx = (t0 == 0)
            gidx = 1 if isctx else 0
            mT, mT_b = mTs[bi % 2]
            tok = t0 + j * 128
            xt, xt_b = xts[0]
            rt, rt_b = rts[0]
            st, st_b = sts[ci["tile"] % 2]
            ci["tile"] += 1
            src = c_src[tok:tok + 128, :] if isctx else x_src[tok - NCTX:tok - NCTX + 128, :]
            S.dma("sp", xt[:], src, reads=[src_b], writes=[xt_b])
            for hf in range(2):
                pq, pq_b = K.ps[6 + hf]
                for k in range(8):
                    S.op("pe", lambda e, k=k: e.matmul(pq[:, :], lhsT=mT[:, k, j * 128:(j + 1) * 128], rhs=wo[:, k, hf * 512:(hf + 1) * 512], start=(k == 0), stop=(k == 7)),
                         reads=[mT_b, wo_b], writes=[pq_b])
                S.op("dve", lambda e: e.tensor_tensor(out=rt[:, hf * 512:(hf + 1) * 512], in0=pq[:, :], in1=K.gate[gidx][:, hf * 512:(hf + 1) * 512], op=ALU.mult),
                     reads=[pq_b, K.gate_b[gidx]], writes=[rt_b])
            S.op("dve", lambda e: e.scalar_tensor_tensor(out=rt[:], in0=xt[:], scalar=alpha, in1=rt[:], op0=ALU.mult, op1=ALU.add), reads=[xt_b, rt_b], writes=[rt_b])
            S.op("dve", lambda e: e.bn_stats(st[:, 0:6], rt[:, 0:512]), reads=[rt_b], writes=[st_b])
            S.op("dve", lambda e: e.bn_stats(st[:, 6:12], rt[:, 512:1024]), reads=[rt_b], writes=[st_b])
            S.op("dve", lambda e: e.bn_aggr(st[:, 12:14], st[:, 0:12]), reads=[st_b], writes=[st_b])
            S.op("act", lambda e: e.activation(out=st[:, 15:16], in_=st[:, 13:14], func=AF.Sqrt, bias=K.epsc[:, 0:1], scale=1.0), reads=[st_b, K.epsc_b], writes=[st_b])
            S.op("dve", lambda e: e.reciprocal(out=st[:, 14:15], in_=st[:, 15:16]), reads=[st_b], writes=[st_b])
            S.op("dve", lambda e: e.tensor_scalar(out=rt[:], in0=rt[:], scalar1=st[:, 12:13], scalar2=st[:, 14:15], op0=ALU.subtract, op1=ALU.mult), reads=[rt_b, st_b], writes=[rt_b])
            S.op("pool", lambda e: e.tensor_tensor(out=rt[:], in0=rt[:], in1=lng[:], op=ALU.mult), reads=[rt_b, lng_b], writes=[rt_b])
            S.op("pool", lambda e: e.tensor_tensor(out=xt[:], in0=rt[:], in1=lnb[:], op=ALU.add), reads=[rt_b, lnb_b], writes=[xt_b])
            if isctx:
                S.dma("sp", c_dst[tok:tok + 128, :], xt[:], reads=[xt_b], writes=[c_dst_b])
            else:
                S.dma("sp", x_dst[tok - NCTX:tok - NCTX + 128, :], xt[:], reads=[xt_b], writes=[x_dst_b])

        load_hb(0)
        issue_loads(0)
        load_hb(1)
        for c in range(8):
            gu_chunk(0, c)
        for bi in range(1, nb + 1):
            prev_tiles = list(range(blocks[bi - 1][1] // 128))
            if bi < nb:
                issue_loads(bi)
                load_hb(bi + 1)
                out_tile(bi - 1, prev_tiles.pop(0))
                for c in range(8):
                    gu_chunk(bi, c)
                    if c % 2 == 1 and prev_tiles:
                        out_tile(bi - 1, prev_tiles.pop(0))
            while prev_tiles:
                out_tile(bi - 1, prev_tiles.pop(0))
        S.barrier()


def tri_consts():
    s = np.arange(128)[:, None]
    c = np.arange(128)[None, :]
    return np.stack([(s <= c), (s >= c), (s > c), (s < c)], 0).astype(np.float32)


def gla_mask():
    t = tri_consts()
    return np.concatenate([t[0], t[0], t[1], t[1]], 1)


def phase6_gla(K, dr, l, need_ctx):
    nc, S = K.nc, K.S
    PT, VS, BR = dr["PT"], dr["VS"], dr["BR"]
    NI = -1.0 / 16.0
    with contextlib.ExitStack() as es:
        tri, tri_b = alloc(es, nc, "g_tri", [128, 4, 128], F32)
        S.dma("sp", tri[:], dr["tri"].rearrange("a s c -> s a c"), writes=[tri_b])
        mk, mk_b = alloc(es, nc, "g_mask", [128, 512], BF16)
        S.dma("sp", mk[:, :], dr["glamask"][:, :], writes=[mk_b])
        gts, gts_b = alloc(es, nc, "g_gt", [33, T], F32)
        S.dma("sp", gts[0:32, :], dr["GT"][:, :], reads=[dr["GT_b"]], writes=[gts_b])
        S.op("pool", lambda e: e.memset(gts[32:33, :], 1.0), writes=[gts_b])
        wg, wg_b = alloc(es, nc, "g_wg", [33, 512], F32)
        S.op("pool", lambda e: e.memset(wg[:, :], 0.0), writes=[wg_b])
        S.dma("sp", wg[0:16, 0:256], dr["gla_w_gate"][l, 0], writes=[wg_b])
        S.dma("sp", wg[16:32, 256:512], dr["gla_w_gate"][l, 1], writes=[wg_b])
        S.dma("sp", wg[32:33, :], dr["gla_b_gate"][l].rearrange("(o a) c -> o (a c)", o=1), writes=[wg_b])
        qT, qT_b = alloc(es, nc, "g_qT", [128, 2, T], BF16)
        kT, kT_b = alloc(es, nc, "g_kT", [128, 2, T], BF16)
        S.dma("sp", qT[:], PT[0:256, :].rearrange("(g p) t -> p g t", p=128), reads=[dr["PT_b"]], writes=[qT_b])
        S.dma("sp", kT[:], PT[256:512, :].rearrange("(g p) t -> p g t", p=128), reads=[dr["PT_b"]], writes=[kT_b])
        vall, vall_b = alloc(es, nc, "g_v", [128, NT, 512], BF16)
        ktok, ktok_b = alloc(es, nc, "g_ktok", [128, NT, 256], BF16)
        S.dma("sp", vall[:], VS[:, :, 0:512], reads=[dr["VS_b"]], writes=[vall_b])
        S.dma("sp", ktok[:], VS[:, :, 1536:1792], reads=[dr["VS_b"]], writes=[ktok_b])
        SBst, SBst_b = alloc(es, nc, "g_SBst", [128, 2, NT, 128], BF16)
        gnT, gnT_b = alloc(es, nc, "g_gnT", [128, 4], F32)
        with nc.allow_non_contiguous_dma(reason="tiny gain vector"):
            S.dma("sp", gnT[:], dr["gla_norm"][l].rearrange("(h v) -> v h", v=128), writes=[gnT_b])
        gnb, gnb_b = alloc(es, nc, "g_gnb", [128, 512], F32)
        for h in range(4):
            S.op("dve", lambda e, h=h: e.tensor_scalar(out=gnb[:, h * 128:(h + 1) * 128], in0=K.ones[:], scalar1=gnT[:, h:h + 1], scalar2=None, op0=ALU.mult),
                 reads=[gnT_b, K.ones_b], writes=[gnb_b])
        avg, avg_b = alloc(es, nc, "g_avg", [128, 128], BF16)
        S.op("pool", lambda e: e.memset(avg[:], 1.0 / 128.0), writes=[avg_b])
        sqbs = [alloc(es, nc, f"g_sqb{i}", [128, 512], BF16) for i in range(2)]
        SF, SF_b = alloc(es, nc, "g_SF", [128, 2, 128], F32)
        SFb, SFb_b = alloc(es, nc, "g_SFb", [128, 2, 128], BF16)
        SBf, SBf_b = alloc(es, nc, "g_SBf", [128, 2, 128], F32)
        for t_, b_ in ((SF, SF_b), (SFb, SFb_b), (SBf, SBf_b)):
            S.op("pool", lambda e, t_=t_: e.memset(t_[:], 0.0), writes=[b_])
        Lt = [alloc(es, nc, f"g_L{i}", [128, 512], F32) for i in range(2)]
        tmps = []
        for i in range(2):
            tmps.append(dict(
                et=alloc(es, nc, f"g_e{i}", [128, 512], F32), Ef=alloc(es, nc, f"g_Ef{i}", [128, 256], F32),
                kh=alloc(es, nc, f"g_kh{i}", [128, 256], BF16), dec=alloc(es, nc, f"g_dec{i}", [128, 2], F32),
                EQ=alloc(es, nc, f"g_EQ{i}", [128, 512], F32), EK=alloc(es, nc, f"g_EK{i}", [128, 512], F32),
                qe=alloc(es, nc, f"g_qe{i}", [128, 2, 2, 128], BF16), ke=alloc(es, nc, f"g_ke{i}", [128, 2, 2, 128], BF16),
                Am=alloc(es, nc, f"g_Am{i}", [128, 2, 512], BF16), sq=alloc(es, nc, f"g_sq{i}", [128, 512], F32),
                rs=alloc(es, nc, f"g_rs{i}", [128, 512], F32), tt=alloc(es, nc, f"g_tt{i}", [128, 512], F32)))
        tsel = {"i": 0}
        ys = [alloc(es, nc, f"g_y{i}", [128, 512], BF16) for i in range(2)]
        cnt = {"L": 0, "y": 0}
        (pZ, pZ_b), (pM, pM_b), (pC, pC_b), (pA0, pA0_b), (pA1, pA1_b), (pkv, pkv_b), (pO, pO_b), (pms, pms_b) = K.ps

        def gate_L(j, lo, hi):
            L, L_b = Lt[cnt["L"] % 2]
            cnt["L"] += 1
            et, et_b = tmps[tsel["i"] % 2]["et"]
            S.op("pe", lambda e: e.matmul(pZ[:, lo:hi], lhsT=gts[0:33, j * 128:(j + 1) * 128], rhs=wg[0:33, lo:hi], start=True, stop=True),
                 reads=[gts_b, wg_b], writes=[pZ_b])
            S.op("act", lambda e: e.activation(out=et[:, lo:hi], in_=pZ[:, lo:hi], func=AF.Exp, scale=-1.0), reads=[pZ_b], writes=[et_b])
            S.op("act", lambda e: e.activation(out=L[:, lo:hi], in_=et[:, lo:hi], func=AF.Ln, bias=K.ones[:, 0:1], scale=1.0), reads=[et_b, K.ones_b], writes=[L_b])
            return L, L_b

        def state_update(St, St_b, dc, dc_b):
            for grp in range(2):
                for hl in range(2):
                    hp = slice(hl * 64, hl * 64 + 64)
                    S.op("dve", lambda e, grp=grp, hl=hl, hp=hp: e.scalar_tensor_tensor(
                        out=St[hp, grp, :], in0=St[hp, grp, :], scalar=dc[hp, grp:grp + 1], in1=pkv[hp, grp * 256 + hl * 128:grp * 256 + hl * 128 + 128],
                        op0=ALU.mult, op1=ALU.add), reads=[St_b, dc_b, pkv_b], writes=[St_b])

        def _pipeline(gens):
            prev = None
            for g_ in gens:
                next(g_)
                if prev is not None:
                    for _ in prev:
                        pass
                prev = g_
            if prev is not None:
                for _ in prev:
                    pass

        order_b = [1, 0] + list(range(NT - 1, 1, -1))

        def tileB(j, seq):
            if j == 2:
                yield
                S.op("act", lambda e: e.activation(out=SBst[:, :, j, :], in_=SBf[:, :, :], func=AF.Copy), reads=[SBf_b], writes=[SBst_b])
                return
            tsel["i"] = seq
            tm_ = tmps[seq % 2]
            (Ef, Ef_b), (kh, kh_b), (dec, dec_b), (EQ, EQ_b), (EK, EK_b) = tm_["Ef"], tm_["kh"], tm_["dec"], tm_["EQ"], tm_["EK"]
            (qe, qe_b), (ke, ke_b), (Am, Am_b), (sq, sq_b), (rs, rs_b), (tt, tt_b) = tm_["qe"], tm_["ke"], tm_["Am"], tm_["sq"], tm_["rs"], tm_["tt"]
            L, L_b = gate_L(j, 256, 512)
            S.op("pe", lambda e: e.matmul(pM[:, 0:256], lhsT=tri[:, 3, :], rhs=L[:, 256:512], start=True, stop=True), reads=[tri_b, L_b], writes=[pM_b])
            S.op("act", lambda e: e.activation(out=Ef[:, :], in_=pM[:, 0:256], func=AF.Exp, scale=NI), reads=[pM_b], writes=[Ef_b])
            S.op("dve", lambda e: e.tensor_tensor(out=kh[:, :], in0=ktok[:, j, :], in1=Ef[:, :], op=ALU.mult), reads=[ktok_b, Ef_b], writes=[kh_b])
            for grp in range(2):
                S.op("pe", lambda e, grp=grp: e.matmul(pC[:, grp:grp + 1], lhsT=L[:, 256 + grp * 128:256 + (grp + 1) * 128], rhs=K.ones[:, 0:1], start=True, stop=True),
                     reads=[L_b, K.ones_b], writes=[pC_b])
            S.op("act", lambda e: e.activation(out=dec[:, 0:2], in_=pC[:, 0:2], func=AF.Exp, scale=NI), reads=[pC_b], writes=[dec_b])
            yield
            S.op("act", lambda e: e.activation(out=SBst[:, :, j, :], in_=SBf[:, :, :], func=AF.Copy), reads=[SBf_b], writes=[SBst_b])
            for grp in range(2):
                S.op("pe", lambda e, grp=grp: e.matmul(pkv[:, grp * 256:(grp + 1) * 256], lhsT=kh[:, grp * 128:(grp + 1) * 128], rhs=vall[:, j, grp * 256:(grp + 1) * 256],
                                                      start=True, stop=True), reads=[kh_b, vall_b], writes=[pkv_b])
            state_update(SBf, SBf_b, dec, dec_b)

        _pipeline([tileB(j, q_) for q_, j in enumerate(order_b)])

        def tileF(j, seq):
            tok = j * 128
            tsel["i"] = seq
            tm_ = tmps[seq % 2]
            (Ef, Ef_b), (kh, kh_b), (dec, dec_b), (EQ, EQ_b), (EK, EK_b) = tm_["Ef"], tm_["kh"], tm_["dec"], tm_["EQ"], tm_["EK"]
            (qe, qe_b), (ke, ke_b), (Am, Am_b), (sq, sq_b), (rs, rs_b), (tt, tt_b) = tm_["qe"], tm_["ke"], tm_["Am"], tm_["sq"], tm_["rs"], tm_["tt"]
            L, L_b = gate_L(j, 0, 512)
            S.op("pe", lambda e: e.matmul(pM[:, 0:256], lhsT=tri[:, 2, :], rhs=L[:, 0:256], start=True, stop=True), reads=[tri_b, L_b], writes=[pM_b])
            S.op("act", lambda e: e.activation(out=Ef[:, :], in_=pM[:, 0:256], func=AF.Exp, scale=NI), reads=[pM_b], writes=[Ef_b])
            S.op("dve", lambda e: e.tensor_tensor(out=kh[:, :], in0=ktok[:, j, :], in1=Ef[:, :], op=ALU.mult), reads=[ktok_b, Ef_b], writes=[kh_b])
            for d in range(2):
                for grp in range(2):
                    c0 = (d * 2 + grp) * 128
                    S.op("pe", lambda e, d=d, grp=grp, c0=c0: e.matmul(pC[:, c0:c0 + 128], lhsT=L[:, d * 256 + grp * 128:d * 256 + (grp + 1) * 128], rhs=tri[:, d, :],
                                                                     start=True, stop=True), reads=[L_b, tri_b], writes=[pC_b])
            S.op("act", lambda e: e.activation(out=EQ[:, :], in_=pC[:, :], func=AF.Exp, scale=NI), reads=[pC_b], writes=[EQ_b])
            S.op("act", lambda e: e.activation(out=EK[:, :], in_=pC[:, :], func=AF.Exp, scale=-NI), reads=[pC_b], writes=[EK_b])
            S.op("act", lambda e: e.activation(out=dec[:, 0:2], in_=pC[:, :].rearrange("p (a c) -> p a c", c=128)[:, 0:2, 127], func=AF.Exp, scale=NI),
                 reads=[pC_b], writes=[dec_b])
            for d in range(2):
                S.op("pool", lambda e, d=d: e.tensor_tensor(out=qe[:, d, :, :], in0=qT[:, :, tok:tok + 128],
                                                         in1=EQ[:, d * 256:(d + 1) * 256].rearrange("p (g c) -> p g c", c=128), op=ALU.mult),
                     reads=[qT_b, EQ_b], writes=[qe_b])
                S.op("dve", lambda e, d=d: e.tensor_tensor(out=ke[:, d, :, :], in0=kT[:, :, tok:tok + 128],
                                                        in1=EK[:, d * 256:(d + 1) * 256].rearrange("p (g c) -> p g c", c=128), op=ALU.mult),
                     reads=[kT_b, EK_b], writes=[ke_b])
            pAs = ((pA0, pA0_b), (pA1, pA1_b))
            for d in range(2):
                for h in range(4):
                    grp, hl = h // 2, h % 2
                    hp = slice(hl * 64, hl * 64 + 64)
                    pA, pA_b = pAs[hl]
                    cb = (d * 2 + grp) * 128
                    S.op("pe", lambda e, d=d, grp=grp, hp=hp, pA=pA, cb=cb: e.matmul(pA[:, cb:cb + 128], lhsT=ke[hp, d, grp, :], rhs=qe[hp, d, grp, :],
                                                                                   start=True, stop=True), reads=[ke_b, qe_b], writes=[pA_b])
            for hl in range(2):
                pA, pA_b = pAs[hl]
                S.op("dve", lambda e, hl=hl, pA=pA: e.tensor_tensor(out=Am[:, hl, :], in0=pA[:, :], in1=mk[:, :], op=ALU.mult), reads=[pA_b, mk_b], writes=[Am_b])
            yield
            for grp in range(2):
                S.op("pe", lambda e, grp=grp: e.matmul(pkv[:, grp * 256:(grp + 1) * 256], lhsT=kh[:, grp * 128:(grp + 1) * 128], rhs=vall[:, j, grp * 256:(grp + 1) * 256],
                                                      start=True, stop=True), reads=[kh_b, vall_b], writes=[pkv_b])
            if j >= 2 or need_ctx:
                for h in range(4):
                    grp, hl = h // 2, h % 2
                    hp = slice(hl * 64, hl * 64 + 64)
                    o_ap = pO[:, h * 128:(h + 1) * 128]
                    S.op("pe", lambda e, h=h, hl=hl, grp=grp: e.matmul(pO[:, h * 128:(h + 1) * 128], lhsT=vall[:, j, h * 128:(h + 1) * 128],
                                                                     rhs=Am[:, hl, grp * 128:(grp + 1) * 128], start=True, stop=False),
                         reads=[vall_b, Am_b], writes=[pO_b])
                    S.op("pe", lambda e, h=h, hl=hl, grp=grp: e.matmul(pO[:, h * 128:(h + 1) * 128], lhsT=vall[:, j, h * 128:(h + 1) * 128],
                                                                     rhs=Am[:, hl, (2 + grp) * 128:(3 + grp) * 128], start=False, stop=False),
                         reads=[vall_b, Am_b], writes=[pO_b])
                    S.op("pe", lambda e, h=h, grp=grp, hp=hp: e.matmul(pO[:, h * 128:(h + 1) * 128], lhsT=SFb[hp, grp, :], rhs=qe[hp, 0, grp, :], start=False, stop=False),
                         reads=[SFb_b, qe_b], writes=[pO_b])
                    S.op("pe", lambda e, h=h, grp=grp, hp=hp: e.matmul(pO[:, h * 128:(h + 1) * 128], lhsT=SBst[hp, grp, j, :], rhs=qe[hp, 1, grp, :], start=False, stop=True),
                         reads=[SBst_b, qe_b], writes=[pO_b])
                sqb, sqb_b = sqbs[seq % 2]
                S.op("act", lambda e: e.activation(out=sqb[:, :], in_=pO[:, :], func=AF.Square), reads=[pO_b], writes=[sqb_b])
                S.op("pe", lambda e: e.matmul(pms[:, :], lhsT=avg[:, :], rhs=sqb[:, :], start=True, stop=True), reads=[avg_b, sqb_b], writes=[pms_b])
                S.op("act", lambda e: e.activation(out=sq[:, :], in_=pms[:, :], func=AF.Ln, bias=K.epsc[:, 0:1], scale=1.0), reads=[pms_b, K.epsc_b], writes=[sq_b])
                S.op("act", lambda e: e.activation(out=rs[:, :], in_=sq[:, :], func=AF.Exp, scale=-0.5), reads=[sq_b], writes=[rs_b])
                S.op("dve", lambda e: e.tensor_tensor(out=tt[:, :], in0=pO[:, :], in1=rs[:, :], op=ALU.mult), reads=[pO_b, rs_b], writes=[tt_b])
                y, y_b = ys[cnt["y"] % 2]
                cnt["y"] += 1
                S.op("pool", lambda e: e.tensor_tensor(out=y[:, :], in0=tt[:, :], in1=gnb[:, :], op=ALU.mult), reads=[tt_b, gnb_b], writes=[y_b])
                with nc.allow_non_contiguous_dma(reason="head-interleaved branch store"):
                    S.dma("sp", BR[0, :, tok:tok + 128].rearrange("(h v) c -> v h c", v=128), y[:, :].rearrange("p (h c) -> p h c", c=128), reads=[y_b], writes=[dr["BR_b"]])
            if j < NT - 1:
                state_update(SF, SF_b, dec, dec_b)
                S.op("act", lambda e: e.activation(out=SFb[:, :, :], in_=SF[:, :, :], func=AF.Copy), reads=[SF_b], writes=[SFb_b])

        _pipeline([tileF(j, j) for j in range(NT)])
        S.barrier()


W_SPECS = [("w_ada", [2, D, 3 * D]), ("b_ada", [2, 3 * D]), ("w_in", [2, D, INC]), ("gla_w_gate", [2, 2, 16, 256]),
           ("gla_b_gate", [2, 2, 256]), ("gla_norm", [2, 512]), ("win_sink", [2, 8]), ("glb_q_norm", [2, 64]),
           ("glb_k_norm", [2, 64]), ("diff_lambda", [2, 4, 64]), ("diff_norm", [2, 128]), ("w_merge", [2, 4, D, D]),
           ("w_up", [2, 4, 512, D]), ("w_out", [2, D, D]), ("ln_g", [2, D]), ("ln_b", [2, D])]
C_SPECS = [("ident", [128, 128], F32), ("bd64", [128, 128], F32), ("ropeC", [128, T], F32), ("ropeS", [128, T], F32),
           ("bandmask", [2, 128, 512], BF16), ("tri", [4, 128, 128], F32), ("glamask", [128, 512], BF16)]


def host_consts():
    import ml_dtypes
    CTn, STn = rope_tables()
    bd = np.zeros((128, 128), np.float32)
    bd[:64, :64] = 1.0 / 64
    bd[64:, 64:] = 1.0 / 64
    return {"ident": np.eye(128, dtype=np.float32), "bd64": bd, "ropeC": CTn, "ropeS": STn,
            "bandmask": band_mask().astype(ml_dtypes.bfloat16), "tri": tri_consts(),
            "glamask": gla_mask().astype(ml_dtypes.bfloat16)}


def declare_io(nc, scratch_kind="Internal"):
    dr = {}
    dr["x"] = nc.dram_tensor("x", [NX, D], F32, kind="ExternalInput").ap()
    dr["ctx"] = nc.dram_tensor("ctx", [NCTX, D], F32, kind="ExternalInput").ap()
    dr["c"] = nc.dram_tensor("c", [D], F32, kind="ExternalInput").ap()
    dr["c_ctx"] = nc.dram_tensor("c_ctx", [D], F32, kind="ExternalInput").ap()
    for nm, shp in W_SPECS:
        dr[nm] = nc.dram_tensor(nm, shp, F32, kind="ExternalInput").ap()
    for nm, shp, dt in C_SPECS:
        dr[nm] = nc.dram_tensor(nm, shp, dt, kind="ExternalInput").ap()
    dr["out"] = nc.dram_tensor("out", [NX, D], F32, kind="ExternalOutput").ap()
    for nm, shp, dt in (("hT_d", [128, 8, T], BF16), ("PT", [INC, T], BF16), ("GT", [32, T], F32), ("VS", [128, NT, VS_W], BF16),
                        ("BR", [4, 512, T], BF16), ("x1", [NX, D], F32), ("c1", [NCTX, D], F32)):
        dr[nm] = nc.dram_tensor(nm, shp, dt, kind=scratch_kind).ap()
        dr[nm + "_b"] = Buf(nm)
    dr["in_b"] = Buf("inputs")
    dr["out_b"] = Buf("out")
    return dr


def emit_full(nc, S, es, layers=(0, 1), scratch_kind="Internal"):
    dr = declare_io(nc, scratch_kind)
    K = make_ctx(nc, S, es)
    load_consts(K, dr)
    for l in layers:
        need_ctx = l < DEPTH - 1
        if l == 0:
            x_src, c_src, src_b = dr["x"], dr["ctx"], dr["in_b"]
        else:
            x_src, c_src, src_b = dr["x1"], dr["c1"], dr["x1_b"]
        phase0_mod(K, dr, l)
        with contextlib.ExitStack() as es2:
            hT, hT_b = alloc(es2, nc, "hT", [128, 8, T], BF16)
            phase1_ln(K, dr, l, x_src, c_src, src_b, hT, hT_b)
            phase2_proj(K, dr, l, hT, hT_b)
        phase6_gla(K, dr, l, need_ctx)
        phase5_window(K, dr, l, need_ctx)
        phase3_global(K, dr, l, need_ctx)
        if l == DEPTH - 1:
            x_dst, x_dst_b = dr["out"], dr["out_b"]
        else:
            x_dst, x_dst_b = dr["x1"], dr["x1_b"]
        with contextlib.ExitStack() as esw:
            W7, issue_w7 = phase7_weights(K, dr, l, esw)
            phase4_diff(K, dr, l, need_ctx, after_loads=issue_w7)
            phase7_merge(K, dr, l, need_ctx, x_src, c_src, src_b, x_dst, dr["c1"], x_dst_b, dr["x1_b"], W7)
    S.finish()
    return dr


def build_program(layers=(0, 1), scratch_kind="Internal"):
    nc0 = bass.Bass("TRN2", target_bir_lowering=False)
    with contextlib.ExitStack() as es0:
        S0 = Sched(nc0)
        emit_full(nc0, S0, es0, layers, scratch_kind)
    nc = bass.Bass("TRN2", target_bir_lowering=False)
    es = contextlib.ExitStack()
    S = Sched(nc, needed=S0.needed, es=es)
    emit_full(nc, S, es, layers, scratch_kind)
    return nc, S, es


def kernel(**inputs):
    nc, S, es = build_program()
    consts = host_consts()
    B = inputs["x"].shape[0]
    shared = {nm: np.ascontiguousarray(np.asarray(inputs[nm], dtype=np.float32)) for nm, _ in W_SPECS}
    shared["c_ctx"] = np.ascontiguousarray(np.asarray(inputs["c_ctx"], dtype=np.float32))
    shared.update(consts)
    in_maps = []
    for b in range(B):
        m = dict(shared)
        m["x"] = np.ascontiguousarray(np.asarray(inputs["x"][b], dtype=np.float32))
        m["ctx"] = np.ascontiguousarray(np.asarray(inputs["ctx"][b], dtype=np.float32))
        m["c"] = np.ascontiguousarray(np.asarray(inputs["c"][b], dtype=np.float32))
        in_maps.append(m)
    res = run_bass_kernel_spmd(nc, in_maps, core_ids=list(range(B)))
    return np.stack([np.asarray(res.results[b]["out"], dtype=np.float32) for b in range(B)], axis=0)
```

```python
import contextlib
import math
import numpy as np
import concourse.bass as bass
import concourse.mybir as mybir
from concourse.bass_utils import run_bass_kernel_spmd

F32 = mybir.dt.float32
BF16 = mybir.dt.bfloat16
AF = mybir.ActivationFunctionType
ALU = mybir.AluOpType
AX = mybir.AxisListType

D = 1024
NCTX = 256
NX = 4096
T = NCTX + NX
NT = T // 128
INC = 5920
DEPTH = 2
EPS = 1e-6
BLOCKS = [(0, 256)] + [(256 + 512 * i, 512) for i in range(8)]


class Buf:
    __slots__ = ("name", "writers", "readers", "psum")

    def __init__(self, name, psum=False):
        self.name = name
        self.writers = {}
        self.readers = {}
        self.psum = psum


class Sched:
    NS = 8
    ENGS = ("pe", "act", "dve", "pool", "sp")

    def __init__(self, nc, needed=None, es=None):
        self.nc = nc
        self.dry = needed is None
        self.needed = {e: set() for e in self.ENGS} if self.dry else needed
        self.rank = None
        if not self.dry:
            self.rank = {}
            for e in self.ENGS:
                self.rank[e] = {idx: i + 1 for i, idx in enumerate(sorted(self.needed[e]))}
        self.h = {"pe": nc.tensor, "act": nc.scalar, "dve": nc.vector, "pool": nc.gpsimd, "sp": nc.sync}
        self.count = {e: 0 for e in self.ENGS}
        self.seen = {e: {} for e in self.ENGS}
        self.ndma = {e: 0 for e in self.ENGS}
        self.sem = {}
        self.n_wait = 0
        self.n_ins = 0
        if not self.dry:
            for e in self.ENGS:
                self.sem[("p", e)] = es.enter_context(nc.semaphore("prog_" + e))
            for q in ("sp", "pool", "act"):
                for s in range(self.NS):
                    self.sem[("d", q, s)] = es.enter_context(nc.semaphore(f"dma_{q}_{s}"))

    def _wait(self, eng, key, val):
        if self.seen[eng].get(key, 0) >= val:
            return
        self.seen[eng][key] = val
        self.n_wait += 1
        if key[0] == "p":
            if self.dry:
                self.needed[key[1]].add(val - 1)
                return
            v = self.rank[key[1]][val - 1]
        else:
            if self.dry:
                return
            v = val
        self.h[eng].wait_ge(self.sem[key], v)

    def _deps(self, eng, reads, writes, skip_own=True):
        own = ("p", eng) if skip_own else None
        waits = {}
        for b in reads:
            for k, v in b.writers.items():
                if waits.get(k, 0) < v:
                    waits[k] = v
            if b.psum:
                for k, v in b.readers.items():
                    if k == own:
                        continue
                    if waits.get(k, 0) < v:
                        waits[k] = v
        for b in writes:
            for k, v in b.writers.items():
                if k == own:
                    continue
                if waits.get(k, 0) < v:
                    waits[k] = v
            for k, v in b.readers.items():
                if k == own:
                    continue
                if waits.get(k, 0) < v:
                    waits[k] = v
        for k, v in waits.items():
            self._wait(eng, k, v)

    def op(self, eng, fn, reads=(), writes=()):
        self._deps(eng, reads, writes)
        idx = self.count[eng]
        self.count[eng] = idx + 1
        self.n_ins += 1
        key = ("p", eng)
        if not self.dry:
            ins = fn(self.h[eng])
            if idx in self.rank[eng]:
                ins.then_inc(self.sem[key], 1)
        for b in reads:
            b.readers[key] = idx + 1
        for b in writes:
            b.writers[key] = idx + 1

    def dma(self, q, out, in_, reads=(), writes=(), **kw):
        self._deps(q, reads, writes, skip_own=False)
        i = self.ndma[q]
        self.ndma[q] = i + 1
        self.n_ins += 1
        slot = i % self.NS
        key = ("d", q, slot)
        val = 16 * (i // self.NS + 1)
        if val > 16:
            self._wait(q, key, val - 16)
        if not self.dry:
            self.h[q].dma_start(out=out, in_=in_, **kw).then_inc(self.sem[key], 16)
        for b in reads:
            b.readers[key] = val
        for b in writes:
            b.writers[key] = val

    def barrier(self):
        toks = {}
        for e in self.ENGS:
            if self.count[e] > 0:
                toks[("p", e)] = self.count[e]
        for q in ("sp", "pool", "act"):
            n = self.ndma[q]
            for s in range(self.NS):
                if n > s:
                    last = ((n - 1 - s) // self.NS) * self.NS + s
                    toks[("d", q, s)] = 16 * (last // self.NS + 1)
        for e in self.ENGS:
            for k, v in toks.items():
                if k == ("p", e):
                    continue
                self._wait(e, k, v)

    def finish(self):
        self.barrier()


class Ctx:
    pass


_UID = [0]


def alloc(es, nc, name, shape, dt):
    _UID[0] += 1
    t = es.enter_context(nc.sbuf_tensor(f"sb{_UID[0]}_{name}", list(shape), dt))
    return t, Buf(name)


def rope_tables():
    half = 16
    freqs = (np.float32(10000.0) ** (-np.arange(half, dtype=np.float32) / np.float32(half))).astype(np.float32)
    t = np.arange(NX)
    row = (t // 64).astype(np.float32)
    col = (t % 64).astype(np.float32)
    ang = np.stack([row[:, None] * freqs[None, :], col[:, None] * freqs[None, :]], axis=1).astype(np.float32)
    cos = np.cos(ang).astype(np.float32)
    sin = np.sin(ang).astype(np.float32)
    CT = np.ones((128, T), np.float32)
    ST = np.zeros((128, T), np.float32)
    for p in range(128):
        d = p % 64
        axis = d // 32
        f = d % 16
        CT[p, NCTX:] = cos[:, axis, f]
        ST[p, NCTX:] = sin[:, axis, f]
    return CT, ST


def make_ctx(nc, S, es):
    K = Ctx()
    K.nc, K.S, K.es = nc, S, es
    K.ps = []
    K.psall = es.enter_context(nc.psum_tensor("psall", [128, 4096], F32))
    for i in range(8):
        K.ps.append((K.psall[:, i * 512:(i + 1) * 512], Buf(f"ps{i}", psum=True)))
    return K


def load_consts(K, dr):
    nc, S, es = K.nc, K.S, K.es
    K.ident, K.ident_b = alloc(es, nc, "ident", [128, 128], F32)
    S.dma("sp", K.ident[:], dr["ident"][:, :], writes=[K.ident_b])
    K.ones, K.ones_b = alloc(es, nc, "ones", [128, 128], F32)
    S.op("pool", lambda e: e.memset(K.ones[:], 1.0), writes=[K.ones_b])
    K.epsc, K.epsc_b = alloc(es, nc, "epsc", [128, 1], F32)
    S.op("pool", lambda e: e.memset(K.epsc[:], EPS), writes=[K.epsc_b])
    K.modT, K.modT_b = alloc(es, nc, "modT", [128, 24, 2], F32)
    K.gate, K.gate_b = [], []
    for n in range(2):
        g, g_b = alloc(es, nc, f"gate{n}", [128, 1024], F32)
        K.gate.append(g)
        K.gate_b.append(g_b)
    K.sc, K.sc_b = alloc(es, nc, "sc", [128, 8, 2], F32)
    craw, craw_b = alloc(es, nc, "craw", [128, 8, 2], F32)
    with nc.allow_non_contiguous_dma(reason="tiny conditioning vector load"):
        S.dma("sp", craw[:, :, 0], dr["c"].rearrange("(k p) -> p k", p=128), writes=[craw_b])
        S.dma("sp", craw[:, :, 1], dr["c_ctx"].rearrange("(k p) -> p k", p=128), writes=[craw_b])
    S.op("act", lambda e: e.activation(out=K.sc[:], in_=craw[:], func=AF.Silu), reads=[craw_b], writes=[K.sc_b])


def phase0_mod(K, dr, l):
    nc, S = K.nc, K.S
    with contextlib.ExitStack() as es:
        wst = [alloc(es, nc, f"wada{i}", [128, 8, 512], F32) for i in range(2)]
        K.scb, K.scb_b = alloc(es, nc, "scb", [128, 8, 2, 128], F32)
        for k in range(8):
            for n in range(2):
                S.op("dve", lambda e, k=k, n=n: e.tensor_scalar(out=K.scb[:, k, n, :], in0=K.ones[:], scalar1=K.sc[:, k, n:n + 1],
                                                              scalar2=None, op0=ALU.mult),
                     reads=[K.sc_b, K.ones_b], writes=[K.scb_b])
        bT, bT_b = alloc(es, nc, "badaT", [128, 24], F32)
        gb, gb_b = alloc(es, nc, "gbias", [128, 1024], F32)
        with nc.allow_non_contiguous_dma(reason="tiny bias load"):
            S.dma("sp", bT[:], dr["b_ada"][l].rearrange("(j p) -> p j", p=128), writes=[bT_b])
        S.dma("sp", gb[:], dr["b_ada"][l, 2048:3072].partition_broadcast(128), writes=[gb_b])
        wv = dr["w_ada"][l].rearrange("(k p) c -> p k c", p=128)
        pm, pm_b = K.ps[0]
        pmv = pm[:, 0:48].rearrange("p (j n) -> p j n", n=2)
        for g in range(6):
            w, w_b = wst[g % 2]
            S.dma("sp", w[:], wv[:, :, g * 512:(g + 1) * 512], writes=[w_b])
            for jj in range(4):
                j = g * 4 + jj
                for k in range(8):
                    S.op("pe", lambda e, w=w, jj=jj, j=j, k=k: e.matmul(pmv[:, j, :], lhsT=w[:, k, jj * 128:(jj + 1) * 128],
                                                                     rhs=K.sc[:, k, :], start=(k == 0), stop=(k == 7)),
                         reads=[w_b, K.sc_b], writes=[pm_b])
            if g >= 4:
                half = g - 4
                for n in range(2):
                    pg, pg_b = K.ps[1 + n]
                    for k in range(8):
                        S.op("pe", lambda e, w=w, n=n, k=k, pg=pg: e.matmul(pg[:], lhsT=K.scb[:, k, n, :], rhs=w[:, k, :],
                                                                         start=(k == 0), stop=(k == 7)),
                             reads=[w_b, K.scb_b], writes=[pg_b])
                    S.op("dve", lambda e, n=n, half=half, pg=pg: e.tensor_tensor(out=K.gate[n][:, half * 512:(half + 1) * 512], in0=pg[:],
                                                                              in1=gb[:, half * 512:(half + 1) * 512], op=ALU.add),
                         reads=[pg_b, gb_b], writes=[K.gate_b[n]])
        for n in range(2):
            S.op("dve", lambda e, n=n: e.tensor_tensor(out=K.modT[:, :, n], in0=pmv[:, :, n], in1=bT[:], op=ALU.add),
                 reads=[pm_b, bT_b], writes=[K.modT_b])
        S.op("dve", lambda e: e.tensor_scalar_add(out=K.modT[:, 8:16, :], in0=K.modT[:, 8:16, :], scalar1=1.0),
             reads=[K.modT_b], writes=[K.modT_b])
        S.barrier()


def phase1_ln(K, dr, l, x_src, c_src, src_b, hT, hT_b):
    nc, S = K.nc, K.S
    with contextlib.ExitStack() as es:
        xt = [alloc(es, nc, f"xt{i}", [128, 1024], F32) for i in range(3)]
        xn = [alloc(es, nc, f"xn{i}", [128, 1024], F32) for i in range(2)]
        st = [alloc(es, nc, f"st{i}", [128, 16], F32) for i in range(2)]
        ti = 0
        for bi, (t0, n) in enumerate(BLOCKS):
            mn = 1 if bi == 0 else 0
            for j in range(n // 128):
                tok = t0 + j * 128
                x_t, x_b = xt[ti % 3]
                n_t, n_b = xn[ti % 2]
                s_t, s_b = st[ti % 2]
                src = c_src[tok:tok + 128, :] if bi == 0 else x_src[tok - NCTX:tok - NCTX + 128, :]
                S.dma("sp", x_t[:], src, reads=[src_b], writes=[x_b])
                S.op("dve", lambda e, s_t=s_t, x_t=x_t: e.bn_stats(s_t[:, 0:6], x_t[:, 0:512]), reads=[x_b], writes=[s_b])
                S.op("dve", lambda e, s_t=s_t, x_t=x_t: e.bn_stats(s_t[:, 6:12], x_t[:, 512:1024]), reads=[x_b], writes=[s_b])
                S.op("dve", lambda e, s_t=s_t: e.bn_aggr(s_t[:, 12:14], s_t[:, 0:12]), reads=[s_b], writes=[s_b])
                S.op("act", lambda e, s_t=s_t: e.activation(out=s_t[:, 15:16], in_=s_t[:, 13:14], func=AF.Sqrt, bias=K.epsc[:, 0:1], scale=1.0),
                     reads=[s_b, K.epsc_b], writes=[s_b])
                S.op("dve", lambda e, s_t=s_t: e.reciprocal(out=s_t[:, 14:15], in_=s_t[:, 15:16]), reads=[s_b], writes=[s_b])
                S.op("dve", lambda e, s_t=s_t, x_t=x_t, n_t=n_t: e.tensor_scalar(out=n_t[:], in0=x_t[:], scalar1=s_t[:, 12:13],
                                                                            scalar2=s_t[:, 14:15], op0=ALU.subtract, op1=ALU.mult),
                     reads=[x_b, s_b], writes=[n_b])
                for k in range(8):
                    pk, pk_b = K.ps[k]
                    S.op("pe", lambda e, pk=pk, n_t=n_t, k=k, j=j: e.transpose(out=pk[:, j * 128:(j + 1) * 128],
                                                                          in_=n_t[:, k * 128:(k + 1) * 128], identity=K.ident[:]),
                         reads=[n_b, K.ident_b], writes=[pk_b])
                ti += 1
            for k in range(8):
                pk, pk_b = K.ps[k]
                if k % 2 == 0:
                    S.op("dve", lambda e, pk=pk, k=k: e.tensor_scalar(out=hT[:, k, t0:t0 + n], in0=pk[:, 0:n], scalar1=K.modT[:, 8 + k, mn:mn + 1],
                                                                  scalar2=K.modT[:, k, mn:mn + 1], op0=ALU.mult, op1=ALU.add),
                         reads=[pk_b, K.modT_b], writes=[hT_b])
                else:
                    S.op("act", lambda e, pk=pk, k=k: e.activation(out=hT[:, k, t0:t0 + n], in_=pk[:, 0:n], func=AF.Identity,
                                                               scale=K.modT[:, 8 + k, mn:mn + 1], bias=K.modT[:, k, mn:mn + 1]),
                         reads=[pk_b, K.modT_b], writes=[hT_b])
            S.dma("pool", dr["hT_d"][:, :, t0:t0 + n], hT[:, :, t0:t0 + n], reads=[hT_b], writes=[dr["hT_d_b"]])
        S.barrier()


def _fm_chunks():
    ch = []
    for c in range(2):
        ch.append((0 + 128 * c, 128, "qA"))
    for c in range(2):
        ch.append((256 + 128 * c, 128, "plain"))
    ch.append((1024, 32, "g"))
    for base in (1056, 2336, 3872, 5408):
        for c in range(4):
            ch.append((base + 128 * c, 128, "silu"))
    for c in range(4):
        ch.append((1568 + 128 * c, 128, "rope"))
    ch.append((2080, 128, "rope"))
    for c in range(4):
        ch.append((4384 + 128 * c, 128, "rope"))
    for c in range(2):
        ch.append((4896 + 128 * c, 128, "rope"))
    for c in range(4):
        ch.append((2848 + 128 * c, 128, "ropeq"))
    for c in range(2):
        ch.append((3360 + 128 * c, 128, "ropek"))
    return ch


VS_W = 1792


def phase2_proj(K, dr, l, hT, hT_b):
    nc, S = K.nc, K.S
    wv_all = dr["w_in"][l].rearrange("(k p) c -> p k c", p=128)
    with contextlib.ExitStack() as es:
        CT, CT_b = alloc(es, nc, "CT", [128, T], F32)
        ST, ST_b = alloc(es, nc, "ST", [128, T], F32)
        S.dma("sp", CT[:], dr["ropeC"][:, :], writes=[CT_b])
        S.dma("sp", ST[:], dr["ropeS"][:, :], writes=[ST_b])
        bd, bd_b = alloc(es, nc, "bd64", [128, 128], BF16)
        S.dma("sp", bd[:], dr["bd64"][:, :], writes=[bd_b])
        sqb2 = [alloc(es, nc, f"sqb2_{i}", [128, 512], BF16) for i in range(2)]
        gq, gq_b = alloc(es, nc, "gq", [128, 4], F32)
        with nc.allow_non_contiguous_dma(reason="tiny gain vectors"):
            for hh in range(2):
                for ci, nm in enumerate(("glb_q_norm", "glb_k_norm")):
                    src = dr[nm][l]
                    S.dma("sp", gq[hh * 64:(hh + 1) * 64, 2 * ci:2 * ci + 1], src.rearrange("(d o) -> d o", o=1), writes=[gq_b])
                    for a in range(2):
                        for hf in range(2):
                            p0 = hh * 64 + a * 32 + hf * 16
                            s0 = a * 32 + (1 - hf) * 16
                            S.dma("sp", gq[p0:p0 + 16, 2 * ci + 1:2 * ci + 2], src[s0:s0 + 16].rearrange("(d o) -> d o", o=1), writes=[gq_b])
        wt = [alloc(es, nc, f"wch{i}", [128, 8, 128], BF16) for i in range(3)]
        wr = [alloc(es, nc, f"wchR{i}", [128, 8, 128], BF16) for i in range(2)]
        stg = [alloc(es, nc, f"stg{i}", [128, 512], BF16) for i in range(4)]
        f32t = [alloc(es, nc, f"f32t{i}", [128, 512], F32) for i in range(8)]
        gst = [alloc(es, nc, f"gst{i}", [32, 512], F32) for i in range(2)]
        cnt = {"stg": 0, "f": 0, "bank": 0, "w": 0, "wr": 0, "g": 0}

        def nxt(name, pool):
            i = cnt[name]
            cnt[name] = i + 1
            return pool[i % len(pool)]

        def bank():
            return nxt("bank", K.ps)

        PT = dr["PT"]
        PT_b = dr["PT_b"]
        chunks = _fm_chunks()
        wv, wv_b = alloc(es, nc, "wvtok", [128, 8, 1408], BF16)
        wtiles = {}

        def load_w(i):
            if i >= len(chunks):
                return
            c0_, M_, _ = chunks[i]
            w_, wb_ = wt[i % 3]
            S.dma("pool", w_[:, :, 0:M_], wv_all[:, :, c0_:c0_ + M_], writes=[wb_])
            wtiles[i] = (w_, wb_)

        def perm_w(i):
            if i >= len(chunks) or not chunks[i][2].startswith("rope"):
                return
            w_, wb_ = wtiles[i]
            wR_, wRb_ = wr[i % 2]
            w4 = w_[:, :, :].rearrange("p k (g h f) -> p (k g) h f", h=2, f=16)
            r4 = wR_[:, :, :].rearrange("p k (g h f) -> p (k g) h f", h=2, f=16)
            S.op("pool", lambda e: e.tensor_scalar(out=r4[:, :, 0, :], in0=w4[:, :, 1, :], scalar1=-1.0, scalar2=None, op0=ALU.mult),
                 reads=[wb_], writes=[wRb_])
            S.op("pool", lambda e: e.tensor_copy(out=r4[:, :, 1, :], in_=w4[:, :, 0, :]), reads=[wb_], writes=[wRb_])

        load_w(0)
        load_w(1)
        for (d0, s0, m) in ((0, 512, 512), (512, 2208, 128), (640, 3616, 256), (896, 5152, 256), (1152, 256, 256)):
            S.dma("pool", wv[:, :, d0:d0 + m], wv_all[:, :, s0:s0 + m], writes=[wv_b])
        perm_w(0)
        for ci, (c0, M, kind) in enumerate(chunks):
            load_w(ci + 2)
            perm_w(ci + 1)
            w, w_b = wtiles[ci]
            isrope = kind.startswith("rope")
            if isrope:
                wR, wR_b = wr[ci % 2]
            for bi, (t0, n) in enumerate(BLOCKS):
                p1, p1_b = bank()
                for k in range(8):
                    S.op("pe", lambda e, k=k: e.matmul(p1[0:M, 0:n], lhsT=w[:, k, 0:M], rhs=hT[:, k, t0:t0 + n], start=(k == 0), stop=(k == 7)),
                         reads=[w_b, hT_b], writes=[p1_b])
                if isrope:
                    p2, p2_b = bank()
                    for k in range(8):
                        S.op("pe", lambda e, k=k: e.matmul(p2[0:M, 0:n], lhsT=wR[:, k, 0:M], rhs=hT[:, k, t0:t0 + n], start=(k == 0), stop=(k == 7)),
                             reads=[wR_b, hT_b], writes=[p2_b])
                if kind == "g":
                    g_t, g_b = nxt("g", gst)
                    S.op("act", lambda e: e.activation(out=g_t[0:32, 0:n], in_=p1[0:32, 0:n], func=AF.Copy), reads=[p1_b], writes=[g_b])
                    S.dma("sp", dr["GT"][:, t0:t0 + n], g_t[0:32, 0:n], reads=[g_b], writes=[dr["GT_b"]])
                    continue
                o_t, o_b = nxt("stg", stg)
                if kind == "plain":
                    S.op("act", lambda e: e.activation(out=o_t[:, 0:n], in_=p1[:, 0:n], func=AF.Copy), reads=[p1_b], writes=[o_b])
                elif kind == "qA":
                    S.op("act", lambda e: e.activation(out=o_t[:, 0:n], in_=p1[:, 0:n], func=AF.Copy, scale=0.125), reads=[p1_b], writes=[o_b])
                elif kind == "silu":
                    S.op("act", lambda e: e.activation(out=o_t[:, 0:n], in_=p1[:, 0:n], func=AF.Silu), reads=[p1_b], writes=[o_b])
                elif kind == "rope":
                    a_t, a_b = nxt("f", f32t)
                    b_t, b_b = nxt("f", f32t)
                    S.op("dve", lambda e: e.tensor_tensor(out=a_t[:, 0:n], in0=p1[:, 0:n], in1=CT[:, t0:t0 + n], op=ALU.mult),
                         reads=[p1_b, CT_b], writes=[a_b])
                    S.op("dve", lambda e: e.tensor_tensor(out=b_t[:, 0:n], in0=p2[:, 0:n], in1=ST[:, t0:t0 + n], op=ALU.mult),
                         reads=[p2_b, ST_b], writes=[b_b])
                    S.op("pool", lambda e: e.tensor_tensor(out=o_t[:, 0:n], in0=a_t[:, 0:n], in1=b_t[:, 0:n], op=ALU.add),
                         reads=[a_b, b_b], writes=[o_b])
                else:
                    gi = 0 if kind == "ropeq" else 2
                    sq_t, sq_b = sqb2[cnt["bank"] % 2]
                    a_t, a_b = nxt("f", f32t)
                    b_t, b_b = nxt("f", f32t)
                    r_t, r_b = nxt("f", f32t)
                    S.op("act", lambda e: e.activation(out=sq_t[:, 0:n], in_=p1[:, 0:n], func=AF.Square), reads=[p1_b], writes=[sq_b])
                    p3, p3_b = bank()
                    S.op("pe", lambda e: e.matmul(p3[:, 0:n], lhsT=bd[:], rhs=sq_t[:, 0:n], start=True, stop=True),
                         reads=[bd_b, sq_b], writes=[p3_b])
                    S.op("act", lambda e: e.activation(out=r_t[:, 0:n], in_=p3[:, 0:n], func=AF.Sqrt, bias=K.epsc[:, 0:1], scale=1.0),
                         reads=[p3_b, K.epsc_b], writes=[r_b])
                    S.op("dve", lambda e: e.reciprocal(out=r_t[:, 0:n], in_=r_t[:, 0:n]), reads=[r_b], writes=[r_b])
                    S.op("dve", lambda e: e.scalar_tensor_tensor(out=a_t[:, 0:n], in0=p1[:, 0:n], scalar=gq[:, gi:gi + 1], in1=CT[:, t0:t0 + n],
                                                               op0=ALU.mult, op1=ALU.mult), reads=[p1_b, CT_b, gq_b], writes=[a_b])
                    S.op("dve", lambda e: e.scalar_tensor_tensor(out=b_t[:, 0:n], in0=p2[:, 0:n], scalar=gq[:, gi + 1:gi + 2], in1=ST[:, t0:t0 + n],
                                                               op0=ALU.mult, op1=ALU.mult), reads=[p2_b, ST_b, gq_b], writes=[b_b])
                    S.op("pool", lambda e: e.tensor_tensor(out=a_t[:, 0:n], in0=a_t[:, 0:n], in1=b_t[:, 0:n], op=ALU.add),
                         reads=[a_b, b_b], writes=[a_b])
                    S.op("pool", lambda e: e.tensor_tensor(out=o_t[:, 0:n], in0=a_t[:, 0:n], in1=r_t[:, 0:n], op=ALU.mult),
                         reads=[a_b, r_b], writes=[o_b])
                S.dma("sp", PT[c0:c0 + M, t0:t0 + n], o_t[0:M, 0:n], reads=[o_b], writes=[PT_b])
        vst = [alloc(es, nc, f"vst{i}", [128, VS_W], BF16) for i in range(2)]
        for v_t, v_b in vst:
            S.op("pool", lambda e, v_t=v_t: e.memset(v_t[:, 512:1280], 1.0), writes=[v_b])
        for j in range(NT):
            v_t, v_b = vst[j % 2]
            pa, pa_b = bank()
            pb, pb_b = bank()
            pc, pc_b = bank()
            for (pp, pp_b, c0, m) in ((pa, pa_b, 0, 512), (pb, pb_b, 512, 512), (pc, pc_b, 1024, 384)):
                for k in range(8):
                    S.op("pe", lambda e, k=k, pp=pp, c0=c0, m=m: e.matmul(pp[:, 0:m], lhsT=hT[:, k, j * 128:(j + 1) * 128], rhs=wv[:, k, c0:c0 + m],
                                                                        start=(k == 0), stop=(k == 7)), reads=[hT_b, wv_b], writes=[pp_b])
            S.op("act", lambda e: e.activation(out=v_t[:, 0:512], in_=pa[:, 0:512], func=AF.Copy), reads=[pa_b], writes=[v_b])
            S.op("dve", lambda e: e.tensor_copy(out=v_t[:, 512:768].rearrange("p (h c) -> p h c", c=128)[:, :, 0:64],
                                                in_=pb[:, 0:128].rearrange("p (h c) -> p h c", c=64)), reads=[pb_b], writes=[v_b])
            S.op("dve", lambda e: e.tensor_copy(out=v_t[:, 768:1280].rearrange("p (h c) -> p h c", c=128)[:, :, 0:64],
                                                in_=pb[:, 128:384].rearrange("p (h c) -> p h c", c=64)), reads=[pb_b], writes=[v_b])
            S.op("act", lambda e: e.activation(out=v_t[:, 1280:1408], in_=pb[:, 384:512], func=AF.Copy), reads=[pb_b], writes=[v_b])
            S.op("act", lambda e: e.activation(out=v_t[:, 1408:1792], in_=pc[:, 0:384], func=AF.Copy), reads=[pc_b], writes=[v_b])
            S.dma("sp", dr["VS"][:, j, :], v_t[:], reads=[v_b], writes=[dr["VS_b"]])
        S.barrier()


def _qblocks(need_ctx):
    qb = []
    if need_ctx:
        qb.append((0, 256, [0, 1]))
    for i in range(8):
        qb.append((256 + 512 * i, 512, list(range(NT))))
    return qb


def _emit_pairs(steps, emit_qk, emit_pv, skp=2, defer=12):
    pend = []
    deferred = []

    def tick(flush=False):
        for d_ in list(deferred):
            d_[0] -= 1
            if d_[0] <= 0 or flush:
                deferred.remove(d_)
                d_[1]()

    def run_pv(st):
        t_ = emit_pv(st)
        if t_ is not None:
            deferred.append([defer, t_])

    for p in range(0, len(steps), 2):
        pair = steps[p:p + 2]
        if pair[0].get("newg") or pair[0].get("newq"):
            while pend:
                for st in pend.pop(0):
                    run_pv(st)
        for st in pair:
            emit_qk(st)
        pend.append(pair)
        if len(pend) > skp:
            for st in pend.pop(0):
                run_pv(st)
        tick()
    while pend:
        for st in pend.pop(0):
            run_pv(st)
    tick(flush=True)


def phase3_global(K, dr, l, need_ctx):
    nc, S = K.nc, K.S
    PT, VS, BR = dr["PT"], dr["VS"], dr["BR"]
    with contextlib.ExitStack() as es:
        sets = []
        for i in range(2):
            kt = alloc(es, nc, f"c_kt{i}", [128, T], BF16)
            qt = alloc(es, nc, f"c_qt{i}", [128, T], BF16)
            vv = alloc(es, nc, f"c_v{i}", [128, NT, 128], BF16)
            sets.append((kt, qt, vv))
        pts = [alloc(es, nc, f"c_pt{i}", [128, 1024], BF16) for i in range(4)]
        dens = [alloc(es, nc, f"c_den{i}", [64, 512], F32) for i in range(4)]
        outs = [alloc(es, nc, f"c_out{i}", [64, 512], BF16) for i in range(4)]
        ci = {"s": 0, "pt": 0, "o": 0}

        def load(g):
            (kt, kt_b), (qt, qt_b), (vv, vv_b) = sets[g % 2]
            r0 = 3360 + 64 * g
            S.dma("sp", kt[0:64, :], PT[r0:r0 + 64, :], reads=[dr["PT_b"]], writes=[kt_b])
            S.dma("sp", kt[64:128, :], PT[r0:r0 + 64, :], reads=[dr["PT_b"]], writes=[kt_b])
            q0 = 2848 + 128 * g
            S.dma("sp", qt[:, :], PT[q0:q0 + 128, :], reads=[dr["PT_b"]], writes=[qt_b])
            S.dma("sp", vv[:, :, :], VS[:, :, 768 + 128 * g:768 + 128 * (g + 1)], reads=[dr["VS_b"]], writes=[vv_b])

        steps = []
        for g in range(4):
            for bi_, (t0, n, ktiles) in enumerate(_qblocks(need_ctx)):
                for ji, j in enumerate(ktiles):
                    for hh in range(2):
                        steps.append(dict(g=g, hh=hh, t0=t0, n=n, j=j, first=(ji == 0), last=(ji == len(ktiles) - 1),
                                          newg=(hh == 0 and bi_ == 0 and ji == 0), blk=g * 16 + bi_))

        def emit_qk(st):
            g, hh, t0, n, j = st["g"], st["hh"], st["t0"], st["n"], st["j"]
            if st["newg"]:
                if g == 0:
                    load(0)
                if g + 1 < 4:
                    load(g + 1)
            (kt, kt_b), (qt, qt_b), (vv, vv_b) = sets[g % 2]
            hp = slice(hh * 64, hh * 64 + 64)
            bk = ci["s"] % 4
            psx, ps_b = K.ps[bk]
            ci["s"] += 1
            if hh == 0:
                pt, pt_b = pts[ci["pt"] % len(pts)]
                ci["pt"] += 1
                ci["cur"] = (pt, pt_b)
            pt, pt_b = ci["cur"]
            st["pt"] = (pt, pt_b)
            S.op("pe", lambda e: e.matmul(psx[:, 0:n], lhsT=kt[hp, j * 128:(j + 1) * 128], rhs=qt[hp, t0:t0 + n], start=True, stop=True),
                 reads=[kt_b, qt_b], writes=[ps_b])
            if hh == 1:
                src = K.psall[:, (bk - 1) * 512:(bk + 1) * 512].rearrange("p (k c) -> p k c", c=512)[:, :, 0:n]
                S.op("act", lambda e: e.activation(out=pt[:, :].rearrange("p (k c) -> p k c", c=512)[:, :, 0:n], in_=src, func=AF.Exp, scale=0.125),
                     reads=[K.ps[bk - 1][1], ps_b], writes=[pt_b])

        def emit_pv(st):
            g, hh, t0, n, j = st["g"], st["hh"], st["t0"], st["n"], st["j"]
            (kt, kt_b), (qt, qt_b), (vv, vv_b) = sets[g % 2]
            pt, pt_b = st["pt"]
            po, po_b = K.ps[4 + 2 * (st["blk"] % 2) + hh]
            S.op("pe", lambda e: e.matmul(po[:, 0:n], lhsT=vv[:, j, :], rhs=pt[:, hh * 512:hh * 512 + n], start=st["first"], stop=st["last"]),
                 reads=[vv_b, pt_b], writes=[po_b])
            if st["last"]:
                h = 2 * g + hh
                dn, dn_b = dens[ci["o"] % len(dens)]
                ot, ot_b = outs[ci["o"] % len(outs)]
                ci["o"] += 1
                S.op("act", lambda e: e.activation(out=dn[0:64, 0:n], in_=po[64:128, 0:n], func=AF.Copy), reads=[po_b], writes=[dn_b])
                S.op("dve", lambda e: e.reciprocal(out=dn[0:64, 0:n], in_=dn[0:64, 0:n]), reads=[dn_b], writes=[dn_b])
                S.op("dve", lambda e: e.tensor_tensor(out=ot[0:64, 0:n], in0=po[0:64, 0:n], in1=dn[0:64, 0:n], op=ALU.mult),
                     reads=[po_b, dn_b], writes=[ot_b])
                S.dma("pool", BR[2, h * 64:(h + 1) * 64, t0:t0 + n], ot[0:64, 0:n], reads=[ot_b], writes=[dr["BR_b"]])

        _emit_pairs(steps, emit_qk, emit_pv)
        S.barrier()


def phase4_diff(K, dr, l, need_ctx, after_loads=None):
    nc, S = K.nc, K.S
    PT, VS, BR = dr["PT"], dr["VS"], dr["BR"]
    lam_init = 0.8 - 0.6 * math.exp(-0.3 * l)
    with contextlib.ExitStack() as es:
        lp, lp_b = alloc(es, nc, "d_lp", [128, 256], F32)
        sm, sm_b = alloc(es, nc, "d_sm", [128, 8], F32)
        S.dma("sp", lp[:], dr["diff_lambda"][l].rearrange("a d -> (a d)").partition_broadcast(128), writes=[lp_b])
        S.op("dve", lambda e: e.tensor_tensor(out=lp[:, 0:64], in0=lp[:, 0:64], in1=lp[:, 64:128], op=ALU.mult), reads=[lp_b], writes=[lp_b])
        S.op("dve", lambda e: e.tensor_tensor(out=lp[:, 128:192], in0=lp[:, 128:192], in1=lp[:, 192:256], op=ALU.mult), reads=[lp_b], writes=[lp_b])
        S.op("dve", lambda e: e.reduce_sum(out=sm[:, 0:1], in_=lp[:, 0:64], axis=AX.X), reads=[lp_b], writes=[sm_b])
        S.op("dve", lambda e: e.reduce_sum(out=sm[:, 1:2], in_=lp[:, 128:192], axis=AX.X), reads=[lp_b], writes=[sm_b])
        S.op("act", lambda e: e.activation(out=sm[:, 2:4], in_=sm[:, 0:2], func=AF.Exp), reads=[sm_b], writes=[sm_b])
        S.op("dve", lambda e: e.scalar_tensor_tensor(out=sm[:, 4:5], in0=sm[:, 3:4], scalar=-lam_init, in1=sm[:, 2:3], op0=ALU.add, op1=ALU.subtract),
             reads=[sm_b], writes=[sm_b])
        sg, sg_b = alloc(es, nc, "d_sg", [128, 1], F32)
        with nc.allow_non_contiguous_dma(reason="tiny gain vector"):
            S.dma("sp", sg[:], dr["diff_norm"][l].rearrange("(d o) -> d o", o=1), writes=[sg_b])
        S.op("dve", lambda e: e.tensor_scalar(out=sg[:], in0=sg[:], scalar1=1.0 - lam_init, scalar2=None, op0=ALU.mult), reads=[sg_b], writes=[sg_b])
        onesb, onesb_b = alloc(es, nc, "d_onesb", [128, 128], BF16)
        S.op("pool", lambda e: e.memset(onesb[:], 1.0), writes=[onesb_b])
        avg, avg_b = alloc(es, nc, "d_avg", [128, 128], F32)
        S.op("pool", lambda e: e.memset(avg[:], 1.0 / 128.0), writes=[avg_b])
        sets = []
        for i in range(2):
            kt = alloc(es, nc, f"d_kt{i}", [128, T], BF16)
            vv = alloc(es, nc, f"d_v{i}", [128, NT, 128], BF16)
            sets.append((kt, vv))
        qts = [alloc(es, nc, f"d_qt{i}", [128, T], BF16) for i in range(2)]
        pts = [alloc(es, nc, f"d_pt{i}", [128, 1024], BF16) for i in range(4)]
        f32t = [alloc(es, nc, f"d_f{i}", [128, 512], F32) for i in range(8)]
        outs = [alloc(es, nc, f"d_out{i}", [128, 512], BF16) for i in range(2)]
        ci = {"s": 0, "pt": 0, "o": 0, "f": 0}

        def nf():
            i = ci["f"]
            ci["f"] += 1
            return f32t[i % 8]

        def load_kv(g, what=("k", "v")):
            (kt, kt_b), (vv, vv_b) = sets[g % 2]
            r0 = 4896 + 128 * g
            if "k" in what:
                S.dma("sp", kt[:, :], PT[r0:r0 + 128, :], reads=[dr["PT_b"]], writes=[kt_b])
            if "v" in what:
                S.dma("sp", vv[:, :, :], VS[:, :, 1280 + 128 * g:1280 + 128 * (g + 1)], reads=[dr["VS_b"]], writes=[vv_b])

        def load_q(hq):
            qt, qt_b = qts[hq % 2]
            q0 = 4384 + 128 * hq
            S.dma("sp", qt[:, :], PT[q0:q0 + 128, :], reads=[dr["PT_b"]], writes=[qt_b])

        SK = 3
        steps = []
        for hq in range(4):
            for bi_, (t0, n, ktiles) in enumerate(_qblocks(need_ctx)):
                for ji, j in enumerate(ktiles):
                    for m in range(2):
                        steps.append(dict(hq=hq, t0=t0, n=n, j=j, m=m, first=(ji == 0), last=(ji == len(ktiles) - 1),
                                          newq=(bi_ == 0 and ji == 0 and m == 0)))
        acc = [(K.ps[4], K.ps[6]), (K.ps[5], K.ps[7])]

        def emit_qk(st):
            hq, t0, n, j, m = st["hq"], st["t0"], st["n"], st["j"], st["m"]
            if st["newq"]:
                if hq == 0:
                    load_kv(0, ("k",))
                    load_q(0)
                    load_kv(0, ("v",))
                    load_kv(1)
                    if after_loads is not None:
                        after_loads()
                if hq + 1 < 4:
                    load_q(hq + 1)
            (kt, kt_b), (vv, vv_b) = sets[(hq // 2) % 2]
            qt, qt_b = qts[hq % 2]
            mp = slice(m * 64, m * 64 + 64)
            bk = ci["s"] % 4
            psx, ps_b = K.ps[bk]
            ci["s"] += 1
            if m == 0:
                pt, pt_b = pts[ci["pt"] % len(pts)]
                ci["pt"] += 1
                ci["cur"] = (pt, pt_b)
            pt, pt_b = ci["cur"]
            st["pt"] = (pt, pt_b)
            S.op("pe", lambda e: e.matmul(psx[:, 0:n], lhsT=kt[mp, j * 128:(j + 1) * 128], rhs=qt[mp, t0:t0 + n], start=True, stop=True),
                 reads=[kt_b, qt_b], writes=[ps_b])
            if m == 1:
                src = K.psall[:, (bk - 1) * 512:(bk + 1) * 512].rearrange("p (k c) -> p k c", c=512)[:, :, 0:n]
                S.op("act", lambda e: e.activation(out=pt[:, :].rearrange("p (k c) -> p k c", c=512)[:, :, 0:n], in_=src, func=AF.Exp, scale=0.125),
                     reads=[K.ps[bk - 1][1], ps_b], writes=[pt_b])

        def emit_pv(st):
            hq, t0, n, j, m = st["hq"], st["t0"], st["n"], st["j"], st["m"]
            (kt, kt_b), (vv, vv_b) = sets[(hq // 2) % 2]
            pt, pt_b = st["pt"]
            (po, po_b), (pd, pd_b) = acc[m]
            st_, sp_ = st["first"], st["last"]
            S.op("pe", lambda e: e.matmul(po[:, 0:n], lhsT=vv[:, j, :], rhs=pt[:, m * 512:m * 512 + n], start=st_, stop=sp_), reads=[vv_b, pt_b], writes=[po_b])
            S.op("pe", lambda e: e.matmul(pd[:, 0:n], lhsT=onesb[:, :], rhs=pt[:, m * 512:m * 512 + n], start=st_, stop=sp_), reads=[onesb_b, pt_b], writes=[pd_b])
            if not (st["last"] and m == 1):
                return
            (po0, po0_b), (pd0, pd0_b) = acc[0]
            (po1, po1_b), (pd1, pd1_b) = acc[1]
            r0_, r0b = nf()
            r1_, r1b = nf()
            a_, ab = nf()
            b_, bb = nf()
            S.op("act", lambda e: e.activation(out=r0_[:, 0:n], in_=pd0[:, 0:n], func=AF.Copy), reads=[pd0_b], writes=[r0b])
            S.op("dve", lambda e: e.tensor_copy(out=a_[:, 0:n], in_=po0[:, 0:n]), reads=[po0_b], writes=[ab])
            S.op("act", lambda e: e.activation(out=r1_[:, 0:n], in_=pd1[:, 0:n], func=AF.Copy), reads=[pd1_b], writes=[r1b])
            S.op("dve", lambda e: e.tensor_copy(out=b_[:, 0:n], in_=po1[:, 0:n]), reads=[po1_b], writes=[bb])
            S.op("dve", lambda e: e.reciprocal(out=r0_[:, 0:n], in_=r0_[:, 0:n]), reads=[r0b], writes=[r0b])
            S.op("dve", lambda e: e.reciprocal(out=r1_[:, 0:n], in_=r1_[:, 0:n]), reads=[r1b], writes=[r1b])
            S.op("pool", lambda e: e.tensor_tensor(out=a_[:, 0:n], in0=a_[:, 0:n], in1=r0_[:, 0:n], op=ALU.mult), reads=[ab, r0b], writes=[ab])
            S.op("pool", lambda e: e.tensor_tensor(out=b_[:, 0:n], in0=b_[:, 0:n], in1=r1_[:, 0:n], op=ALU.mult), reads=[bb, r1b], writes=[bb])
            S.op("dve", lambda e: e.scalar_tensor_tensor(out=a_[:, 0:n], in0=b_[:, 0:n], scalar=sm[:, 4:5], in1=a_[:, 0:n], op0=ALU.mult, op1=ALU.add),
                 reads=[ab, bb, sm_b], writes=[ab])
            S.op("pool", lambda e: e.tensor_tensor(out=b_[:, 0:n], in0=a_[:, 0:n], in1=a_[:, 0:n], op=ALU.mult), reads=[ab], writes=[bb])
            return lambda: fin_tail(hq, t0, n, a_, ab, b_, bb, r0_, r0b, r1_, r1b)

        def fin_tail(hq, t0, n, a_, ab, b_, bb, r0_, r0b, r1_, r1b):
            psm, psm_b = K.ps[ci["s"] % 4]
            ci["s"] += 2
            S.op("pe", lambda e: e.matmul(psm[:, 0:n], lhsT=avg[:, :], rhs=b_[:, 0:n], start=True, stop=True), reads=[avg_b, bb], writes=[psm_b])
            S.op("act", lambda e: e.activation(out=r1_[:, 0:n], in_=psm[:, 0:n], func=AF.Ln, bias=K.epsc[:, 0:1], scale=1.0),
                 reads=[psm_b, K.epsc_b], writes=[r1b])
            S.op("act", lambda e: e.activation(out=r0_[:, 0:n], in_=r1_[:, 0:n], func=AF.Exp, scale=-0.5), reads=[r1b], writes=[r0b])
            ot, ot_b = outs[ci["o"] % 2]
            ci["o"] += 1
            S.op("dve", lambda e: e.scalar_tensor_tensor(out=ot[:, 0:n], in0=a_[:, 0:n], scalar=sg[:, 0:1], in1=r0_[:, 0:n], op0=ALU.mult, op1=ALU.mult),
                 reads=[ab, r0b, sg_b], writes=[ot_b])
            S.dma("pool", BR[3, hq * 128:(hq + 1) * 128, t0:t0 + n], ot[:, 0:n], reads=[ot_b], writes=[dr["BR_b"]])

        _emit_pairs(steps, emit_qk, emit_pv)
        S.barrier()


def band_mask():
    kl = np.arange(128)[:, None]
    ql = np.arange(128)[None, :]
    m0 = (kl >= ql).astype(np.float32)
    m1 = (kl <= ql).astype(np.float32)
    return np.stack([np.tile(m0, (1, 4)), np.tile(m1, (1, 4))], 0)


def phase5_window(K, dr, l, need_ctx):
    nc, S = K.nc, K.S
    PT, VS, BR = dr["PT"], dr["VS"], dr["BR"]
    with contextlib.ExitStack() as es:
        kt, kt_b = alloc(es, nc, "w_kt", [128, T], BF16)
        q4, q4_b = alloc(es, nc, "w_q4", [128, 4, T], BF16)
        vv, vv_b = alloc(es, nc, "w_v", [128, NT, 256], BF16)
        mk, mk_b = alloc(es, nc, "w_mask", [128, 2, 512], BF16)
        S.dma("sp", kt[:, :], PT[2080:2208, :], reads=[dr["PT_b"]], writes=[kt_b])
        for g in range(2):
            S.dma("sp", q4[g * 64:(g + 1) * 64, :, :], PT[1568 + 256 * g:1568 + 256 * (g + 1), :].rearrange("(i d) t -> d i t", d=64),
                  reads=[dr["PT_b"]], writes=[q4_b])
        S.dma("sp", vv[:, :, :], VS[:, :, 512:768], reads=[dr["VS_b"]], writes=[vv_b])
        for m in range(2):
            S.dma("sp", mk[:, m, :], dr["bandmask"][m], writes=[mk_b])
        sk, sk_b = alloc(es, nc, "w_sink", [1, 16], F32)
        S.dma("sp", sk[0:1, 0:8], dr["win_sink"][l].rearrange("(o h) -> o h", o=1), writes=[sk_b])
        S.op("act", lambda e: e.activation(out=sk[0:1, 8:16], in_=sk[0:1, 0:8], func=AF.Exp), reads=[sk_b], writes=[sk_b])
        srow, srow_b = alloc(es, nc, "w_srow", [1, 2, 512], F32)
        for h in range(8):
            S.op("dve", lambda e, h=h: e.tensor_scalar(out=srow[0:1, h // 4, (h % 4) * 128:(h % 4 + 1) * 128], in0=K.ones[0:1, 0:128],
                                                      scalar1=sk[0:1, 8 + h:9 + h], scalar2=None, op0=ALU.mult),
                 reads=[sk_b, K.ones_b], writes=[srow_b])
        sel, sel_b = alloc(es, nc, "w_sel", [1, 128], F32)
        S.op("pool", lambda e: e.memset(sel[0:1, 0:64], 0.0), writes=[sel_b])
        S.op("pool", lambda e: e.memset(sel[0:1, 64:128], 1.0), writes=[sel_b])
        pts = [alloc(es, nc, f"w_pt{i}", [128, 512], BF16) for i in range(6)]
        dens = [alloc(es, nc, f"w_den{i}", [64, 512], F32) for i in range(2)]
        outs = [alloc(es, nc, f"w_out{i}", [64, 512], BF16) for i in range(2)]
        ci = {"s": 0, "pt": 0, "o": 0}
        qblocks = []
        if need_ctx:
            qblocks += [(0, [(0, None), (1, None)]), (128, [(0, None), (1, None)])]
        for nb in range(32):
            kl = [(0, None), (1, None)]
            if nb > 0:
                kl.append((2 + nb - 1, 0))
            kl.append((2 + nb, None))
            if nb < 31:
                kl.append((2 + nb + 1, 1))
            qblocks.append((256 + 128 * nb, kl))
        steps = []
        bno = 0
        for g in range(2):
            for (t0, klist) in qblocks:
                for ji, (j, msk) in enumerate(klist):
                    steps.append(dict(g=g, t0=t0, j=j, msk=msk, first=(ji == 0), last=(ji == len(klist) - 1), blk=bno))
                bno += 1

        def emit_qk(st):
            g, t0, j, msk = st["g"], st["t0"], st["j"], st["msk"]
            gp = slice(g * 64, g * 64 + 64)
            psx, ps_b = K.ps[ci["s"] % 6]
            ci["s"] += 1
            pt, pt_b = pts[ci["pt"] % len(pts)]
            ci["pt"] += 1
            st["pt"] = (pt, pt_b)
            S.op("pe", lambda e: e.matmul(psx[:, :].rearrange("p (i q) -> p i q", i=4), lhsT=kt[gp, j * 128:(j + 1) * 128],
                                          rhs=q4[gp, :, t0:t0 + 128], start=True, stop=True), reads=[kt_b, q4_b], writes=[ps_b])
            S.op("act", lambda e: e.activation(out=pt[:, :], in_=psx[:, :], func=AF.Exp, scale=0.125), reads=[ps_b], writes=[pt_b])
            if msk is not None:
                S.op("pool", lambda e: e.tensor_tensor(out=pt[:, :], in0=pt[:, :], in1=mk[:, msk, :], op=ALU.mult),
                     reads=[pt_b, mk_b], writes=[pt_b])

        def emit_pv(st):
            g, t0, j = st["g"], st["t0"], st["j"]
            pt, pt_b = st["pt"]
            po, po_b = K.ps[6 + st["blk"] % 2]
            S.op("pe", lambda e: e.matmul(po[:, :], lhsT=vv[:, j, g * 128:(g + 1) * 128], rhs=pt[:, :], start=st["first"], stop=False),
                 reads=[vv_b, pt_b], writes=[po_b])
            if not st["last"]:
                return
            S.op("pe", lambda e: e.matmul(po[:, :], lhsT=sel[0:1, :], rhs=srow[0:1, g, :], start=False, stop=True),
                 reads=[sel_b, srow_b], writes=[po_b])
            dn, dn_b = dens[st["blk"] % 2]
            ot, ot_b = outs[st["blk"] % 2]
            S.op("act", lambda e: e.activation(out=dn[0:64, :], in_=po[64:128, :], func=AF.Copy), reads=[po_b], writes=[dn_b])
            S.op("dve", lambda e: e.reciprocal(out=dn[0:64, :], in_=dn[0:64, :]), reads=[dn_b], writes=[dn_b])
            S.op("dve", lambda e: e.tensor_tensor(out=ot[0:64, :], in0=po[0:64, :], in1=dn[0:64, :], op=ALU.mult), reads=[po_b, dn_b], writes=[ot_b])
            with nc.allow_non_contiguous_dma(reason="head-interleaved branch store"):
                S.dma("pool", BR[1, 256 * g:256 * (g + 1), t0:t0 + 128].rearrange("(i d) q -> d i q", d=64),
                      ot[0:64, :].rearrange("p (i q) -> p i q", i=4), reads=[ot_b], writes=[dr["BR_b"]])

        pend = []
        for st in steps:
            emit_qk(st)
            pend.append(st)
            if len(pend) > 3:
                emit_pv(pend.pop(0))
        while pend:
            emit_pv(pend.pop(0))
        S.barrier()


def phase7_weights(K, dr, l, es):
    nc, S = K.nc, K.S
    wm, wm_b = alloc(es, nc, "m_wm", [128, 4, 8, 1024], BF16)
    wu, wu_b = alloc(es, nc, "m_wu", [128, 4, 4, 1024], BF16)
    wo, wo_b = alloc(es, nc, "m_wo", [128, 8, 1024], BF16)
    def issue():
        for i in range(4):
            for kk in range(0, 8, 4):
                S.dma("pool", wm[:, i, kk:kk + 4, :], dr["w_merge"][l, i].rearrange("(k p) c -> p k c", p=128)[:, kk:kk + 4, :], writes=[wm_b])
            S.dma("pool", wu[:, i, :, :], dr["w_up"][l, i].rearrange("(k p) c -> p k c", p=128), writes=[wu_b])
        for kk in range(0, 8, 4):
            S.dma("pool", wo[:, kk:kk + 4, :], dr["w_out"][l].rearrange("(k p) c -> p k c", p=128)[:, kk:kk + 4, :], writes=[wo_b])

    return ((wm, wm_b), (wu, wu_b), (wo, wo_b)), issue


def phase7_merge(K, dr, l, need_ctx, x_src, c_src, src_b, x_dst, c_dst, x_dst_b, c_dst_b, W7):
    nc, S = K.nc, K.S
    alpha = (2 * DEPTH) ** 0.25
    with contextlib.ExitStack() as es:
        (wm, wm_b), (wu, wu_b), (wo, wo_b) = W7
        lng, lng_b = alloc(es, nc, "m_lng", [128, 1024], F32)
        lnb, lnb_b = alloc(es, nc, "m_lnb", [128, 1024], F32)
        S.dma("sp", lng[:], dr["ln_g"][l].partition_broadcast(128), writes=[lng_b])
        S.dma("sp", lnb[:], dr["ln_b"][l].partition_broadcast(128), writes=[lnb_b])
        hbs = [alloc(es, nc, f"m_h{i}", [128, 8, 512], BF16) for i in range(2)]
        brs = [alloc(es, nc, f"m_br{i}", [128, 16, 512], BF16) for i in range(1)]
        zs = [alloc(es, nc, f"m_z{i}", [128, 4, 512], BF16) for i in range(2)]
        sigs = [alloc(es, nc, f"m_sig{i}", [128, 512], F32) for i in range(2)]
        accs = [alloc(es, nc, f"m_acc{i}", [128, 512], F32) for i in range(2)]
        tmps = [alloc(es, nc, f"m_tmp{i}", [128, 512], F32) for i in range(2)]
        mTs = [alloc(es, nc, f"m_mT{i}", [128, 8, 512], BF16) for i in range(2)]
        xts = [alloc(es, nc, f"m_x{i}", [128, 1024], F32) for i in range(1)]
        rts = [alloc(es, nc, f"m_r{i}", [128, 1024], F32) for i in range(1)]
        sts = [alloc(es, nc, f"m_st{i}", [128, 16], F32) for i in range(2)]
        zrows = (1056, 2336, 3872, 5408)
        ci = {"b": 0, "sig": 0, "tmp": 0, "tile": 0}
        blocks = BLOCKS if need_ctx else BLOCKS[1:]
        nb = len(blocks)
        br, _ = brs[0]
        br_bs = [Buf(f"br{i}") for i in range(4)]

        def load_hb(bi):
            if bi >= nb:
                return
            t0, n = blocks[bi]
            hb, hb_b = hbs[bi % 2]
            S.dma("sp", hb[:, :, 0:n], dr["hT_d"][:, :, t0:t0 + n], reads=[dr["hT_d_b"]], writes=[hb_b])

        def issue_loads(bi):
            t0, n = blocks[bi]
            for i in range(4):
                zz, zz_b = zs[i % 2]
                br_b = br_bs[i]
                S.dma("sp", br[:, 4 * i:4 * i + 4, 0:n], dr["BR"][i, :, t0:t0 + n].rearrange("(k p) t -> p k t", p=128), reads=[dr["BR_b"]], writes=[br_b])
                S.dma("sp", zz[:, :, 0:n], dr["PT"][zrows[i]:zrows[i] + 512, t0:t0 + n].rearrange("(k p) t -> p k t", p=128),
                      reads=[dr["PT_b"]], writes=[zz_b])
                S.op("pool", lambda e, i=i, zz=zz: e.tensor_tensor(out=br[:, 4 * i:4 * i + 4, 0:n], in0=br[:, 4 * i:4 * i + 4, 0:n], in1=zz[:, :, 0:n], op=ALU.mult),
                     reads=[br_b, zz_b], writes=[br_b])

        def gu_chunk(bi, c):
            t0, n = blocks[bi]
            hb, hb_b = hbs[bi % 2]
            mT, mT_b = mTs[bi % 2]
            ac, ac_b = accs[c % 2]
            for i in range(4):
                pg, pg_b = K.ps[ci["b"] % 6]
                ci["b"] += 1
                pu, pu_b = K.ps[ci["b"] % 6]
                ci["b"] += 1
                for k in range(8):
                    S.op("pe", lambda e, k=k: e.matmul(pg[:, 0:n], lhsT=wm[:, i, k, c * 128:(c + 1) * 128], rhs=hb[:, k, 0:n], start=(k == 0), stop=(k == 7)),
                         reads=[wm_b, hb_b], writes=[pg_b])
                for k in range(4):
                    S.op("pe", lambda e, k=k: e.matmul(pu[:, 0:n], lhsT=wu[:, i, k, c * 128:(c + 1) * 128], rhs=br[:, 4 * i + k, 0:n], start=(k == 0), stop=(k == 3)),
                         reads=[wu_b, br_bs[i]], writes=[pu_b])
                sg, sg_b = sigs[ci["sig"] % 2]
                ci["sig"] += 1
                S.op("act", lambda e: e.activation(out=sg[:, 0:n], in_=pg[:, 0:n], func=AF.Sigmoid), reads=[pg_b], writes=[sg_b])
                if i == 0:
                    S.op("dve", lambda e: e.tensor_tensor(out=ac[:, 0:n], in0=pu[:, 0:n], in1=sg[:, 0:n], op=ALU.mult), reads=[pu_b, sg_b], writes=[ac_b])
                else:
                    tm, tm_b = tmps[ci["tmp"] % 2]
                    ci["tmp"] += 1
                    S.op("dve", lambda e: e.tensor_tensor(out=tm[:, 0:n], in0=pu[:, 0:n], in1=sg[:, 0:n], op=ALU.mult), reads=[pu_b, sg_b], writes=[tm_b])
                    if i < 3:
                        S.op("pool", lambda e: e.tensor_tensor(out=ac[:, 0:n], in0=ac[:, 0:n], in1=tm[:, 0:n], op=ALU.add), reads=[ac_b, tm_b], writes=[ac_b])
                    else:
                        S.op("pool", lambda e: e.tensor_tensor(out=mT[:, c, 0:n], in0=ac[:, 0:n], in1=tm[:, 0:n], op=ALU.add), reads=[ac_b, tm_b], writes=[mT_b])

        def out_tile(bi, j):
            t0, n = blocks[bi]
            isctx = (t0 == 0)
            gidx = 1 if isctx else 0
            mT, mT_b = mTs[bi % 2]
            tok = t0 + j * 128
            xt, xt_b = xts[0]
            rt, rt_b = rts[0]
            st, st_b = sts[ci["tile"] % 2]
            ci["tile"] += 1
            src = c_src[tok:tok + 128, :] if isctx else x_src[tok - NCTX:tok - NCTX + 128, :]
            S.dma("sp", xt[:], src, reads=[src_b], writes=[xt_b])
            for hf in range(2):
                pq, pq_b = K.ps[6 + hf]
                for k in range(8):
                    S.op("pe", lambda e, k=k: e.matmul(pq[:, :], lhsT=mT[:, k, j * 128:(j + 1) * 128], rhs=wo[:, k, hf * 512:(hf + 1) * 512], start=(k == 0), stop=(k == 7)),
                         reads=[mT_b, wo_b], writes=[pq_b])
                S.op("dve", lambda e: e.tensor_tensor(out=rt[:, hf * 512:(hf + 1) * 512], in0=pq[:, :], in1=K.gate[gidx][:, hf * 512:(hf + 1) * 512], op=ALU.mult),
                     reads=[pq_b, K.gate_b[gidx]], writes=[rt_b])
            S.op("dve", lambda e: e.scalar_tensor_tensor(out=rt[:], in0=xt[:], scalar=alpha, in1=rt[:], op0=ALU.mult, op1=ALU.add), reads=[xt_b, rt_b], writes=[rt_b])
            S.op("dve", lambda e: e.bn_stats(st[:, 0:6], rt[:, 0:512]), reads=[rt_b], writes=[st_b])
            S.op("dve", lambda e: e.bn_stats(st[:, 6:12], rt[:, 512:1024]), reads=[rt_b], writes=[st_b])
            S.op("dve", lambda e: e.bn_aggr(st[:, 12:14], st[:, 0:12]), reads=[st_b], writes=[st_b])
            S.op("act", lambda e: e.activation(out=st[:, 15:16], in_=st[:, 13:14], func=AF.Sqrt, bias=K.epsc[:, 0:1], scale=1.0), reads=[st_b, K.epsc_b], writes=[st_b])
            S.op("dve", lambda e: e.reciprocal(out=st[:, 14:15], in_=st[:, 15:16]), reads=[st_b], writes=[st_b])
            S.op("dve", lambda e: e.tensor_scalar(out=rt[:], in0=rt[:], scalar1=st[:, 12:13], scalar2=st[:, 14:15], op0=ALU.subtract, op1=ALU.mult), reads=[rt_b, st_b], writes=[rt_b])
            S.op("pool", lambda e: e.tensor_tensor(out=rt[:], in0=rt[:], in1=lng[:], op=ALU.mult), reads=[rt_b, lng_b], writes=[rt_b])
            S.op("pool", lambda e: e.tensor_tensor(out=xt[:], in0=rt[:], in1=lnb[:], op=ALU.add), reads=[rt_b, lnb_b], writes=[xt_b])
            if isctx:
                S.dma("sp", c_dst[tok:tok + 128, :], xt[:], reads=[xt_b], writes=[c_dst_b])
            else:
                S.dma("sp", x_dst[tok - NCTX:tok - NCTX + 128, :], xt[:], reads=[xt_b], writes=[x_dst_b])

        load_hb(0)
        issue_loads(0)
        load_hb(1)
        for c in range(8):
            gu_chunk(0, c)
        for bi in range(1, nb + 1):
            prev_tiles = list(range(blocks[bi - 1][1] // 128))
            if bi < nb:
                issue_loads(bi)
                load_hb(bi + 1)
                out_tile(bi - 1, prev_tiles.pop(0))
                for c in range(8):
                    gu_chunk(bi, c)
                    if c % 2 == 1 and prev_tiles:
                        out_tile(bi - 1, prev_tiles.pop(0))
            while prev_tiles:
                out_tile(bi - 1, prev_tiles.pop(0))
        S.barrier()


def tri_consts():
    s = np.arange(128)[:, None]
    c = np.arange(128)[None, :]
    return np.stack([(s <= c), (s >= c), (s > c), (s < c)], 0).astype(np.float32)


def gla_mask():
    t = tri_consts()
    return np.concatenate([t[0], t[0], t[1], t[1]], 1)


def phase6_gla(K, dr, l, need_ctx):
    nc, S = K.nc, K.S
    PT, VS, BR = dr["PT"], dr["VS"], dr["BR"]
    NI = -1.0 / 16.0
    with contextlib.ExitStack() as es:
        tri, tri_b = alloc(es, nc, "g_tri", [128, 4, 128], F32)
        S.dma("sp", tri[:], dr["tri"].rearrange("a s c -> s a c"), writes=[tri_b])
        mk, mk_b = alloc(es, nc, "g_mask", [128, 512], BF16)
        S.dma("sp", mk[:, :], dr["glamask"][:, :], writes=[mk_b])
        gts, gts_b = alloc(es, nc, "g_gt", [33, T], F32)
        S.dma("sp", gts[0:32, :], dr["GT"][:, :], reads=[dr["GT_b"]], writes=[gts_b])
        S.op("pool", lambda e: e.memset(gts[32:33, :], 1.0), writes=[gts_b])
        wg, wg_b = alloc(es, nc, "g_wg", [33, 512], F32)
        S.op("pool", lambda e: e.memset(wg[:, :], 0.0), writes=[wg_b])
        S.dma("sp", wg[0:16, 0:256], dr["gla_w_gate"][l, 0], writes=[wg_b])
        S.dma("sp", wg[16:32, 256:512], dr["gla_w_gate"][l, 1], writes=[wg_b])
        S.dma("sp", wg[32:33, :], dr["gla_b_gate"][l].rearrange("(o a) c -> o (a c)", o=1), writes=[wg_b])
        qT, qT_b = alloc(es, nc, "g_qT", [128, 2, T], BF16)
        kT, kT_b = alloc(es, nc, "g_kT", [128, 2, T], BF16)
        S.dma("sp", qT[:], PT[0:256, :].rearrange("(g p) t -> p g t", p=128), reads=[dr["PT_b"]], writes=[qT_b])
        S.dma("sp", kT[:], PT[256:512, :].rearrange("(g p) t -> p g t", p=128), reads=[dr["PT_b"]], writes=[kT_b])
        vall, vall_b = alloc(es, nc, "g_v", [128, NT, 512], BF16)
        ktok, ktok_b = alloc(es, nc, "g_ktok", [128, NT, 256], BF16)
        S.dma("sp", vall[:], VS[:, :, 0:512], reads=[dr["VS_b"]], writes=[vall_b])
        S.dma("sp", ktok[:], VS[:, :, 1536:1792], reads=[dr["VS_b"]], writes=[ktok_b])
        SBst, SBst_b = alloc(es, nc, "g_SBst", [128, 2, NT, 128], BF16)
        gnT, gnT_b = alloc(es, nc, "g_gnT", [128, 4], F32)
        with nc.allow_non_contiguous_dma(reason="tiny gain vector"):
            S.dma("sp", gnT[:], dr["gla_norm"][l].rearrange("(h v) -> v h", v=128), writes=[gnT_b])
        gnb, gnb_b = alloc(es, nc, "g_gnb", [128, 512], F32)
        for h in range(4):
            S.op("dve", lambda e, h=h: e.tensor_scalar(out=gnb[:, h * 128:(h + 1) * 128], in0=K.ones[:], scalar1=gnT[:, h:h + 1], scalar2=None, op0=ALU.mult),
                 reads=[gnT_b, K.ones_b], writes=[gnb_b])
        avg, avg_b = alloc(es, nc, "g_avg", [128, 128], BF16)
        S.op("pool", lambda e: e.memset(avg[:], 1.0 / 128.0), writes=[avg_b])
        sqbs = [alloc(es, nc, f"g_sqb{i}", [128, 512], BF16) for i in range(2)]
        SF, SF_b = alloc(es, nc, "g_SF", [128, 2, 128], F32)
        SFb, SFb_b = alloc(es, nc, "g_SFb", [128, 2, 128], BF16)
        SBf, SBf_b = alloc(es, nc, "g_SBf", [128, 2, 128], F32)
        for t_, b_ in ((SF, SF_b), (SFb, SFb_b), (SBf, SBf_b)):
            S.op("pool", lambda e, t_=t_: e.memset(t_[:], 0.0), writes=[b_])
        Lt = [alloc(es, nc, f"g_L{i}", [128, 512], F32) for i in range(2)]
        tmps = []
        for i in range(2):
            tmps.append(dict(
                et=alloc(es, nc, f"g_e{i}", [128, 512], F32), Ef=alloc(es, nc, f"g_Ef{i}", [128, 256], F32),
                kh=alloc(es, nc, f"g_kh{i}", [128, 256], BF16), dec=alloc(es, nc, f"g_dec{i}", [128, 2], F32),
                EQ=alloc(es, nc, f"g_EQ{i}", [128, 512], F32), EK=alloc(es, nc, f"g_EK{i}", [128, 512], F32),
                qe=alloc(es, nc, f"g_qe{i}", [128, 2, 2, 128], BF16), ke=alloc(es, nc, f"g_ke{i}", [128, 2, 2, 128], BF16),
                Am=alloc(es, nc, f"g_Am{i}", [128, 2, 512], BF16), sq=alloc(es, nc, f"g_sq{i}", [128, 512], F32),
                rs=alloc(es, nc, f"g_rs{i}", [128, 512], F32), tt=alloc(es, nc, f"g_tt{i}", [128, 512], F32)))
        tsel = {"i": 0}
        ys = [alloc(es, nc, f"g_y{i}", [128, 512], BF16) for i in range(2)]
        cnt = {"L": 0, "y": 0}
        (pZ, pZ_b), (pM, pM_b), (pC, pC_b), (pA0, pA0_b), (pA1, pA1_b), (pkv, pkv_b), (pO, pO_b), (pms, pms_b) = K.ps

        def gate_L(j, lo, hi):
            L, L_b = Lt[cnt["L"] % 2]
            cnt["L"] += 1
            et, et_b = tmps[tsel["i"] % 2]["et"]
            S.op("pe", lambda e: e.matmul(pZ[:, lo:hi], lhsT=gts[0:33, j * 128:(j + 1) * 128], rhs=wg[0:33, lo:hi], start=True, stop=True),
                 reads=[gts_b, wg_b], writes=[pZ_b])
            S.op("act", lambda e: e.activation(out=et[:, lo:hi], in_=pZ[:, lo:hi], func=AF.Exp, scale=-1.0), reads=[pZ_b], writes=[et_b])
            S.op("act", lambda e: e.activation(out=L[:, lo:hi], in_=et[:, lo:hi], func=AF.Ln, bias=K.ones[:, 0:1], scale=1.0), reads=[et_b, K.ones_b], writes=[L_b])
            return L, L_b

        def state_update(St, St_b, dc, dc_b):
            for grp in range(2):
                for hl in range(2):
                    hp = slice(hl * 64, hl * 64 + 64)
                    S.op("dve", lambda e, grp=grp, hl=hl, hp=hp: e.scalar_tensor_tensor(
                        out=St[hp, grp, :], in0=St[hp, grp, :], scalar=dc[hp, grp:grp + 1], in1=pkv[hp, grp * 256 + hl * 128:grp * 256 + hl * 128 + 128],
                        op0=ALU.mult, op1=ALU.add), reads=[St_b, dc_b, pkv_b], writes=[St_b])

        def _pipeline(gens):
            prev = None
            for g_ in gens:
                next(g_)
                if prev is not None:
                    for _ in prev:
                        pass
                prev = g_
            if prev is not None:
                for _ in prev:
                    pass

        order_b = [1, 0] + list(range(NT - 1, 1, -1))

        def tileB(j, seq):
            if j == 2:
                yield
                S.op("act", lambda e: e.activation(out=SBst[:, :, j, :], in_=SBf[:, :, :], func=AF.Copy), reads=[SBf_b], writes=[SBst_b])
                return
            tsel["i"] = seq
            tm_ = tmps[seq % 2]
            (Ef, Ef_b), (kh, kh_b), (dec, dec_b), (EQ, EQ_b), (EK, EK_b) = tm_["Ef"], tm_["kh"], tm_["dec"], tm_["EQ"], tm_["EK"]
            (qe, qe_b), (ke, ke_b), (Am, Am_b), (sq, sq_b), (rs, rs_b), (tt, tt_b) = tm_["qe"], tm_["ke"], tm_["Am"], tm_["sq"], tm_["rs"], tm_["tt"]
            L, L_b = gate_L(j, 256, 512)
            S.op("pe", lambda e: e.matmul(pM[:, 0:256], lhsT=tri[:, 3, :], rhs=L[:, 256:512], start=True, stop=True), reads=[tri_b, L_b], writes=[pM_b])
            S.op("act", lambda e: e.activation(out=Ef[:, :], in_=pM[:, 0:256], func=AF.Exp, scale=NI), reads=[pM_b], writes=[Ef_b])
            S.op("dve", lambda e: e.tensor_tensor(out=kh[:, :], in0=ktok[:, j, :], in1=Ef[:, :], op=ALU.mult), reads=[ktok_b, Ef_b], writes=[kh_b])
            for grp in range(2):
                S.op("pe", lambda e, grp=grp: e.matmul(pC[:, grp:grp + 1], lhsT=L[:, 256 + grp * 128:256 + (grp + 1) * 128], rhs=K.ones[:, 0:1], start=True, stop=True),
                     reads=[L_b, K.ones_b], writes=[pC_b])
            S.op("act", lambda e: e.activation(out=dec[:, 0:2], in_=pC[:, 0:2], func=AF.Exp, scale=NI), reads=[pC_b], writes=[dec_b])
            yield
            S.op("act", lambda e: e.activation(out=SBst[:, :, j, :], in_=SBf[:, :, :], func=AF.Copy), reads=[SBf_b], writes=[SBst_b])
            for grp in range(2):
                S.op("pe", lambda e, grp=grp: e.matmul(pkv[:, grp * 256:(grp + 1) * 256], lhsT=kh[:, grp * 128:(grp + 1) * 128], rhs=vall[:, j, grp * 256:(grp + 1) * 256],
                                                      start=True, stop=True), reads=[kh_b, vall_b], writes=[pkv_b])
            state_update(SBf, SBf_b, dec, dec_b)

        _pipeline([tileB(j, q_) for q_, j in enumerate(order_b)])

        def tileF(j, seq):
            tok = j * 128
            tsel["i"] = seq
            tm_ = tmps[seq % 2]
            (Ef, Ef_b), (kh, kh_b), (dec, dec_b), (EQ, EQ_b), (EK, EK_b) = tm_["Ef"], tm_["kh"], tm_["dec"], tm_["EQ"], tm_["EK"]
            (qe, qe_b), (ke, ke_b), (Am, Am_b), (sq, sq_b), (rs, rs_b), (tt, tt_b) = tm_["qe"], tm_["ke"], tm_["Am"], tm_["sq"], tm_["rs"], tm_["tt"]
            L, L_b = gate_L(j, 0, 512)
            S.op("pe", lambda e: e.matmul(pM[:, 0:256], lhsT=tri[:, 2, :], rhs=L[:, 0:256], start=True, stop=True), reads=[tri_b, L_b], writes=[pM_b])
            S.op("act", lambda e: e.activation(out=Ef[:, :], in_=pM[:, 0:256], func=AF.Exp, scale=NI), reads=[pM_b], writes=[Ef_b])
            S.op("dve", lambda e: e.tensor_tensor(out=kh[:, :], in0=ktok[:, j, :], in1=Ef[:, :], op=ALU.mult), reads=[ktok_b, Ef_b], writes=[kh_b])
            for d in range(2):
                for grp in range(2):
                    c0 = (d * 2 + grp) * 128
                    S.op("pe", lambda e, d=d, grp=grp, c0=c0: e.matmul(pC[:, c0:c0 + 128], lhsT=L[:, d * 256 + grp * 128:d * 256 + (grp + 1) * 128], rhs=tri[:, d, :],
                                                                     start=True, stop=True), reads=[L_b, tri_b], writes=[pC_b])
            S.op("act", lambda e: e.activation(out=EQ[:, :], in_=pC[:, :], func=AF.Exp, scale=NI), reads=[pC_b], writes=[EQ_b])
            S.op("act", lambda e: e.activation(out=EK[:, :], in_=pC[:, :], func=AF.Exp, scale=-NI), reads=[pC_b], writes=[EK_b])
            S.op("act", lambda e: e.activation(out=dec[:, 0:2], in_=pC[:, :].rearrange("p (a c) -> p a c", c=128)[:, 0:2, 127], func=AF.Exp, scale=NI),
                 reads=[pC_b], writes=[dec_b])
            for d in range(2):
                S.op("pool", lambda e, d=d: e.tensor_tensor(out=qe[:, d, :, :], in0=qT[:, :, tok:tok + 128],
                                                         in1=EQ[:, d * 256:(d + 1) * 256].rearrange("p (g c) -> p g c", c=128), op=ALU.mult),
                     reads=[qT_b, EQ_b], writes=[qe_b])
                S.op("dve", lambda e, d=d: e.tensor_tensor(out=ke[:, d, :, :], in0=kT[:, :, tok:tok + 128],
                                                        in1=EK[:, d * 256:(d + 1) * 256].rearrange("p (g c) -> p g c", c=128), op=ALU.mult),
                     reads=[kT_b, EK_b], writes=[ke_b])
            pAs = ((pA0, pA0_b), (pA1, pA1_b))
            for d in range(2):
                for h in range(4):
                    grp, hl = h // 2, h % 2
                    hp = slice(hl * 64, hl * 64 + 64)
                    pA, pA_b = pAs[hl]
                    cb = (d * 2 + grp) * 128
                    S.op("pe", lambda e, d=d, grp=grp, hp=hp, pA=pA, cb=cb: e.matmul(pA[:, cb:cb + 128], lhsT=ke[hp, d, grp, :], rhs=qe[hp, d, grp, :],
                                                                                   start=True, stop=True), reads=[ke_b, qe_b], writes=[pA_b])
            for hl in range(2):
                pA, pA_b = pAs[hl]
                S.op("dve", lambda e, hl=hl, pA=pA: e.tensor_tensor(out=Am[:, hl, :], in0=pA[:, :], in1=mk[:, :], op=ALU.mult), reads=[pA_b, mk_b], writes=[Am_b])
            yield
            for grp in range(2):
                S.op("pe", lambda e, grp=grp: e.matmul(pkv[:, grp * 256:(grp + 1) * 256], lhsT=kh[:, grp * 128:(grp + 1) * 128], rhs=vall[:, j, grp * 256:(grp + 1) * 256],
                                                      start=True, stop=True), reads=[kh_b, vall_b], writes=[pkv_b])
            if j >= 2 or need_ctx:
                for h in range(4):
                    grp, hl = h // 2, h % 2
                    hp = slice(hl * 64, hl * 64 + 64)
                    o_ap = pO[:, h * 128:(h + 1) * 128]
                    S.op("pe", lambda e, h=h, hl=hl, grp=grp: e.matmul(pO[:, h * 128:(h + 1) * 128], lhsT=vall[:, j, h * 128:(h + 1) * 128],
                                                                     rhs=Am[:, hl, grp * 128:(grp + 1) * 128], start=True, stop=False),
                         reads=[vall_b, Am_b], writes=[pO_b])
                    S.op("pe", lambda e, h=h, hl=hl, grp=grp: e.matmul(pO[:, h * 128:(h + 1) * 128], lhsT=vall[:, j, h * 128:(h + 1) * 128],
                                                                     rhs=Am[:, hl, (2 + grp) * 128:(3 + grp) * 128], start=False, stop=False),
                         reads=[vall_b, Am_b], writes=[pO_b])
                    S.op("pe", lambda e, h=h, grp=grp, hp=hp: e.matmul(pO[:, h * 128:(h + 1) * 128], lhsT=SFb[hp, grp, :], rhs=qe[hp, 0, grp, :], start=False, stop=False),
                         reads=[SFb_b, qe_b], writes=[pO_b])
                    S.op("pe", lambda e, h=h, grp=grp, hp=hp: e.matmul(pO[:, h * 128:(h + 1) * 128], lhsT=SBst[hp, grp, j, :], rhs=qe[hp, 1, grp, :], start=False, stop=True),
                         reads=[SBst_b, qe_b], writes=[pO_b])
                sqb, sqb_b = sqbs[seq % 2]
                S.op("act", lambda e: e.activation(out=sqb[:, :], in_=pO[:, :], func=AF.Square), reads=[pO_b], writes=[sqb_b])
                S.op("pe", lambda e: e.matmul(pms[:, :], lhsT=avg[:, :], rhs=sqb[:, :], start=True, stop=True), reads=[avg_b, sqb_b], writes=[pms_b])
                S.op("act", lambda e: e.activation(out=sq[:, :], in_=pms[:, :], func=AF.Ln, bias=K.epsc[:, 0:1], scale=1.0), reads=[pms_b, K.epsc_b], writes=[sq_b])
                S.op("act", lambda e: e.activation(out=rs[:, :], in_=sq[:, :], func=AF.Exp, scale=-0.5), reads=[sq_b], writes=[rs_b])
                S.op("dve", lambda e: e.tensor_tensor(out=tt[:, :], in0=pO[:, :], in1=rs[:, :], op=ALU.mult), reads=[pO_b, rs_b], writes=[tt_b])
                y, y_b = ys[cnt["y"] % 2]
                cnt["y"] += 1
                S.op("pool", lambda e: e.tensor_tensor(out=y[:, :], in0=tt[:, :], in1=gnb[:, :], op=ALU.mult), reads=[tt_b, gnb_b], writes=[y_b])
                with nc.allow_non_contiguous_dma(reason="head-interleaved branch store"):
                    S.dma("sp", BR[0, :, tok:tok + 128].rearrange("(h v) c -> v h c", v=128), y[:, :].rearrange("p (h c) -> p h c", c=128), reads=[y_b], writes=[dr["BR_b"]])
            if j < NT - 1:
                state_update(SF, SF_b, dec, dec_b)
                S.op("act", lambda e: e.activation(out=SFb[:, :, :], in_=SF[:, :, :], func=AF.Copy), reads=[SF_b], writes=[SFb_b])

        _pipeline([tileF(j, j) for j in range(NT)])
        S.barrier()


W_SPECS = [("w_ada", [2, D, 3 * D]), ("b_ada", [2, 3 * D]), ("w_in", [2, D, INC]), ("gla_w_gate", [2, 2, 16, 256]),
           ("gla_b_gate", [2, 2, 256]), ("gla_norm", [2, 512]), ("win_sink", [2, 8]), ("glb_q_norm", [2, 64]),
           ("glb_k_norm", [2, 64]), ("diff_lambda", [2, 4, 64]), ("diff_norm", [2, 128]), ("w_merge", [2, 4, D, D]),
           ("w_up", [2, 4, 512, D]), ("w_out", [2, D, D]), ("ln_g", [2, D]), ("ln_b", [2, D])]
C_SPECS = [("ident", [128, 128], F32), ("bd64", [128, 128], BF16), ("ropeC", [128, T], F32), ("ropeS", [128, T], F32),
           ("bandmask", [2, 128, 512], BF16), ("tri", [4, 128, 128], F32), ("glamask", [128, 512], BF16)]


def host_consts():
    import ml_dtypes
    CTn, STn = rope_tables()
    bd = np.zeros((128, 128), np.float32)
    bd[:64, :64] = 1.0 / 64
    bd[64:, 64:] = 1.0 / 64
    return {"ident": np.eye(128, dtype=np.float32), "bd64": bd.astype(ml_dtypes.bfloat16), "ropeC": CTn, "ropeS": STn,
            "bandmask": band_mask().astype(ml_dtypes.bfloat16), "tri": tri_consts(),
            "glamask": gla_mask().astype(ml_dtypes.bfloat16)}


def declare_io(nc, scratch_kind="Internal"):
    dr = {}
    dr["x"] = nc.dram_tensor("x", [NX, D], F32, kind="ExternalInput").ap()
    dr["ctx"] = nc.dram_tensor("ctx", [NCTX, D], F32, kind="ExternalInput").ap()
    dr["c"] = nc.dram_tensor("c", [D], F32, kind="ExternalInput").ap()
    dr["c_ctx"] = nc.dram_tensor("c_ctx", [D], F32, kind="ExternalInput").ap()
    for nm, shp in W_SPECS:
        dr[nm] = nc.dram_tensor(nm, shp, F32, kind="ExternalInput").ap()
    for nm, shp, dt in C_SPECS:
        dr[nm] = nc.dram_tensor(nm, shp, dt, kind="ExternalInput").ap()
    dr["out"] = nc.dram_tensor("out", [NX, D], F32, kind="ExternalOutput").ap()
    for nm, shp, dt in (("hT_d", [128, 8, T], BF16), ("PT", [INC, T], BF16), ("GT", [32, T], F32), ("VS", [128, NT, VS_W], BF16),
                        ("BR", [4, 512, T], BF16), ("x1", [NX, D], F32), ("c1", [NCTX, D], F32)):
        dr[nm] = nc.dram_tensor(nm, shp, dt, kind=scratch_kind).ap()
        dr[nm + "_b"] = Buf(nm)
    dr["in_b"] = Buf("inputs")
    dr["out_b"] = Buf("out")
    return dr


def emit_full(nc, S, es, layers=(0, 1), scratch_kind="Internal"):
    dr = declare_io(nc, scratch_kind)
    K = make_ctx(nc, S, es)
    load_consts(K, dr)
    for l in layers:
        need_ctx = l < DEPTH - 1
        if l == 0:
            x_src, c_src, src_b = dr["x"], dr["ctx"], dr["in_b"]
        else:
            x_src, c_src, src_b = dr["x1"], dr["c1"], dr["x1_b"]
        phase0_mod(K, dr, l)
        with contextlib.ExitStack() as es2:
            hT, hT_b = alloc(es2, nc, "hT", [128, 8, T], BF16)
            phase1_ln(K, dr, l, x_src, c_src, src_b, hT, hT_b)
            phase2_proj(K, dr, l, hT, hT_b)
        phase6_gla(K, dr, l, need_ctx)
        phase5_window(K, dr, l, need_ctx)
        phase3_global(K, dr, l, need_ctx)
        if l == DEPTH - 1:
            x_dst, x_dst_b = dr["out"], dr["out_b"]
        else:
            x_dst, x_dst_b = dr["x1"], dr["x1_b"]
        with contextlib.ExitStack() as esw:
            W7, issue_w7 = phase7_weights(K, dr, l, esw)
            phase4_diff(K, dr, l, need_ctx, after_loads=issue_w7)
            phase7_merge(K, dr, l, need_ctx, x_src, c_src, src_b, x_dst, dr["c1"], x_dst_b, dr["x1_b"], W7)
    S.finish()
    return dr


def build_program(layers=(0, 1), scratch_kind="Internal"):
    nc0 = bass.Bass("TRN2", target_bir_lowering=False)
    with contextlib.ExitStack() as es0:
        S0 = Sched(nc0)
        emit_full(nc0, S0, es0, layers, scratch_kind)
    nc = bass.Bass("TRN2", target_bir_lowering=False)
    es = contextlib.ExitStack()
    S = Sched(nc, needed=S0.needed, es=es)
    emit_full(nc, S, es, layers, scratch_kind)
    return nc, S, es


def kernel(**inputs):
    nc, S, es = build_program()
    consts = host_consts()
    B = inputs["x"].shape[0]
    shared = {nm: np.ascontiguousarray(np.asarray(inputs[nm], dtype=np.float32)) for nm, _ in W_SPECS}
    shared["c_ctx"] = np.ascontiguousarray(np.asarray(inputs["c_ctx"], dtype=np.float32))
    shared.update(consts)
    in_maps = []
    for b in range(B):
        m = dict(shared)
        m["x"] = np.ascontiguousarray(np.asarray(inputs["x"][b], dtype=np.float32))
        m["ctx"] = np.ascontiguousarray(np.asarray(inputs["ctx"][b], dtype=np.float32))
        m["c"] = np.ascontiguousarray(np.asarray(inputs["c"][b], dtype=np.float32))
        in_maps.append(m)
    res = run_bass_kernel_spmd(nc, in_maps, core_ids=list(range(B)))
    return np.stack([np.asarray(res.results[b]["out"], dtype=np.float32) for b in range(B)], axis=0)
```

```python
import contextlib
import math
import numpy as np
import concourse.bass as bass
import concourse.mybir as mybir
from concourse.bass_utils import run_bass_kernel_spmd

F32 = mybir.dt.float32
BF16 = mybir.dt.bfloat16
AF = mybir.ActivationFunctionType
ALU = mybir.AluOpType
AX = mybir.AxisListType

D = 1024
NCTX = 256
NX = 4096
T = NCTX + NX
NT = T // 128
INC = 5920
DEPTH = 2
EPS = 1e-6
BLOCKS = [(0, 256)] + [(256 + 512 * i, 512) for i in range(8)]


class Buf:
    __slots__ = ("name", "writers", "readers", "psum")

    def __init__(self, name, psum=False):
        self.name = name
        self.writers = {}
        self.readers = {}
        self.psum = psum


class Sched:
    NS = 8
    ENGS = ("pe", "act", "dve", "pool", "sp")

    def __init__(self, nc, needed=None, es=None):
        self.nc = nc
        self.dry = needed is None
        self.needed = {e: set() for e in self.ENGS} if self.dry else needed
        self.rank = None
        if not self.dry:
            self.rank = {}
            for e in self.ENGS:
                self.rank[e] = {idx: i + 1 for i, idx in enumerate(sorted(self.needed[e]))}
        self.h = {"pe": nc.tensor, "act": nc.scalar, "dve": nc.vector, "pool": nc.gpsimd, "sp": nc.sync}
        self.count = {e: 0 for e in self.ENGS}
        self.seen = {e: {} for e in self.ENGS}
        self.ndma = {e: 0 for e in self.ENGS}
        self.sem = {}
        self.n_wait = 0
        self.n_ins = 0
        if not self.dry:
            for e in self.ENGS:
                self.sem[("p", e)] = es.enter_context(nc.semaphore("prog_" + e))
            for q in ("sp", "pool", "act"):
                for s in range(self.NS):
                    self.sem[("d", q, s)] = es.enter_context(nc.semaphore(f"dma_{q}_{s}"))

    def _wait(self, eng, key, val):
        if self.seen[eng].get(key, 0) >= val:
            return
        self.seen[eng][key] = val
        self.n_wait += 1
        if key[0] == "p":
            if self.dry:
                self.needed[key[1]].add(val - 1)
                return
            v = self.rank[key[1]][val - 1]
        else:
            if self.dry:
                return
            v = val
        self.h[eng].wait_ge(self.sem[key], v)

    def _deps(self, eng, reads, writes, skip_own=True):
        own = ("p", eng) if skip_own else None
        waits = {}
        for b in reads:
            for k, v in b.writers.items():
                if waits.get(k, 0) < v:
                    waits[k] = v
            if b.psum:
                for k, v in b.readers.items():
                    if k == own:
                        continue
                    if waits.get(k, 0) < v:
                        waits[k] = v
        for b in writes:
            for k, v in b.writers.items():
                if k == own:
                    continue
                if waits.get(k, 0) < v:
                    waits[k] = v
            for k, v in b.readers.items():
                if k == own:
                    continue
                if waits.get(k, 0) < v:
                    waits[k] = v
        for k, v in waits.items():
            self._wait(eng, k, v)

    def op(self, eng, fn, reads=(), writes=()):
        self._deps(eng, reads, writes)
        idx = self.count[eng]
        self.count[eng] = idx + 1
        self.n_ins += 1
        key = ("p", eng)
        if not self.dry:
            ins = fn(self.h[eng])
            if idx in self.rank[eng]:
                ins.then_inc(self.sem[key], 1)
        for b in reads:
            b.readers[key] = idx + 1
        for b in writes:
            b.writers[key] = idx + 1

    def dma(self, q, out, in_, reads=(), writes=(), **kw):
        self._deps(q, reads, writes, skip_own=False)
        i = self.ndma[q]
        self.ndma[q] = i + 1
        self.n_ins += 1
        slot = i % self.NS
        key = ("d", q, slot)
        val = 16 * (i // self.NS + 1)
        if val > 16:
            self._wait(q, key, val - 16)
        if not self.dry:
            self.h[q].dma_start(out=out, in_=in_, **kw).then_inc(self.sem[key], 16)
        for b in reads:
            b.readers[key] = val
        for b in writes:
            b.writers[key] = val

    def barrier(self):
        toks = {}
        for e in self.ENGS:
            if self.count[e] > 0:
                toks[("p", e)] = self.count[e]
        for q in ("sp", "pool", "act"):
            n = self.ndma[q]
            for s in range(self.NS):
                if n > s:
                    last = ((n - 1 - s) // self.NS) * self.NS + s
                    toks[("d", q, s)] = 16 * (last // self.NS + 1)
        for e in self.ENGS:
            for k, v in toks.items():
                if k == ("p", e):
                    continue
                self._wait(e, k, v)

    def finish(self):
        self.barrier()


class Ctx:
    pass


_UID = [0]


def alloc(es, nc, name, shape, dt):
    _UID[0] += 1
    t = es.enter_context(nc.sbuf_tensor(f"sb{_UID[0]}_{name}", list(shape), dt))
    return t, Buf(name)


def rope_tables():
    half = 16
    freqs = (np.float32(10000.0) ** (-np.arange(half, dtype=np.float32) / np.float32(half))).astype(np.float32)
    t = np.arange(NX)
    row = (t // 64).astype(np.float32)
    col = (t % 64).astype(np.float32)
    ang = np.stack([row[:, None] * freqs[None, :], col[:, None] * freqs[None, :]], axis=1).astype(np.float32)
    cos = np.cos(ang).astype(np.float32)
    sin = np.sin(ang).astype(np.float32)
    CT = np.ones((128, T), np.float32)
    ST = np.zeros((128, T), np.float32)
    for p in range(128):
        d = p % 64
        axis = d // 32
        f = d % 16
        CT[p, NCTX:] = cos[:, axis, f]
        ST[p, NCTX:] = sin[:, axis, f]
    return CT, ST


def make_ctx(nc, S, es):
    K = Ctx()
    K.nc, K.S, K.es = nc, S, es
    K.ps = []
    K.psall = es.enter_context(nc.psum_tensor("psall", [128, 4096], F32))
    for i in range(8):
        K.ps.append((K.psall[:, i * 512:(i + 1) * 512], Buf(f"ps{i}", psum=True)))
    return K


def load_consts(K, dr):
    nc, S, es = K.nc, K.S, K.es
    K.ident, K.ident_b = alloc(es, nc, "ident", [128, 128], F32)
    S.dma("sp", K.ident[:], dr["ident"][:, :], writes=[K.ident_b])
    K.ones, K.ones_b = alloc(es, nc, "ones", [128, 128], F32)
    S.op("pool", lambda e: e.memset(K.ones[:], 1.0), writes=[K.ones_b])
    K.epsc, K.epsc_b = alloc(es, nc, "epsc", [128, 1], F32)
    S.op("pool", lambda e: e.memset(K.epsc[:], EPS), writes=[K.epsc_b])
    K.modT, K.modT_b = alloc(es, nc, "modT", [128, 24, 2], F32)
    K.gate, K.gate_b = [], []
    for n in range(2):
        g, g_b = alloc(es, nc, f"gate{n}", [128, 1024], F32)
        K.gate.append(g)
        K.gate_b.append(g_b)
    K.sc, K.sc_b = alloc(es, nc, "sc", [128, 8, 2], F32)
    craw, craw_b = alloc(es, nc, "craw", [128, 8, 2], F32)
    with nc.allow_non_contiguous_dma(reason="tiny conditioning vector load"):
        S.dma("sp", craw[:, :, 0], dr["c"].rearrange("(k p) -> p k", p=128), writes=[craw_b])
        S.dma("sp", craw[:, :, 1], dr["c_ctx"].rearrange("(k p) -> p k", p=128), writes=[craw_b])
    S.op("act", lambda e: e.activation(out=K.sc[:], in_=craw[:], func=AF.Silu), reads=[craw_b], writes=[K.sc_b])


def phase0_mod(K, dr, l):
    nc, S = K.nc, K.S
    with contextlib.ExitStack() as es:
        wst = [alloc(es, nc, f"wada{i}", [128, 8, 512], F32) for i in range(2)]
        K.scb, K.scb_b = alloc(es, nc, "scb", [128, 8, 2, 128], F32)
        for k in range(8):
            for n in range(2):
                S.op("dve", lambda e, k=k, n=n: e.tensor_scalar(out=K.scb[:, k, n, :], in0=K.ones[:], scalar1=K.sc[:, k, n:n + 1],
                                                              scalar2=None, op0=ALU.mult),
                     reads=[K.sc_b, K.ones_b], writes=[K.scb_b])
        bT, bT_b = alloc(es, nc, "badaT", [128, 24], F32)
        gb, gb_b = alloc(es, nc, "gbias", [128, 1024], F32)
        with nc.allow_non_contiguous_dma(reason="tiny bias load"):
            S.dma("sp", bT[:], dr["b_ada"][l].rearrange("(j p) -> p j", p=128), writes=[bT_b])
        S.dma("sp", gb[:], dr["b_ada"][l, 2048:3072].partition_broadcast(128), writes=[gb_b])
        wv = dr["w_ada"][l].rearrange("(k p) c -> p k c", p=128)
        pm, pm_b = K.ps[0]
        pmv = pm[:, 0:48].rearrange("p (j n) -> p j n", n=2)
        for g in range(6):
            w, w_b = wst[g % 2]
            S.dma("sp", w[:], wv[:, :, g * 512:(g + 1) * 512], writes=[w_b])
            for jj in range(4):
                j = g * 4 + jj
                for k in range(8):
                    S.op("pe", lambda e, w=w, jj=jj, j=j, k=k: e.matmul(pmv[:, j, :], lhsT=w[:, k, jj * 128:(jj + 1) * 128],
                                                                     rhs=K.sc[:, k, :], start=(k == 0), stop=(k == 7)),
                         reads=[w_b, K.sc_b], writes=[pm_b])
            if g >= 4:
                half = g - 4
                for n in range(2):
                    pg, pg_b = K.ps[1 + n]
                    for k in range(8):
                        S.op("pe", lambda e, w=w, n=n, k=k, pg=pg: e.matmul(pg[:], lhsT=K.scb[:, k, n, :], rhs=w[:, k, :],
                                                                         start=(k == 0), stop=(k == 7)),
                             reads=[w_b, K.scb_b], writes=[pg_b])
                    S.op("dve", lambda e, n=n, half=half, pg=pg: e.tensor_tensor(out=K.gate[n][:, half * 512:(half + 1) * 512], in0=pg[:],
                                                                              in1=gb[:, half * 512:(half + 1) * 512], op=ALU.add),
                         reads=[pg_b, gb_b], writes=[K.gate_b[n]])
        for n in range(2):
            S.op("dve", lambda e, n=n: e.tensor_tensor(out=K.modT[:, :, n], in0=pmv[:, :, n], in1=bT[:], op=ALU.add),
                 reads=[pm_b, bT_b], writes=[K.modT_b])
        S.op("dve", lambda e: e.tensor_scalar_add(out=K.modT[:, 8:16, :], in0=K.modT[:, 8:16, :], scalar1=1.0),
             reads=[K.modT_b], writes=[K.modT_b])
        S.barrier()


def phase1_ln(K, dr, l, x_src, c_src, src_b, hT, hT_b):
    nc, S = K.nc, K.S
    with contextlib.ExitStack() as es:
        xt = [alloc(es, nc, f"xt{i}", [128, 1024], F32) for i in range(3)]
        xn = [alloc(es, nc, f"xn{i}", [128, 1024], F32) for i in range(2)]
        st = [alloc(es, nc, f"st{i}", [128, 16], F32) for i in range(2)]
        ti = 0
        for bi, (t0, n) in enumerate(BLOCKS):
            mn = 1 if bi == 0 else 0
            for j in range(n // 128):
                tok = t0 + j * 128
                x_t, x_b = xt[ti % 3]
                n_t, n_b = xn[ti % 2]
                s_t, s_b = st[ti % 2]
                src = c_src[tok:tok + 128, :] if bi == 0 else x_src[tok - NCTX:tok - NCTX + 128, :]
                S.dma("sp", x_t[:], src, reads=[src_b], writes=[x_b])
                S.op("dve", lambda e, s_t=s_t, x_t=x_t: e.bn_stats(s_t[:, 0:6], x_t[:, 0:512]), reads=[x_b], writes=[s_b])
                S.op("dve", lambda e, s_t=s_t, x_t=x_t: e.bn_stats(s_t[:, 6:12], x_t[:, 512:1024]), reads=[x_b], writes=[s_b])
                S.op("dve", lambda e, s_t=s_t: e.bn_aggr(s_t[:, 12:14], s_t[:, 0:12]), reads=[s_b], writes=[s_b])
                S.op("act", lambda e, s_t=s_t: e.activation(out=s_t[:, 15:16], in_=s_t[:, 13:14], func=AF.Sqrt, bias=K.epsc[:, 0:1], scale=1.0),
                     reads=[s_b, K.epsc_b], writes=[s_b])
                S.op("dve", lambda e, s_t=s_t: e.reciprocal(out=s_t[:, 14:15], in_=s_t[:, 15:16]), reads=[s_b], writes=[s_b])
                S.op("dve", lambda e, s_t=s_t, x_t=x_t, n_t=n_t: e.tensor_scalar(out=n_t[:], in0=x_t[:], scalar1=s_t[:, 12:13],
                                                                            scalar2=s_t[:, 14:15], op0=ALU.subtract, op1=ALU.mult),
                     reads=[x_b, s_b], writes=[n_b])
                for k in range(8):
                    pk, pk_b = K.ps[k]
                    S.op("pe", lambda e, pk=pk, n_t=n_t, k=k, j=j: e.transpose(out=pk[:, j * 128:(j + 1) * 128],
                                                                          in_=n_t[:, k * 128:(k + 1) * 128], identity=K.ident[:]),
                         reads=[n_b, K.ident_b], writes=[pk_b])
                ti += 1
            for k in range(8):
                pk, pk_b = K.ps[k]
                if k % 2 == 0:
                    S.op("dve", lambda e, pk=pk, k=k: e.tensor_scalar(out=hT[:, k, t0:t0 + n], in0=pk[:, 0:n], scalar1=K.modT[:, 8 + k, mn:mn + 1],
                                                                  scalar2=K.modT[:, k, mn:mn + 1], op0=ALU.mult, op1=ALU.add),
                         reads=[pk_b, K.modT_b], writes=[hT_b])
                else:
                    S.op("act", lambda e, pk=pk, k=k: e.activation(out=hT[:, k, t0:t0 + n], in_=pk[:, 0:n], func=AF.Identity,
                                                               scale=K.modT[:, 8 + k, mn:mn + 1], bias=K.modT[:, k, mn:mn + 1]),
                         reads=[pk_b, K.modT_b], writes=[hT_b])
            S.dma("pool", dr["hT_d"][:, :, t0:t0 + n], hT[:, :, t0:t0 + n], reads=[hT_b], writes=[dr["hT_d_b"]])
        S.barrier()


def _fm_chunks():
    ch = []
    for c in range(2):
        ch.append((0 + 128 * c, 128, "qA"))
    for c in range(2):
        ch.append((256 + 128 * c, 128, "plain"))
    ch.append((1024, 32, "g"))
    for base in (1056, 2336, 3872, 5408):
        for c in range(4):
            ch.append((base + 128 * c, 128, "silu"))
    for c in range(4):
        ch.append((1568 + 128 * c, 128, "rope"))
    ch.append((2080, 128, "rope"))
    for c in range(4):
        ch.append((4384 + 128 * c, 128, "rope"))
    for c in range(2):
        ch.append((4896 + 128 * c, 128, "rope"))
    for c in range(4):
        ch.append((2848 + 128 * c, 128, "ropeq"))
    for c in range(2):
        ch.append((3360 + 128 * c, 128, "ropek"))
    return ch


VS_W = 1792


def phase2_proj(K, dr, l, hT, hT_b):
    nc, S = K.nc, K.S
    wv_all = dr["w_in"][l].rearrange("(k p) c -> p k c", p=128)
    with contextlib.ExitStack() as es:
        CT, CT_b = alloc(es, nc, "CT", [128, T], F32)
        ST, ST_b = alloc(es, nc, "ST", [128, T], F32)
        S.dma("sp", CT[:], dr["ropeC"][:, :], writes=[CT_b])
        S.dma("sp", ST[:], dr["ropeS"][:, :], writes=[ST_b])
        bd, bd_b = alloc(es, nc, "bd64", [128, 128], BF16)
        S.dma("sp", bd[:], dr["bd64"][:, :], writes=[bd_b])
        sqb2 = [alloc(es, nc, f"sqb2_{i}", [128, 512], BF16) for i in range(2)]
        gq, gq_b = alloc(es, nc, "gq", [128, 4], F32)
        with nc.allow_non_contiguous_dma(reason="tiny gain vectors"):
            for hh in range(2):
                for ci, nm in enumerate(("glb_q_norm", "glb_k_norm")):
                    src = dr[nm][l]
                    S.dma("sp", gq[hh * 64:(hh + 1) * 64, 2 * ci:2 * ci + 1], src.rearrange("(d o) -> d o", o=1), writes=[gq_b])
                    for a in range(2):
                        for hf in range(2):
                            p0 = hh * 64 + a * 32 + hf * 16
                            s0 = a * 32 + (1 - hf) * 16
                            S.dma("sp", gq[p0:p0 + 16, 2 * ci + 1:2 * ci + 2], src[s0:s0 + 16].rearrange("(d o) -> d o", o=1), writes=[gq_b])
        wt = [alloc(es, nc, f"wch{i}", [128, 8, 128], BF16) for i in range(3)]
        wr = [alloc(es, nc, f"wchR{i}", [128, 8, 128], BF16) for i in range(2)]
        stg = [alloc(es, nc, f"stg{i}", [128, 512], BF16) for i in range(4)]
        f32t = [alloc(es, nc, f"f32t{i}", [128, 512], F32) for i in range(8)]
        gst = [alloc(es, nc, f"gst{i}", [32, 512], F32) for i in range(2)]
        cnt = {"stg": 0, "f": 0, "bank": 0, "w": 0, "wr": 0, "g": 0}

        def nxt(name, pool):
            i = cnt[name]
            cnt[name] = i + 1
            return pool[i % len(pool)]

        def bank():
            return nxt("bank", K.ps)

        PT = dr["PT"]
        PT_b = dr["PT_b"]
        chunks = _fm_chunks()
        wv, wv_b = alloc(es, nc, "wvtok", [128, 8, 1408], BF16)
        wtiles = {}

        def load_w(i):
            if i >= len(chunks):
                return
            c0_, M_, _ = chunks[i]
            w_, wb_ = wt[i % 3]
            S.dma("pool", w_[:, :, 0:M_], wv_all[:, :, c0_:c0_ + M_], writes=[wb_])
            wtiles[i] = (w_, wb_)

        def perm_w(i):
            if i >= len(chunks) or not chunks[i][2].startswith("rope"):
                return
            w_, wb_ = wtiles[i]
            wR_, wRb_ = wr[i % 2]
            w4 = w_[:, :, :].rearrange("p k (g h f) -> p (k g) h f", h=2, f=16)
            r4 = wR_[:, :, :].rearrange("p k (g h f) -> p (k g) h f", h=2, f=16)
            S.op("pool", lambda e: e.tensor_scalar(out=r4[:, :, 0, :], in0=w4[:, :, 1, :], scalar1=-1.0, scalar2=None, op0=ALU.mult),
                 reads=[wb_], writes=[wRb_])
            S.op("pool", lambda e: e.tensor_copy(out=r4[:, :, 1, :], in_=w4[:, :, 0, :]), reads=[wb_], writes=[wRb_])

        load_w(0)
        load_w(1)
        for (d0, s0, m) in ((0, 512, 512), (512, 2208, 128), (640, 3616, 256), (896, 5152, 256), (1152, 256, 256)):
            S.dma("pool", wv[:, :, d0:d0 + m], wv_all[:, :, s0:s0 + m], writes=[wv_b])
        perm_w(0)
        for ci, (c0, M, kind) in enumerate(chunks):
            load_w(ci + 2)
            perm_w(ci + 1)
            w, w_b = wtiles[ci]
            isrope = kind.startswith("rope")
            if isrope:
                wR, wR_b = wr[ci % 2]
            for bi, (t0, n) in enumerate(BLOCKS):
                p1, p1_b = bank()
                for k in range(8):
                    S.op("pe", lambda e, k=k: e.matmul(p1[0:M, 0:n], lhsT=w[:, k, 0:M], rhs=hT[:, k, t0:t0 + n], start=(k == 0), stop=(k == 7)),
                         reads=[w_b, hT_b], writes=[p1_b])
                if isrope:
                    p2, p2_b = bank()
                    for k in range(8):
                        S.op("pe", lambda e, k=k: e.matmul(p2[0:M, 0:n], lhsT=wR[:, k, 0:M], rhs=hT[:, k, t0:t0 + n], start=(k == 0), stop=(k == 7)),
                             reads=[wR_b, hT_b], writes=[p2_b])
                if kind == "g":
                    g_t, g_b = nxt("g", gst)
                    S.op("act", lambda e: e.activation(out=g_t[0:32, 0:n], in_=p1[0:32, 0:n], func=AF.Copy), reads=[p1_b], writes=[g_b])
                    S.dma("sp", dr["GT"][:, t0:t0 + n], g_t[0:32, 0:n], reads=[g_b], writes=[dr["GT_b"]])
                    continue
                o_t, o_b = nxt("stg", stg)
                if kind == "plain":
                    S.op("act", lambda e: e.activation(out=o_t[:, 0:n], in_=p1[:, 0:n], func=AF.Copy), reads=[p1_b], writes=[o_b])
                elif kind == "qA":
                    S.op("act", lambda e: e.activation(out=o_t[:, 0:n], in_=p1[:, 0:n], func=AF.Copy, scale=0.125), reads=[p1_b], writes=[o_b])
                elif kind == "silu":
                    S.op("act", lambda e: e.activation(out=o_t[:, 0:n], in_=p1[:, 0:n], func=AF.Silu), reads=[p1_b], writes=[o_b])
                elif kind == "rope":
                    a_t, a_b = nxt("f", f32t)
                    b_t, b_b = nxt("f", f32t)
                    S.op("dve", lambda e: e.tensor_tensor(out=a_t[:, 0:n], in0=p1[:, 0:n], in1=CT[:, t0:t0 + n], op=ALU.mult),
                         reads=[p1_b, CT_b], writes=[a_b])
                    S.op("dve", lambda e: e.tensor_tensor(out=b_t[:, 0:n], in0=p2[:, 0:n], in1=ST[:, t0:t0 + n], op=ALU.mult),
                         reads=[p2_b, ST_b], writes=[b_b])
                    S.op("pool", lambda e: e.tensor_tensor(out=o_t[:, 0:n], in0=a_t[:, 0:n], in1=b_t[:, 0:n], op=ALU.add),
                         reads=[a_b, b_b], writes=[o_b])
                else:
                    gi = 0 if kind == "ropeq" else 2
                    sq_t, sq_b = sqb2[cnt["bank"] % 2]
                    a_t, a_b = nxt("f", f32t)
                    b_t, b_b = nxt("f", f32t)
                    r_t, r_b = nxt("f", f32t)
                    S.op("act", lambda e: e.activation(out=sq_t[:, 0:n], in_=p1[:, 0:n], func=AF.Square), reads=[p1_b], writes=[sq_b])
                    p3, p3_b = bank()
                    S.op("pe", lambda e: e.matmul(p3[:, 0:n], lhsT=bd[:], rhs=sq_t[:, 0:n], start=True, stop=True),
                         reads=[bd_b, sq_b], writes=[p3_b])
                    S.op("act", lambda e: e.activation(out=r_t[:, 0:n], in_=p3[:, 0:n], func=AF.Sqrt, bias=K.epsc[:, 0:1], scale=1.0),
                         reads=[p3_b, K.epsc_b], writes=[r_b])
                    S.op("dve", lambda e: e.reciprocal(out=r_t[:, 0:n], in_=r_t[:, 0:n]), reads=[r_b], writes=[r_b])
                    S.op("dve", lambda e: e.scalar_tensor_tensor(out=a_t[:, 0:n], in0=p1[:, 0:n], scalar=gq[:, gi:gi + 1], in1=CT[:, t0:t0 + n],
                                                               op0=ALU.mult, op1=ALU.mult), reads=[p1_b, CT_b, gq_b], writes=[a_b])
                    S.op("dve", lambda e: e.scalar_tensor_tensor(out=b_t[:, 0:n], in0=p2[:, 0:n], scalar=gq[:, gi + 1:gi + 2], in1=ST[:, t0:t0 + n],
                                                               op0=ALU.mult, op1=ALU.mult), reads=[p2_b, ST_b, gq_b], writes=[b_b])
                    S.op("pool", lambda e: e.tensor_tensor(out=a_t[:, 0:n], in0=a_t[:, 0:n], in1=b_t[:, 0:n], op=ALU.add),
                         reads=[a_b, b_b], writes=[a_b])
                    S.op("pool", lambda e: e.tensor_tensor(out=o_t[:, 0:n], in0=a_t[:, 0:n], in1=r_t[:, 0:n], op=ALU.mult),
                         reads=[a_b, r_b], writes=[o_b])
                S.dma("sp", PT[c0:c0 + M, t0:t0 + n], o_t[0:M, 0:n], reads=[o_b], writes=[PT_b])
        vst = [alloc(es, nc, f"vst{i}", [128, VS_W], BF16) for i in range(2)]
        for v_t, v_b in vst:
            S.op("pool", lambda e, v_t=v_t: e.memset(v_t[:, 512:1280], 1.0), writes=[v_b])
        for j in range(NT):
            v_t, v_b = vst[j % 2]
            pa, pa_b = bank()
            pb, pb_b = bank()
            pc, pc_b = bank()
            for (pp, pp_b, c0, m) in ((pa, pa_b, 0, 512), (pb, pb_b, 512, 512), (pc, pc_b, 1024, 384)):
                for k in range(8):
                    S.op("pe", lambda e, k=k, pp=pp, c0=c0, m=m: e.matmul(pp[:, 0:m], lhsT=hT[:, k, j * 128:(j + 1) * 128], rhs=wv[:, k, c0:c0 + m],
                                                                        start=(k == 0), stop=(k == 7)), reads=[hT_b, wv_b], writes=[pp_b])
            S.op("act", lambda e: e.activation(out=v_t[:, 0:512], in_=pa[:, 0:512], func=AF.Copy), reads=[pa_b], writes=[v_b])
            S.op("dve", lambda e: e.tensor_copy(out=v_t[:, 512:768].rearrange("p (h c) -> p h c", c=128)[:, :, 0:64],
                                                in_=pb[:, 0:128].rearrange("p (h c) -> p h c", c=64)), reads=[pb_b], writes=[v_b])
            S.op("dve", lambda e: e.tensor_copy(out=v_t[:, 768:1280].rearrange("p (h c) -> p h c", c=128)[:, :, 0:64],
                                                in_=pb[:, 128:384].rearrange("p (h c) -> p h c", c=64)), reads=[pb_b], writes=[v_b])
            S.op("act", lambda e: e.activation(out=v_t[:, 1280:1408], in_=pb[:, 384:512], func=AF.Copy), reads=[pb_b], writes=[v_b])
            S.op("act", lambda e: e.activation(out=v_t[:, 1408:1792], in_=pc[:, 0:384], func=AF.Copy), reads=[pc_b], writes=[v_b])
            S.dma("sp", dr["VS"][:, j, :], v_t[:], reads=[v_b], writes=[dr["VS_b"]])
        S.barrier()


def _qblocks(need_ctx):
    qb = []
    if need_ctx:
        qb.append((0, 256, [0, 1]))
    for i in range(8):
        qb.append((256 + 512 * i, 512, list(range(NT))))
    return qb


def _emit_pairs(steps, emit_qk, emit_pv, skp=2, defer=12):
    pend = []
    deferred = []

    def tick(flush=False):
        for d_ in list(deferred):
            d_[0] -= 1
            if d_[0] <= 0 or flush:
                deferred.remove(d_)
                d_[1]()

    def run_pv(st):
        t_ = emit_pv(st)
        if t_ is not None:
            deferred.append([defer, t_])

    for p in range(0, len(steps), 2):
        pair = steps[p:p + 2]
        if pair[0].get("newg") or pair[0].get("newq"):
            while pend:
                for st in pend.pop(0):
                    run_pv(st)
        for st in pair:
            emit_qk(st)
        pend.append(pair)
        if len(pend) > skp:
            for st in pend.pop(0):
                run_pv(st)
        tick()
    while pend:
        for st in pend.pop(0):
            run_pv(st)
    tick(flush=True)


def phase3_global(K, dr, l, need_ctx):
    nc, S = K.nc, K.S
    PT, VS, BR = dr["PT"], dr["VS"], dr["BR"]
    with contextlib.ExitStack() as es:
        sets = []
        for i in range(2):
            kt = alloc(es, nc, f"c_kt{i}", [128, T], BF16)
            qt = alloc(es, nc, f"c_qt{i}", [128, T], BF16)
            vv = alloc(es, nc, f"c_v{i}", [128, NT, 128], BF16)
            sets.append((kt, qt, vv))
        pts = [alloc(es, nc, f"c_pt{i}", [128, 1024], BF16) for i in range(4)]
        dens = [alloc(es, nc, f"c_den{i}", [64, 512], F32) for i in range(4)]
        outs = [alloc(es, nc, f"c_out{i}", [64, 512], BF16) for i in range(4)]
        ci = {"s": 0, "pt": 0, "o": 0}

        def load(g):
            (kt, kt_b), (qt, qt_b), (vv, vv_b) = sets[g % 2]
            r0 = 3360 + 64 * g
            S.dma("sp", kt[0:64, :], PT[r0:r0 + 64, :], reads=[dr["PT_b"]], writes=[kt_b])
            S.dma("sp", kt[64:128, :], PT[r0:r0 + 64, :], reads=[dr["PT_b"]], writes=[kt_b])
            q0 = 2848 + 128 * g
            S.dma("sp", qt[:, :], PT[q0:q0 + 128, :], reads=[dr["PT_b"]], writes=[qt_b])
            S.dma("sp", vv[:, :, :], VS[:, :, 768 + 128 * g:768 + 128 * (g + 1)], reads=[dr["VS_b"]], writes=[vv_b])

        steps = []
        for g in range(4):
            for bi_, (t0, n, ktiles) in enumerate(_qblocks(need_ctx)):
                for ji, j in enumerate(ktiles):
                    for hh in range(2):
                        steps.append(dict(g=g, hh=hh, t0=t0, n=n, j=j, first=(ji == 0), last=(ji == len(ktiles) - 1),
                                          newg=(hh == 0 and bi_ == 0 and ji == 0), blk=g * 16 + bi_))

        def emit_qk(st):
            g, hh, t0, n, j = st["g"], st["hh"], st["t0"], st["n"], st["j"]
            if st["newg"]:
                if g == 0:
                    load(0)
                if g + 1 < 4:
                    load(g + 1)
            (kt, kt_b), (qt, qt_b), (vv, vv_b) = sets[g % 2]
            hp = slice(hh * 64, hh * 64 + 64)
            bk = ci["s"] % 4
            psx, ps_b = K.ps[bk]
            ci["s"] += 1
            if hh == 0:
                pt, pt_b = pts[ci["pt"] % len(pts)]
                ci["pt"] += 1
                ci["cur"] = (pt, pt_b)
            pt, pt_b = ci["cur"]
            st["pt"] = (pt, pt_b)
            S.op("pe", lambda e: e.matmul(psx[:, 0:n], lhsT=kt[hp, j * 128:(j + 1) * 128], rhs=qt[hp, t0:t0 + n], start=True, stop=True),
                 reads=[kt_b, qt_b], writes=[ps_b])
            if hh == 1:
                src = K.psall[:, (bk - 1) * 512:(bk + 1) * 512].rearrange("p (k c) -> p k c", c=512)[:, :, 0:n]
                S.op("act", lambda e: e.activation(out=pt[:, :].rearrange("p (k c) -> p k c", c=512)[:, :, 0:n], in_=src, func=AF.Exp, scale=0.125),
                     reads=[K.ps[bk - 1][1], ps_b], writes=[pt_b])

        def emit_pv(st):
            g, hh, t0, n, j = st["g"], st["hh"], st["t0"], st["n"], st["j"]
            (kt, kt_b), (qt, qt_b), (vv, vv_b) = sets[g % 2]
            pt, pt_b = st["pt"]
            po, po_b = K.ps[4 + 2 * (st["blk"] % 2) + hh]
            S.op("pe", lambda e: e.matmul(po[:, 0:n], lhsT=vv[:, j, :], rhs=pt[:, hh * 512:hh * 512 + n], start=st["first"], stop=st["last"]),
                 reads=[vv_b, pt_b], writes=[po_b])
            if st["last"]:
                h = 2 * g + hh
                dn, dn_b = dens[ci["o"] % len(dens)]
                ot, ot_b = outs[ci["o"] % len(outs)]
                ci["o"] += 1
                S.op("dve", lambda e: e.tensor_copy(out=dn[0:64, 0:n], in_=po[64:128, 0:n]), reads=[po_b], writes=[dn_b])
                S.op("dve", lambda e: e.reciprocal(out=dn[0:64, 0:n], in_=dn[0:64, 0:n]), reads=[dn_b], writes=[dn_b])
                S.op("dve", lambda e: e.tensor_tensor(out=ot[0:64, 0:n], in0=po[0:64, 0:n], in1=dn[0:64, 0:n], op=ALU.mult),
                     reads=[po_b, dn_b], writes=[ot_b])
                S.dma("pool", BR[2, h * 64:(h + 1) * 64, t0:t0 + n], ot[0:64, 0:n], reads=[ot_b], writes=[dr["BR_b"]])

        _emit_pairs(steps, emit_qk, emit_pv)
        S.barrier()


def phase4_diff(K, dr, l, need_ctx, after_loads=None):
    nc, S = K.nc, K.S
    PT, VS, BR = dr["PT"], dr["VS"], dr["BR"]
    lam_init = 0.8 - 0.6 * math.exp(-0.3 * l)
    with contextlib.ExitStack() as es:
        lp, lp_b = alloc(es, nc, "d_lp", [128, 256], F32)
        sm, sm_b = alloc(es, nc, "d_sm", [128, 8], F32)
        S.dma("sp", lp[:], dr["diff_lambda"][l].rearrange("a d -> (a d)").partition_broadcast(128), writes=[lp_b])
        S.op("dve", lambda e: e.tensor_tensor(out=lp[:, 0:64], in0=lp[:, 0:64], in1=lp[:, 64:128], op=ALU.mult), reads=[lp_b], writes=[lp_b])
        S.op("dve", lambda e: e.tensor_tensor(out=lp[:, 128:192], in0=lp[:, 128:192], in1=lp[:, 192:256], op=ALU.mult), reads=[lp_b], writes=[lp_b])
        S.op("dve", lambda e: e.reduce_sum(out=sm[:, 0:1], in_=lp[:, 0:64], axis=AX.X), reads=[lp_b], writes=[sm_b])
        S.op("dve", lambda e: e.reduce_sum(out=sm[:, 1:2], in_=lp[:, 128:192], axis=AX.X), reads=[lp_b], writes=[sm_b])
        S.op("act", lambda e: e.activation(out=sm[:, 2:4], in_=sm[:, 0:2], func=AF.Exp), reads=[sm_b], writes=[sm_b])
        S.op("dve", lambda e: e.scalar_tensor_tensor(out=sm[:, 4:5], in0=sm[:, 3:4], scalar=-lam_init, in1=sm[:, 2:3], op0=ALU.add, op1=ALU.subtract),
             reads=[sm_b], writes=[sm_b])
        sg, sg_b = alloc(es, nc, "d_sg", [128, 1], F32)
        with nc.allow_non_contiguous_dma(reason="tiny gain vector"):
            S.dma("sp", sg[:], dr["diff_norm"][l].rearrange("(d o) -> d o", o=1), writes=[sg_b])
        S.op("dve", lambda e: e.tensor_scalar(out=sg[:], in0=sg[:], scalar1=1.0 - lam_init, scalar2=None, op0=ALU.mult), reads=[sg_b], writes=[sg_b])
        onesb, onesb_b = alloc(es, nc, "d_onesb", [128, 128], BF16)
        S.op("pool", lambda e: e.memset(onesb[:], 1.0), writes=[onesb_b])
        avg, avg_b = alloc(es, nc, "d_avg", [128, 128], F32)
        S.op("pool", lambda e: e.memset(avg[:], 1.0 / 128.0), writes=[avg_b])
        sets = []
        for i in range(2):
            kt = alloc(es, nc, f"d_kt{i}", [128, T], BF16)
            vv = alloc(es, nc, f"d_v{i}", [128, NT, 128], BF16)
            sets.append((kt, vv))
        qts = [alloc(es, nc, f"d_qt{i}", [128, T], BF16) for i in range(2)]
        pts = [alloc(es, nc, f"d_pt{i}", [128, 1024], BF16) for i in range(4)]
        f32t = [alloc(es, nc, f"d_f{i}", [128, 512], F32) for i in range(8)]
        outs = [alloc(es, nc, f"d_out{i}", [128, 512], BF16) for i in range(2)]
        ci = {"s": 0, "pt": 0, "o": 0, "f": 0}

        def nf():
            i = ci["f"]
            ci["f"] += 1
            return f32t[i % 8]

        def load_kv(g, what=("k", "v")):
            (kt, kt_b), (vv, vv_b) = sets[g % 2]
            r0 = 4896 + 128 * g
            if "k" in what:
                S.dma("sp", kt[:, :], PT[r0:r0 + 128, :], reads=[dr["PT_b"]], writes=[kt_b])
            if "v" in what:
                S.dma("sp", vv[:, :, :], VS[:, :, 1280 + 128 * g:1280 + 128 * (g + 1)], reads=[dr["VS_b"]], writes=[vv_b])

        def load_q(hq):
            qt, qt_b = qts[hq % 2]
            q0 = 4384 + 128 * hq
            S.dma("sp", qt[:, :], PT[q0:q0 + 128, :], reads=[dr["PT_b"]], writes=[qt_b])

        SK = 3
        steps = []
        for hq in range(4):
            for bi_, (t0, n, ktiles) in enumerate(_qblocks(need_ctx)):
                for ji, j in enumerate(ktiles):
                    for m in range(2):
                        steps.append(dict(hq=hq, t0=t0, n=n, j=j, m=m, first=(ji == 0), last=(ji == len(ktiles) - 1),
                                          newq=(bi_ == 0 and ji == 0 and m == 0)))
        acc = [(K.ps[4], K.ps[6]), (K.ps[5], K.ps[7])]

        def emit_qk(st):
            hq, t0, n, j, m = st["hq"], st["t0"], st["n"], st["j"], st["m"]
            if st["newq"]:
                if hq == 0:
                    load_kv(0, ("k",))
                    load_q(0)
                    load_kv(0, ("v",))
                    load_kv(1)
                    if after_loads is not None:
                        after_loads()
                if hq + 1 < 4:
                    load_q(hq + 1)
            (kt, kt_b), (vv, vv_b) = sets[(hq // 2) % 2]
            qt, qt_b = qts[hq % 2]
            mp = slice(m * 64, m * 64 + 64)
            bk = ci["s"] % 4
            psx, ps_b = K.ps[bk]
            ci["s"] += 1
            if m == 0:
                pt, pt_b = pts[ci["pt"] % len(pts)]
                ci["pt"] += 1
                ci["cur"] = (pt, pt_b)
            pt, pt_b = ci["cur"]
            st["pt"] = (pt, pt_b)
            S.op("pe", lambda e: e.matmul(psx[:, 0:n], lhsT=kt[mp, j * 128:(j + 1) * 128], rhs=qt[mp, t0:t0 + n], start=True, stop=True),
                 reads=[kt_b, qt_b], writes=[ps_b])
            if m == 1:
                src = K.psall[:, (bk - 1) * 512:(bk + 1) * 512].rearrange("p (k c) -> p k c", c=512)[:, :, 0:n]
                S.op("act", lambda e: e.activation(out=pt[:, :].rearrange("p (k c) -> p k c", c=512)[:, :, 0:n], in_=src, func=AF.Exp, scale=0.125),
                     reads=[K.ps[bk - 1][1], ps_b], writes=[pt_b])

        def emit_pv(st):
            hq, t0, n, j, m = st["hq"], st["t0"], st["n"], st["j"], st["m"]
            (kt, kt_b), (vv, vv_b) = sets[(hq // 2) % 2]
            pt, pt_b = st["pt"]
            (po, po_b), (pd, pd_b) = acc[m]
            st_, sp_ = st["first"], st["last"]
            S.op("pe", lambda e: e.matmul(po[:, 0:n], lhsT=vv[:, j, :], rhs=pt[:, m * 512:m * 512 + n], start=st_, stop=sp_), reads=[vv_b, pt_b], writes=[po_b])
            S.op("pe", lambda e: e.matmul(pd[:, 0:n], lhsT=onesb[:, :], rhs=pt[:, m * 512:m * 512 + n], start=st_, stop=sp_), reads=[onesb_b, pt_b], writes=[pd_b])
            if not (st["last"] and m == 1):
                return
            (po0, po0_b), (pd0, pd0_b) = acc[0]
            (po1, po1_b), (pd1, pd1_b) = acc[1]
            r0_, r0b = nf()
            r1_, r1b = nf()
            a_, ab = nf()
            b_, bb = nf()
            S.op("act", lambda e: e.activation(out=r0_[:, 0:n], in_=pd0[:, 0:n], func=AF.Copy), reads=[pd0_b], writes=[r0b])
            S.op("dve", lambda e: e.tensor_copy(out=a_[:, 0:n], in_=po0[:, 0:n]), reads=[po0_b], writes=[ab])
            S.op("act", lambda e: e.activation(out=r1_[:, 0:n], in_=pd1[:, 0:n], func=AF.Copy), reads=[pd1_b], writes=[r1b])
            S.op("dve", lambda e: e.tensor_copy(out=b_[:, 0:n], in_=po1[:, 0:n]), reads=[po1_b], writes=[bb])
            S.op("dve", lambda e: e.reciprocal(out=r0_[:, 0:n], in_=r0_[:, 0:n]), reads=[r0b], writes=[r0b])
            S.op("dve", lambda e: e.reciprocal(out=r1_[:, 0:n], in_=r1_[:, 0:n]), reads=[r1b], writes=[r1b])
            S.op("pool", lambda e: e.tensor_tensor(out=a_[:, 0:n], in0=a_[:, 0:n], in1=r0_[:, 0:n], op=ALU.mult), reads=[ab, r0b], writes=[ab])
            S.op("pool", lambda e: e.tensor_tensor(out=b_[:, 0:n], in0=b_[:, 0:n], in1=r1_[:, 0:n], op=ALU.mult), reads=[bb, r1b], writes=[bb])
            S.op("dve", lambda e: e.scalar_tensor_tensor(out=a_[:, 0:n], in0=b_[:, 0:n], scalar=sm[:, 4:5], in1=a_[:, 0:n], op0=ALU.mult, op1=ALU.add),
                 reads=[ab, bb, sm_b], writes=[ab])
            S.op("pool", lambda e: e.tensor_tensor(out=b_[:, 0:n], in0=a_[:, 0:n], in1=a_[:, 0:n], op=ALU.mult), reads=[ab], writes=[bb])
            return lambda: fin_tail(hq, t0, n, a_, ab, b_, bb, r0_, r0b, r1_, r1b)

        def fin_tail(hq, t0, n, a_, ab, b_, bb, r0_, r0b, r1_, r1b):
            psm, psm_b = K.ps[ci["s"] % 4]
            ci["s"] += 2
            S.op("pe", lambda e: e.matmul(psm[:, 0:n], lhsT=avg[:, :], rhs=b_[:, 0:n], start=True, stop=True), reads=[avg_b, bb], writes=[psm_b])
            S.op("act", lambda e: e.activation(out=r1_[:, 0:n], in_=psm[:, 0:n], func=AF.Ln, bias=K.epsc[:, 0:1], scale=1.0),
                 reads=[psm_b, K.epsc_b], writes=[r1b])
            S.op("act", lambda e: e.activation(out=r0_[:, 0:n], in_=r1_[:, 0:n], func=AF.Exp, scale=-0.5), reads=[r1b], writes=[r0b])
            ot, ot_b = outs[ci["o"] % 2]
            ci["o"] += 1
            S.op("dve", lambda e: e.scalar_tensor_tensor(out=ot[:, 0:n], in0=a_[:, 0:n], scalar=sg[:, 0:1], in1=r0_[:, 0:n], op0=ALU.mult, op1=ALU.mult),
                 reads=[ab, r0b, sg_b], writes=[ot_b])
            S.dma("pool", BR[3, hq * 128:(hq + 1) * 128, t0:t0 + n], ot[:, 0:n], reads=[ot_b], writes=[dr["BR_b"]])

        _emit_pairs(steps, emit_qk, emit_pv)
        S.barrier()


def band_mask():
    kl = np.arange(128)[:, None]
    ql = np.arange(128)[None, :]
    m0 = (kl >= ql).astype(np.float32)
    m1 = (kl <= ql).astype(np.float32)
    return np.stack([np.tile(m0, (1, 4)), np.tile(m1, (1, 4))], 0)


def phase5_window(K, dr, l, need_ctx):
    nc, S = K.nc, K.S
    PT, VS, BR = dr["PT"], dr["VS"], dr["BR"]
    with contextlib.ExitStack() as es:
        kt, kt_b = alloc(es, nc, "w_kt", [128, T], BF16)
        q4, q4_b = alloc(es, nc, "w_q4", [128, 4, T], BF16)
        vv, vv_b = alloc(es, nc, "w_v", [128, NT, 256], BF16)
        mk, mk_b = alloc(es, nc, "w_mask", [128, 2, 512], BF16)
        S.dma("sp", kt[:, :], PT[2080:2208, :], reads=[dr["PT_b"]], writes=[kt_b])
        for g in range(2):
            S.dma("sp", q4[g * 64:(g + 1) * 64, :, :], PT[1568 + 256 * g:1568 + 256 * (g + 1), :].rearrange("(i d) t -> d i t", d=64),
                  reads=[dr["PT_b"]], writes=[q4_b])
        S.dma("sp", vv[:, :, :], VS[:, :, 512:768], reads=[dr["VS_b"]], writes=[vv_b])
        for m in range(2):
            S.dma("sp", mk[:, m, :], dr["bandmask"][m], writes=[mk_b])
        sk, sk_b = alloc(es, nc, "w_sink", [1, 16], F32)
        S.dma("sp", sk[0:1, 0:8], dr["win_sink"][l].rearrange("(o h) -> o h", o=1), writes=[sk_b])
        S.op("act", lambda e: e.activation(out=sk[0:1, 8:16], in_=sk[0:1, 0:8], func=AF.Exp), reads=[sk_b], writes=[sk_b])
        srow, srow_b = alloc(es, nc, "w_srow", [1, 2, 512], F32)
        for h in range(8):
            S.op("dve", lambda e, h=h: e.tensor_scalar(out=srow[0:1, h // 4, (h % 4) * 128:(h % 4 + 1) * 128], in0=K.ones[0:1, 0:128],
                                                      scalar1=sk[0:1, 8 + h:9 + h], scalar2=None, op0=ALU.mult),
                 reads=[sk_b, K.ones_b], writes=[srow_b])
        sel, sel_b = alloc(es, nc, "w_sel", [1, 128], F32)
        S.op("pool", lambda e: e.memset(sel[0:1, 0:64], 0.0), writes=[sel_b])
        S.op("pool", lambda e: e.memset(sel[0:1, 64:128], 1.0), writes=[sel_b])
        pts = [alloc(es, nc, f"w_pt{i}", [128, 512], BF16) for i in range(6)]
        dens = [alloc(es, nc, f"w_den{i}", [64, 512], F32) for i in range(2)]
        outs = [alloc(es, nc, f"w_out{i}", [64, 512], BF16) for i in range(2)]
        ci = {"s": 0, "pt": 0, "o": 0}
        qblocks = []
        if need_ctx:
            qblocks += [(0, [(0, None), (1, None)]), (128, [(0, None), (1, None)])]
        for nb in range(32):
            kl = [(0, None), (1, None)]
            if nb > 0:
                kl.append((2 + nb - 1, 0))
            kl.append((2 + nb, None))
            if nb < 31:
                kl.append((2 + nb + 1, 1))
            qblocks.append((256 + 128 * nb, kl))
        steps = []
        bno = 0
        for g in range(2):
            for (t0, klist) in qblocks:
                for ji, (j, msk) in enumerate(klist):
                    steps.append(dict(g=g, t0=t0, j=j, msk=msk, first=(ji == 0), last=(ji == len(klist) - 1), blk=bno))
                bno += 1

        def emit_qk(st):
            g, t0, j, msk = st["g"], st["t0"], st["j"], st["msk"]
            gp = slice(g * 64, g * 64 + 64)
            psx, ps_b = K.ps[ci["s"] % 6]
            ci["s"] += 1
            pt, pt_b = pts[ci["pt"] % len(pts)]
            ci["pt"] += 1
            st["pt"] = (pt, pt_b)
            S.op("pe", lambda e: e.matmul(psx[:, :].rearrange("p (i q) -> p i q", i=4), lhsT=kt[gp, j * 128:(j + 1) * 128],
                                          rhs=q4[gp, :, t0:t0 + 128], start=True, stop=True), reads=[kt_b, q4_b], writes=[ps_b])
            S.op("act", lambda e: e.activation(out=pt[:, :], in_=psx[:, :], func=AF.Exp, scale=0.125), reads=[ps_b], writes=[pt_b])
            if msk is not None:
                S.op("pool", lambda e: e.tensor_tensor(out=pt[:, :], in0=pt[:, :], in1=mk[:, msk, :], op=ALU.mult),
                     reads=[pt_b, mk_b], writes=[pt_b])

        def emit_pv(st):
            g, t0, j = st["g"], st["t0"], st["j"]
            pt, pt_b = st["pt"]
            po, po_b = K.ps[6 + st["blk"] % 2]
            S.op("pe", lambda e: e.matmul(po[:, :], lhsT=vv[:, j, g * 128:(g + 1) * 128], rhs=pt[:, :], start=st["first"], stop=False),
                 reads=[vv_b, pt_b], writes=[po_b])
            if not st["last"]:
                return
            S.op("pe", lambda e: e.matmul(po[:, :], lhsT=sel[0:1, :], rhs=srow[0:1, g, :], start=False, stop=True),
                 reads=[sel_b, srow_b], writes=[po_b])
            dn, dn_b = dens[st["blk"] % 2]
            ot, ot_b = outs[st["blk"] % 2]
            S.op("act", lambda e: e.activation(out=dn[0:64, :], in_=po[64:128, :], func=AF.Copy), reads=[po_b], writes=[dn_b])
            S.op("dve", lambda e: e.reciprocal(out=dn[0:64, :], in_=dn[0:64, :]), reads=[dn_b], writes=[dn_b])
            S.op("dve", lambda e: e.tensor_tensor(out=ot[0:64, :], in0=po[0:64, :], in1=dn[0:64, :], op=ALU.mult), reads=[po_b, dn_b], writes=[ot_b])
            with nc.allow_non_contiguous_dma(reason="head-interleaved branch store"):
                S.dma("pool", BR[1, 256 * g:256 * (g + 1), t0:t0 + 128].rearrange("(i d) q -> d i q", d=64),
                      ot[0:64, :].rearrange("p (i q) -> p i q", i=4), reads=[ot_b], writes=[dr["BR_b"]])

        pend = []
        for st in steps:
            emit_qk(st)
            pend.append(st)
            if len(pend) > 3:
                emit_pv(pend.pop(0))
        while pend:
            emit_pv(pend.pop(0))
        S.barrier()


def phase7_weights(K, dr, l, es):
    nc, S = K.nc, K.S
    wm, wm_b = alloc(es, nc, "m_wm", [128, 4, 8, 1024], BF16)
    wu, wu_b = alloc(es, nc, "m_wu", [128, 4, 4, 1024], BF16)
    wo, wo_b = alloc(es, nc, "m_wo", [128, 8, 1024], BF16)
    def issue():
        for i in range(4):
            for kk in range(0, 8, 4):
                S.dma("pool", wm[:, i, kk:kk + 4, :], dr["w_merge"][l, i].rearrange("(k p) c -> p k c", p=128)[:, kk:kk + 4, :], writes=[wm_b])
            S.dma("pool", wu[:, i, :, :], dr["w_up"][l, i].rearrange("(k p) c -> p k c", p=128), writes=[wu_b])
        for kk in range(0, 8, 4):
            S.dma("pool", wo[:, kk:kk + 4, :], dr["w_out"][l].rearrange("(k p) c -> p k c", p=128)[:, kk:kk + 4, :], writes=[wo_b])

    return ((wm, wm_b), (wu, wu_b), (wo, wo_b)), issue


def phase7_merge(K, dr, l, need_ctx, x_src, c_src, src_b, x_dst, c_dst, x_dst_b, c_dst_b, W7):
    nc, S = K.nc, K.S
    alpha = (2 * DEPTH) ** 0.25
    with contextlib.ExitStack() as es:
        (wm, wm_b), (wu, wu_b), (wo, wo_b) = W7
        lng, lng_b = alloc(es, nc, "m_lng", [128, 1024], F32)
        lnb, lnb_b = alloc(es, nc, "m_lnb", [128, 1024], F32)
        S.dma("sp", lng[:], dr["ln_g"][l].partition_broadcast(128), writes=[lng_b])
        S.dma("sp", lnb[:], dr["ln_b"][l].partition_broadcast(128), writes=[lnb_b])
        hbs = [alloc(es, nc, f"m_h{i}", [128, 8, 512], BF16) for i in range(2)]
        brs = [alloc(es, nc, f"m_br{i}", [128, 16, 512], BF16) for i in range(1)]
        zs = [alloc(es, nc, f"m_z{i}", [128, 4, 512], BF16) for i in range(2)]
        sigs = [alloc(es, nc, f"m_sig{i}", [128, 512], F32) for i in range(2)]
        accs = [alloc(es, nc, f"m_acc{i}", [128, 512], F32) for i in range(2)]
        tmps = [alloc(es, nc, f"m_tmp{i}", [128, 512], F32) for i in range(2)]
        mTs = [alloc(es, nc, f"m_mT{i}", [128, 8, 512], BF16) for i in range(2)]
        xts = [alloc(es, nc, f"m_x{i}", [128, 1024], F32) for i in range(1)]
        rts = [alloc(es, nc, f"m_r{i}", [128, 1024], F32) for i in range(1)]
        sts = [alloc(es, nc, f"m_st{i}", [128, 16], F32) for i in range(2)]
        zrows = (1056, 2336, 3872, 5408)
        ci = {"b": 0, "sig": 0, "tmp": 0, "tile": 0}
        blocks = BLOCKS if need_ctx else BLOCKS[1:]
        nb = len(blocks)
        br, _ = brs[0]
        br_bs = [Buf(f"br{i}") for i in range(4)]

        def load_hb(bi):
            if bi >= nb:
                return
            t0, n = blocks[bi]
            hb, hb_b = hbs[bi % 2]
            S.dma("sp", hb[:, :, 0:n], dr["hT_d"][:, :, t0:t0 + n], reads=[dr["hT_d_b"]], writes=[hb_b])

        def issue_loads(bi):
            t0, n = blocks[bi]
            for i in range(4):
                zz, zz_b = zs[i % 2]
                br_b = br_bs[i]
                S.dma("sp", br[:, 4 * i:4 * i + 4, 0:n], dr["BR"][i, :, t0:t0 + n].rearrange("(k p) t -> p k t", p=128), reads=[dr["BR_b"]], writes=[br_b])
                S.dma("sp", zz[:, :, 0:n], dr["PT"][zrows[i]:zrows[i] + 512, t0:t0 + n].rearrange("(k p) t -> p k t", p=128),
                      reads=[dr["PT_b"]], writes=[zz_b])
                S.op("pool", lambda e, i=i, zz=zz: e.tensor_tensor(out=br[:, 4 * i:4 * i + 4, 0:n], in0=br[:, 4 * i:4 * i + 4, 0:n], in1=zz[:, :, 0:n], op=ALU.mult),
                     reads=[br_b, zz_b], writes=[br_b])

        def gu_chunk(bi, c):
            t0, n = blocks[bi]
            hb, hb_b = hbs[bi % 2]
            mT, mT_b = mTs[bi % 2]
            ac, ac_b = accs[c % 2]
            for i in range(4):
                pg, pg_b = K.ps[ci["b"] % 6]
                ci["b"] += 1
                pu, pu_b = K.ps[ci["b"] % 6]
                ci["b"] += 1
                for k in range(8):
                    S.op("pe", lambda e, k=k: e.matmul(pg[:, 0:n], lhsT=wm[:, i, k, c * 128:(c + 1) * 128], rhs=hb[:, k, 0:n], start=(k == 0), stop=(k == 7)),
                         reads=[wm_b, hb_b], writes=[pg_b])
                for k in range(4):
                    S.op("pe", lambda e, k=k: e.matmul(pu[:, 0:n], lhsT=wu[:, i, k, c * 128:(c + 1) * 128], rhs=br[:, 4 * i + k, 0:n], start=(k == 0), stop=(k == 3)),
                         reads=[wu_b, br_bs[i]], writes=[pu_b])
                sg, sg_b = sigs[ci["sig"] % 2]
                ci["sig"] += 1
                S.op("act", lambda e: e.activation(out=sg[:, 0:n], in_=pg[:, 0:n], func=AF.Sigmoid), reads=[pg_b], writes=[sg_b])
                if i == 0:
                    S.op("dve", lambda e: e.tensor_tensor(out=ac[:, 0:n], in0=pu[:, 0:n], in1=sg[:, 0:n], op=ALU.mult), reads=[pu_b, sg_b], writes=[ac_b])
                else:
                    tm, tm_b = tmps[ci["tmp"] % 2]
                    ci["tmp"] += 1
                    S.op("dve", lambda e: e.tensor_tensor(out=tm[:, 0:n], in0=pu[:, 0:n], in1=sg[:, 0:n], op=ALU.mult), reads=[pu_b, sg_b], writes=[tm_b])
                    if i < 3:
                        S.op("pool", lambda e: e.tensor_tensor(out=ac[:, 0:n], in0=ac[:, 0:n], in1=tm[:, 0:n], op=ALU.add), reads=[ac_b, tm_b], writes=[ac_b])
                    else:
                        S.op("pool", lambda e: e.tensor_tensor(out=mT[:, c, 0:n], in0=ac[:, 0:n], in1=tm[:, 0:n], op=ALU.add), reads=[ac_b, tm_b], writes=[mT_b])

        def out_tile(bi, j):
            t0, n = blocks[bi]
            isctx = (t0 == 0)
            gidx = 1 if isctx else 0
            mT, mT_b = mTs[bi % 2]
            tok = t0 + j * 128
            xt, xt_b = xts[0]
            rt, rt_b = rts[0]
            st, st_b = sts[ci["tile"] % 2]
            ci["tile"] += 1
            src = c_src[tok:tok + 128, :] if isctx else x_src[tok - NCTX:tok - NCTX + 128, :]
            S.dma("sp", xt[:], src, reads=[src_b], writes=[xt_b])
            for hf in range(2):
                pq, pq_b = K.ps[6 + hf]
                for k in range(8):
                    S.op("pe", lambda e, k=k: e.matmul(pq[:, :], lhsT=mT[:, k, j * 128:(j + 1) * 128], rhs=wo[:, k, hf * 512:(hf + 1) * 512], start=(k == 0), stop=(k == 7)),
                         reads=[mT_b, wo_b], writes=[pq_b])
                S.op("dve", lambda e: e.tensor_tensor(out=rt[:, hf * 512:(hf + 1) * 512], in0=pq[:, :], in1=K.gate[gidx][:, hf * 512:(hf + 1) * 512], op=ALU.mult),
                     reads=[pq_b, K.gate_b[gidx]], writes=[rt_b])
            S.op("dve", lambda e: e.scalar_tensor_tensor(out=rt[:], in0=xt[:], scalar=alpha, in1=rt[:], op0=ALU.mult, op1=ALU.add), reads=[xt_b, rt_b], writes=[rt_b])
            S.op("dve", lambda e: e.bn_stats(st[:, 0:6], rt[:, 0:512]), reads=[rt_b], writes=[st_b])
            S.op("dve", lambda e: e.bn_stats(st[:, 6:12], rt[:, 512:1024]), reads=[rt_b], writes=[st_b])
            S.op("dve", lambda e: e.bn_aggr(st[:, 12:14], st[:, 0:12]), reads=[st_b], writes=[st_b])
            S.op("act", lambda e: e.activation(out=st[:, 15:16], in_=st[:, 13:14], func=AF.Sqrt, bias=K.epsc[:, 0:1], scale=1.0), reads=[st_b, K.epsc_b], writes=[st_b])
            S.op("dve", lambda e: e.reciprocal(out=st[:, 14:15], in_=st[:, 15:16]), reads=[st_b], writes=[st_b])
            S.op("dve", lambda e: e.tensor_scalar(out=rt[:], in0=rt[:], scalar1=st[:, 12:13], scalar2=st[:, 14:15], op0=ALU.subtract, op1=ALU.mult), reads=[rt_b, st_b], writes=[rt_b])
            S.op("pool", lambda e: e.tensor_tensor(out=rt[:], in0=rt[:], in1=lng[:], op=ALU.mult), reads=[rt_b, lng_b], writes=[rt_b])
            S.op("pool", lambda e: e.tensor_tensor(out=xt[:], in0=rt[:], in1=lnb[:], op=ALU.add), reads=[rt_b, lnb_b], writes=[xt_b])
            if isctx:
                S.dma("sp", c_dst[tok:tok + 128, :], xt[:], reads=[xt_b], writes=[c_dst_b])
            else:
                S.dma("sp", x_dst[tok - NCTX:tok - NCTX + 128, :], xt[:], reads=[xt_b], writes=[x_dst_b])

        load_hb(0)
        issue_loads(0)
        load_hb(1)
        for c in range(8):
            gu_chunk(0, c)
        for bi in range(1, nb + 1):
            prev_tiles = list(range(blocks[bi - 1][1] // 128))
            if bi < nb:
                issue_loads(bi)
                load_hb(bi + 1)
                out_tile(bi - 1, prev_tiles.pop(0))
                for c in range(8):
                    gu_chunk(bi, c)
                    if c % 2 == 1 and prev_tiles:
                        out_tile(bi - 1, prev_tiles.pop(0))
            while prev_tiles:
                out_tile(bi - 1, prev_tiles.pop(0))
        S.barrier()


def tri_consts():
    s = np.arange(128)[:, None]
    c = np.arange(128)[None, :]
    return np.stack([(s <= c), (s >= c), (s > c), (s < c)], 0).astype(np.float32)


def gla_mask():
    t = tri_consts()
    return np.concatenate([t[0], t[0], t[1], t[1]], 1)


def phase6_gla(K, dr, l, need_ctx):
    nc, S = K.nc, K.S
    PT, VS, BR = dr["PT"], dr["VS"], dr["BR"]
    NI = -1.0 / 16.0
    with contextlib.ExitStack() as es:
        tri, tri_b = alloc(es, nc, "g_tri", [128, 4, 128], F32)
        S.dma("sp", tri[:], dr["tri"].rearrange("a s c -> s a c"), writes=[tri_b])
        mk, mk_b = alloc(es, nc, "g_mask", [128, 512], BF16)
        S.dma("sp", mk[:, :], dr["glamask"][:, :], writes=[mk_b])
        gts, gts_b = alloc(es, nc, "g_gt", [33, T], F32)
        S.dma("sp", gts[0:32, :], dr["GT"][:, :], reads=[dr["GT_b"]], writes=[gts_b])
        S.op("pool", lambda e: e.memset(gts[32:33, :], 1.0), writes=[gts_b])
        wg, wg_b = alloc(es, nc, "g_wg", [33, 512], F32)
        S.op("pool", lambda e: e.memset(wg[:, :], 0.0), writes=[wg_b])
        S.dma("sp", wg[0:16, 0:256], dr["gla_w_gate"][l, 0], writes=[wg_b])
        S.dma("sp", wg[16:32, 256:512], dr["gla_w_gate"][l, 1], writes=[wg_b])
        S.dma("sp", wg[32:33, :], dr["gla_b_gate"][l].rearrange("(o a) c -> o (a c)", o=1), writes=[wg_b])
        qT, qT_b = alloc(es, nc, "g_qT", [128, 2, T], BF16)
        kT, kT_b = alloc(es, nc, "g_kT", [128, 2, T], BF16)
        S.dma("sp", qT[:], PT[0:256, :].rearrange("(g p) t -> p g t", p=128), reads=[dr["PT_b"]], writes=[qT_b])
        S.dma("sp", kT[:], PT[256:512, :].rearrange("(g p) t -> p g t", p=128), reads=[dr["PT_b"]], writes=[kT_b])
        vall, vall_b = alloc(es, nc, "g_v", [128, NT, 512], BF16)
        ktok, ktok_b = alloc(es, nc, "g_ktok", [128, NT, 256], BF16)
        S.dma("sp", vall[:], VS[:, :, 0:512], reads=[dr["VS_b"]], writes=[vall_b])
        S.dma("sp", ktok[:], VS[:, :, 1536:1792], reads=[dr["VS_b"]], writes=[ktok_b])
        SBst, SBst_b = alloc(es, nc, "g_SBst", [128, 2, NT, 128], BF16)
        gnT, gnT_b = alloc(es, nc, "g_gnT", [128, 4], F32)
        with nc.allow_non_contiguous_dma(reason="tiny gain vector"):
            S.dma("sp", gnT[:], dr["gla_norm"][l].rearrange("(h v) -> v h", v=128), writes=[gnT_b])
        gnb, gnb_b = alloc(es, nc, "g_gnb", [128, 512], F32)
        for h in range(4):
            S.op("dve", lambda e, h=h: e.tensor_scalar(out=gnb[:, h * 128:(h + 1) * 128], in0=K.ones[:], scalar1=gnT[:, h:h + 1], scalar2=None, op0=ALU.mult),
                 reads=[gnT_b, K.ones_b], writes=[gnb_b])
        avg, avg_b = alloc(es, nc, "g_avg", [128, 128], BF16)
        S.op("pool", lambda e: e.memset(avg[:], 1.0 / 128.0), writes=[avg_b])
        sqbs = [alloc(es, nc, f"g_sqb{i}", [128, 512], BF16) for i in range(2)]
        SF, SF_b = alloc(es, nc, "g_SF", [128, 2, 128], F32)
        SFb, SFb_b = alloc(es, nc, "g_SFb", [128, 2, 128], BF16)
        SBf, SBf_b = alloc(es, nc, "g_SBf", [128, 2, 128], F32)
        for t_, b_ in ((SF, SF_b), (SFb, SFb_b), (SBf, SBf_b)):
            S.op("pool", lambda e, t_=t_: e.memset(t_[:], 0.0), writes=[b_])
        Lt = [alloc(es, nc, f"g_L{i}", [128, 512], F32) for i in range(2)]
        tmps = []
        for i in range(2):
            tmps.append(dict(
                et=alloc(es, nc, f"g_e{i}", [128, 512], F32), Ef=alloc(es, nc, f"g_Ef{i}", [128, 256], F32),
                kh=alloc(es, nc, f"g_kh{i}", [128, 256], BF16), dec=alloc(es, nc, f"g_dec{i}", [128, 2], F32),
                EQ=alloc(es, nc, f"g_EQ{i}", [128, 512], F32), EK=alloc(es, nc, f"g_EK{i}", [128, 512], F32),
                qe=alloc(es, nc, f"g_qe{i}", [128, 2, 2, 128], BF16), ke=alloc(es, nc, f"g_ke{i}", [128, 2, 2, 128], BF16),
                Am=alloc(es, nc, f"g_Am{i}", [128, 2, 512], BF16), sq=alloc(es, nc, f"g_sq{i}", [128, 512], F32),
                rs=alloc(es, nc, f"g_rs{i}", [128, 512], F32), tt=alloc(es, nc, f"g_tt{i}", [128, 512], F32)))
        tsel = {"i": 0}
        ys = [alloc(es, nc, f"g_y{i}", [128, 512], BF16) for i in range(2)]
        cnt = {"L": 0, "y": 0}
        (pZ, pZ_b), (pM, pM_b), (pC, pC_b), (pA0, pA0_b), (pA1, pA1_b), (pkv, pkv_b), (pO, pO_b), (pms, pms_b) = K.ps

        def gate_L(j, lo, hi):
            L, L_b = Lt[cnt["L"] % 2]
            cnt["L"] += 1
            et, et_b = tmps[tsel["i"] % 2]["et"]
            S.op("pe", lambda e: e.matmul(pZ[:, lo:hi], lhsT=gts[0:33, j * 128:(j + 1) * 128], rhs=wg[0:33, lo:hi], start=True, stop=True),
                 reads=[gts_b, wg_b], writes=[pZ_b])
            S.op("act", lambda e: e.activation(out=et[:, lo:hi], in_=pZ[:, lo:hi], func=AF.Exp, scale=-1.0), reads=[pZ_b], writes=[et_b])
            S.op("act", lambda e: e.activation(out=L[:, lo:hi], in_=et[:, lo:hi], func=AF.Ln, bias=K.ones[:, 0:1], scale=1.0), reads=[et_b, K.ones_b], writes=[L_b])
            return L, L_b

        def state_update(St, St_b, dc, dc_b):
            for grp in range(2):
                for hl in range(2):
                    hp = slice(hl * 64, hl * 64 + 64)
                    S.op("dve", lambda e, grp=grp, hl=hl, hp=hp: e.scalar_tensor_tensor(
                        out=St[hp, grp, :], in0=St[hp, grp, :], scalar=dc[hp, grp:grp + 1], in1=pkv[hp, grp * 256 + hl * 128:grp * 256 + hl * 128 + 128],
                        op0=ALU.mult, op1=ALU.add), reads=[St_b, dc_b, pkv_b], writes=[St_b])

        def _pipeline(gens):
            prev = None
            for g_ in gens:
                next(g_)
                if prev is not None:
                    for _ in prev:
                        pass
                prev = g_
            if prev is not None:
                for _ in prev:
                    pass

        order_b = [1, 0] + list(range(NT - 1, 1, -1))

        def tileB(j, seq):
            if j == 2:
                yield
                S.op("act", lambda e: e.activation(out=SBst[:, :, j, :], in_=SBf[:, :, :], func=AF.Copy), reads=[SBf_b], writes=[SBst_b])
                return
            tsel["i"] = seq
            tm_ = tmps[seq % 2]
            (Ef, Ef_b), (kh, kh_b), (dec, dec_b), (EQ, EQ_b), (EK, EK_b) = tm_["Ef"], tm_["kh"], tm_["dec"], tm_["EQ"], tm_["EK"]
            (qe, qe_b), (ke, ke_b), (Am, Am_b), (sq, sq_b), (rs, rs_b), (tt, tt_b) = tm_["qe"], tm_["ke"], tm_["Am"], tm_["sq"], tm_["rs"], tm_["tt"]
            L, L_b = gate_L(j, 256, 512)
            S.op("pe", lambda e: e.matmul(pM[:, 0:256], lhsT=tri[:, 3, :], rhs=L[:, 256:512], start=True, stop=True), reads=[tri_b, L_b], writes=[pM_b])
            S.op("act", lambda e: e.activation(out=Ef[:, :], in_=pM[:, 0:256], func=AF.Exp, scale=NI), reads=[pM_b], writes=[Ef_b])
            S.op("dve", lambda e: e.tensor_tensor(out=kh[:, :], in0=ktok[:, j, :], in1=Ef[:, :], op=ALU.mult), reads=[ktok_b, Ef_b], writes=[kh_b])
            for grp in range(2):
                S.op("pe", lambda e, grp=grp: e.matmul(pC[:, grp:grp + 1], lhsT=L[:, 256 + grp * 128:256 + (grp + 1) * 128], rhs=K.ones[:, 0:1], start=True, stop=True),
                     reads=[L_b, K.ones_b], writes=[pC_b])
            S.op("act", lambda e: e.activation(out=dec[:, 0:2], in_=pC[:, 0:2], func=AF.Exp, scale=NI), reads=[pC_b], writes=[dec_b])
            yield
            S.op("act", lambda e: e.activation(out=SBst[:, :, j, :], in_=SBf[:, :, :], func=AF.Copy), reads=[SBf_b], writes=[SBst_b])
            for grp in range(2):
                S.op("pe", lambda e, grp=grp: e.matmul(pkv[:, grp * 256:(grp + 1) * 256], lhsT=kh[:, grp * 128:(grp + 1) * 128], rhs=vall[:, j, grp * 256:(grp + 1) * 256],
                                                      start=True, stop=True), reads=[kh_b, vall_b], writes=[pkv_b])
            state_update(SBf, SBf_b, dec, dec_b)

        _pipeline([tileB(j, q_) for q_, j in enumerate(order_b)])

        def tileF(j, seq):
            tok = j * 128
            tsel["i"] = seq
            tm_ = tmps[seq % 2]
            (Ef, Ef_b), (kh, kh_b), (dec, dec_b), (EQ, EQ_b), (EK, EK_b) = tm_["Ef"], tm_["kh"], tm_["dec"], tm_["EQ"], tm_["EK"]
            (qe, qe_b), (ke, ke_b), (Am, Am_b), (sq, sq_b), (rs, rs_b), (tt, tt_b) = tm_["qe"], tm_["ke"], tm_["Am"], tm_["sq"], tm_["rs"], tm_["tt"]
            L, L_b = gate_L(j, 0, 512)
            S.op("pe", lambda e: e.matmul(pM[:, 0:256], lhsT=tri[:, 2, :], rhs=L[:, 0:256], start=True, stop=True), reads=[tri_b, L_b], writes=[pM_b])
            S.op("act", lambda e: e.activation(out=Ef[:, :], in_=pM[:, 0:256], func=AF.Exp, scale=NI), reads=[pM_b], writes=[Ef_b])
            S.op("dve", lambda e: e.tensor_tensor(out=kh[:, :], in0=ktok[:, j, :], in1=Ef[:, :], op=ALU.mult), reads=[ktok_b, Ef_b], writes=[kh_b])
            for d in range(2):
                for grp in range(2):
                    c0 = (d * 2 + grp) * 128
                    S.op("pe", lambda e, d=d, grp=grp, c0=c0: e.matmul(pC[:, c0:c0 + 128], lhsT=L[:, d * 256 + grp * 128:d * 256 + (grp + 1) * 128], rhs=tri[:, d, :],
                                                                     start=True, stop=True), reads=[L_b, tri_b], writes=[pC_b])
            S.op("act", lambda e: e.activation(out=EQ[:, :], in_=pC[:, :], func=AF.Exp, scale=NI), reads=[pC_b], writes=[EQ_b])
            S.op("act", lambda e: e.activation(out=EK[:, :], in_=pC[:, :], func=AF.Exp, scale=-NI), reads=[pC_b], writes=[EK_b])
            S.op("act", lambda e: e.activation(out=dec[:, 0:2], in_=pC[:, :].rearrange("p (a c) -> p a c", c=128)[:, 0:2, 127], func=AF.Exp, scale=NI),
                 reads=[pC_b], writes=[dec_b])
            for d in range(2):
                S.op("pool", lambda e, d=d: e.tensor_tensor(out=qe[:, d, :, :], in0=qT[:, :, tok:tok + 128],
                                                         in1=EQ[:, d * 256:(d + 1) * 256].rearrange("p (g c) -> p g c", c=128), op=ALU.mult),
                     reads=[qT_b, EQ_b], writes=[qe_b])
                S.op("dve", lambda e, d=d: e.tensor_tensor(out=ke[:, d, :, :], in0=kT[:, :, tok:tok + 128],
                                                        in1=EK[:, d * 256:(d + 1) * 256].rearrange("p (g c) -> p g c", c=128), op=ALU.mult),
                     reads=[kT_b, EK_b], writes=[ke_b])
            pAs = ((pA0, pA0_b), (pA1, pA1_b))
            for d in range(2):
                for h in range(4):
                    grp, hl = h // 2, h % 2
                    hp = slice(hl * 64, hl * 64 + 64)
                    pA, pA_b = pAs[hl]
                    cb = (d * 2 + grp) * 128
                    S.op("pe", lambda e, d=d, grp=grp, hp=hp, pA=pA, cb=cb: e.matmul(pA[:, cb:cb + 128], lhsT=ke[hp, d, grp, :], rhs=qe[hp, d, grp, :],
                                                                                   start=True, stop=True), reads=[ke_b, qe_b], writes=[pA_b])
            for hl in range(2):
                pA, pA_b = pAs[hl]
                S.op("dve", lambda e, hl=hl, pA=pA: e.tensor_tensor(out=Am[:, hl, :], in0=pA[:, :], in1=mk[:, :], op=ALU.mult), reads=[pA_b, mk_b], writes=[Am_b])
            yield
            for grp in range(2):
                S.op("pe", lambda e, grp=grp: e.matmul(pkv[:, grp * 256:(grp + 1) * 256], lhsT=kh[:, grp * 128:(grp + 1) * 128], rhs=vall[:, j, grp * 256:(grp + 1) * 256],
                                                      start=True, stop=True), reads=[kh_b, vall_b], writes=[pkv_b])
            if j >= 2 or need_ctx:
                for h in range(4):
                    grp, hl = h // 2, h % 2
                    hp = slice(hl * 64, hl * 64 + 64)
                    o_ap = pO[:, h * 128:(h + 1) * 128]
                    S.op("pe", lambda e, h=h, hl=hl, grp=grp: e.matmul(pO[:, h * 128:(h + 1) * 128], lhsT=vall[:, j, h * 128:(h + 1) * 128],
                                                                     rhs=Am[:, hl, grp * 128:(grp + 1) * 128], start=True, stop=False),
                         reads=[vall_b, Am_b], writes=[pO_b])
                    S.op("pe", lambda e, h=h, hl=hl, grp=grp: e.matmul(pO[:, h * 128:(h + 1) * 128], lhsT=vall[:, j, h * 128:(h + 1) * 128],
                                                                     rhs=Am[:, hl, (2 + grp) * 128:(3 + grp) * 128], start=False, stop=False),
                         reads=[vall_b, Am_b], writes=[pO_b])
                    S.op("pe", lambda e, h=h, grp=grp, hp=hp: e.matmul(pO[:, h * 128:(h + 1) * 128], lhsT=SFb[hp, grp, :], rhs=qe[hp, 0, grp, :], start=False, stop=False),
                         reads=[SFb_b, qe_b], writes=[pO_b])
                    S.op("pe", lambda e, h=h, grp=grp, hp=hp: e.matmul(pO[:, h * 128:(h + 1) * 128], lhsT=SBst[hp, grp, j, :], rhs=qe[hp, 1, grp, :], start=False, stop=True),
                         reads=[SBst_b, qe_b], writes=[pO_b])
                sqb, sqb_b = sqbs[seq % 2]
                S.op("act", lambda e: e.activation(out=sqb[:, :], in_=pO[:, :], func=AF.Square), reads=[pO_b], writes=[sqb_b])
                S.op("pe", lambda e: e.matmul(pms[:, :], lhsT=avg[:, :], rhs=sqb[:, :], start=True, stop=True), reads=[avg_b, sqb_b], writes=[pms_b])
                S.op("act", lambda e: e.activation(out=sq[:, :], in_=pms[:, :], func=AF.Ln, bias=K.epsc[:, 0:1], scale=1.0), reads=[pms_b, K.epsc_b], writes=[sq_b])
                S.op("act", lambda e: e.activation(out=rs[:, :], in_=sq[:, :], func=AF.Exp, scale=-0.5), reads=[sq_b], writes=[rs_b])
                S.op("dve", lambda e: e.tensor_tensor(out=tt[:, :], in0=pO[:, :], in1=rs[:, :], op=ALU.mult), reads=[pO_b, rs_b], writes=[tt_b])
                y, y_b = ys[cnt["y"] % 2]
                cnt["y"] += 1
                S.op("pool", lambda e: e.tensor_tensor(out=y[:, :], in0=tt[:, :], in1=gnb[:, :], op=ALU.mult), reads=[tt_b, gnb_b], writes=[y_b])
                with nc.allow_non_contiguous_dma(reason="head-interleaved branch store"):
                    S.dma("sp", BR[0, :, tok:tok + 128].rearrange("(h v) c -> v h c", v=128), y[:, :].rearrange("p (h c) -> p h c", c=128), reads=[y_b], writes=[dr["BR_b"]])
            if j < NT - 1:
                state_update(SF, SF_b, dec, dec_b)
                S.op("act", lambda e: e.activation(out=SFb[:, :, :], in_=SF[:, :, :], func=AF.Copy), reads=[SF_b], writes=[SFb_b])

        _pipeline([tileF(j, j) for j in range(NT)])
        S.barrier()


W_SPECS = [("w_ada", [2, D, 3 * D]), ("b_ada", [2, 3 * D]), ("w_in", [2, D, INC]), ("gla_w_gate", [2, 2, 16, 256]),
           ("gla_b_gate", [2, 2, 256]), ("gla_norm", [2, 512]), ("win_sink", [2, 8]), ("glb_q_norm", [2, 64]),
           ("glb_k_norm", [2, 64]), ("diff_lambda", [2, 4, 64]), ("diff_norm", [2, 128]), ("w_merge", [2, 4, D, D]),
           ("w_up", [2, 4, 512, D]), ("w_out", [2, D, D]), ("ln_g", [2, D]), ("ln_b", [2, D])]
C_SPECS = [("ident", [128, 128], F32), ("bd64", [128, 128], BF16), ("ropeC", [128, T], F32), ("ropeS", [128, T], F32),
           ("bandmask", [2, 128, 512], BF16), ("tri", [4, 128, 128], F32), ("glamask", [128, 512], BF16)]


def host_consts():
    import ml_dtypes
    CTn, STn = rope_tables()
    bd = np.zeros((128, 128), np.float32)
    bd[:64, :64] = 1.0 / 64
    bd[64:, 64:] = 1.0 / 64
    return {"ident": np.eye(128, dtype=np.float32), "bd64": bd.astype(ml_dtypes.bfloat16), "ropeC": CTn, "ropeS": STn,
            "bandmask": band_mask().astype(ml_dtypes.bfloat16), "tri": tri_consts(),
            "glamask": gla_mask().astype(ml_dtypes.bfloat16)}


def declare_io(nc, scratch_kind="Internal"):
    dr = {}
    dr["x"] = nc.dram_tensor("x", [NX, D], F32, kind="ExternalInput").ap()
    dr["ctx"] = nc.dram_tensor("ctx", [NCTX, D], F32, kind="ExternalInput").ap()
    dr["c"] = nc.dram_tensor("c", [D], F32, kind="ExternalInput").ap()
    dr["c_ctx"] = nc.dram_tensor("c_ctx", [D], F32, kind="ExternalInput").ap()
    for nm, shp in W_SPECS:
        dr[nm] = nc.dram_tensor(nm, shp, F32, kind="ExternalInput").ap()
    for nm, shp, dt in C_SPECS:
        dr[nm] = nc.dram_tensor(nm, shp, dt, kind="ExternalInput").ap()
    dr["out"] = nc.dram_tensor("out", [NX, D], F32, kind="ExternalOutput").ap()
    for nm, shp, dt in (("hT_d", [128, 8, T], BF16), ("PT", [INC, T], BF16), ("GT", [32, T], F32), ("VS", [128, NT, VS_W], BF16),
                        ("BR", [4, 512, T], BF16), ("x1", [NX, D], F32), ("c1", [NCTX, D], F32)):
        dr[nm] = nc.dram_tensor(nm, shp, dt, kind=scratch_kind).ap()
        dr[nm + "_b"] = Buf(nm)
    dr["in_b"] = Buf("inputs")
    dr["out_b"] = Buf("out")
    return dr


def emit_full(nc, S, es, layers=(0, 1), scratch_kind="Internal"):
    dr = declare_io(nc, scratch_kind)
    K = make_ctx(nc, S, es)
    load_consts(K, dr)
    for l in layers:
        need_ctx = l < DEPTH - 1
        if l == 0:
            x_src, c_src, src_b = dr["x"], dr["ctx"], dr["in_b"]
        else:
            x_src, c_src, src_b = dr["x1"], dr["c1"], dr["x1_b"]
        phase0_mod(K, dr, l)
        with contextlib.ExitStack() as es2:
            hT, hT_b = alloc(es2, nc, "hT", [128, 8, T], BF16)
            phase1_ln(K, dr, l, x_src, c_src, src_b, hT, hT_b)
            phase2_proj(K, dr, l, hT, hT_b)
        phase6_gla(K, dr, l, need_ctx)
        phase5_window(K, dr, l, need_ctx)
        phase3_global(K, dr, l, need_ctx)
        if l == DEPTH - 1:
            x_dst, x_dst_b = dr["out"], dr["out_b"]
        else:
            x_dst, x_dst_b = dr["x1"], dr["x1_b"]
        with contextlib.ExitStack() as esw:
            W7, issue_w7 = phase7_weights(K, dr, l, esw)
            phase4_diff(K, dr, l, need_ctx, after_loads=issue_w7)
            phase7_merge(K, dr, l, need_ctx, x_src, c_src, src_b, x_dst, dr["c1"], x_dst_b, dr["x1_b"], W7)
    S.finish()
    return dr


def build_program(layers=(0, 1), scratch_kind="Internal"):
    nc0 = bass.Bass("TRN2", target_bir_lowering=False)
    with contextlib.ExitStack() as es0:
        S0 = Sched(nc0)
        emit_full(nc0, S0, es0, layers, scratch_kind)
    nc = bass.Bass("TRN2", target_bir_lowering=False)
    es = contextlib.ExitStack()
    S = Sched(nc, needed=S0.needed, es=es)
    emit_full(nc, S, es, layers, scratch_kind)
    return nc, S, es


def kernel(**inputs):
    nc, S, es = build_program()
    consts = host_consts()
    B = inputs["x"].shape[0]
    shared = {nm: np.ascontiguousarray(np.asarray(inputs[nm], dtype=np.float32)) for nm, _ in W_SPECS}
    shared["c_ctx"] = np.ascontiguousarray(np.asarray(inputs["c_ctx"], dtype=np.float32))
    shared.update(consts)
    in_maps = []
    for b in range(B):
        m = dict(shared)
        m["x"] = np.ascontiguousarray(np.asarray(inputs["x"][b], dtype=np.float32))
        m["ctx"] = np.ascontiguousarray(np.asarray(inputs["ctx"][b], dtype=np.float32))
        m["c"] = np.ascontiguousarray(np.asarray(inputs["c"][b], dtype=np.float32))
        in_maps.append(m)
    res = run_bass_kernel_spmd(nc, in_maps, core_ids=list(range(B)))
    return np.stack([np.asarray(res.results[b]["out"], dtype=np.float32) for b in range(B)], axis=0)
```
